# Optimizing a Trainium2 kernel written in Bass

```python
import math
import jax, jax.numpy as jnp
from jax import lax
import numpy as np

D_MODEL = 1024
BATCH = 2
SEQ = 16384
DEPTH = 1

N_MEM = 256
MIX_WIDTH = D_MODEL
DA_WIDTH = MIX_WIDTH // 2
CV_WIDTH = MIX_WIDTH - DA_WIDTH
DA_HEADS = 4
DA_VDIM = DA_WIDTH // DA_HEADS
DA_QKDIM = DA_VDIM // 2
ROT_DIM = DA_QKDIM // 4
ROPE_THETA = 500000.0
Q_BLOCK = 128
CV_KERNEL = 31
X_HEADS = 4
X_HEAD_DIM = D_MODEL // X_HEADS
D_FF = 2816
FFN_KERNEL = 3
EPS = 1e-6

Q_COLS = DA_HEADS * 2 * DA_QKDIM
K_COLS = DA_HEADS * 2 * DA_QKDIM
V_COLS = DA_HEADS * DA_VDIM
CV_COLS = 2 * CV_WIDTH
IN_COLS = Q_COLS + K_COLS + V_COLS + CV_COLS

kernel_name = "hybrid_diffattn_conformer_convffn_block"


def rmsnorm(x, g):
    xf = x.astype(jnp.float32)
    y = xf * lax.rsqrt(jnp.mean(xf * xf, axis=-1, keepdims=True) + EPS)
    return (y * g.astype(jnp.float32)).astype(x.dtype)


def layernorm(x, g, b):
    xf = x.astype(jnp.float32)
    mu = jnp.mean(xf, axis=-1, keepdims=True)
    var = jnp.mean(jnp.square(xf - mu), axis=-1, keepdims=True)
    y = (xf - mu) * lax.rsqrt(var + EPS)
    return (y * g.astype(jnp.float32) + b.astype(jnp.float32)).astype(x.dtype)


def causal_dwconv(x, w):
    k, c = w.shape
    return lax.conv_general_dilated(
        x, w[:, None, :].astype(x.dtype), window_strides=(1,),
        padding=[(k - 1, 0)], dimension_numbers=("NWC", "WIO", "NWC"),
        feature_group_count=c)


def partial_rope(t, positions):
    half = ROT_DIM // 2
    inv_freq = ROPE_THETA ** (-jnp.arange(0, ROT_DIM, 2, dtype=jnp.float32) / ROT_DIM)
    ang = positions.astype(jnp.float32)[..., None] * inv_freq
    cos = jnp.cos(ang)[:, :, None, None, :]
    sin = jnp.sin(ang)[:, :, None, None, :]
    tf = t.astype(jnp.float32)
    t1, t2, rest = tf[..., :half], tf[..., half:ROT_DIM], tf[..., ROT_DIM:]
    out = jnp.concatenate([t1 * cos - t2 * sin, t2 * cos + t1 * sin, rest], axis=-1)
    return out.astype(t.dtype)


def diff_attention(q, k, v, lam):
    b, h, _, s, d = q.shape
    dv = v.shape[-1]
    nblk = s // Q_BLOCK
    scale = 1.0 / math.sqrt(d)
    qb = q.reshape(b, h, 2, nblk, Q_BLOCK, d).transpose(3, 0, 1, 2, 4, 5)
    kpos = jnp.arange(s)

    def one_block(args):
        q_blk, i = args
        sc = jnp.einsum("bhcqd,bhckd->bhcqk", q_blk, k).astype(jnp.float32) * scale
        qpos = i * Q_BLOCK + jnp.arange(Q_BLOCK)
        mask = kpos[None, :] <= qpos[:, None]
        sc = jnp.where(mask, sc, -jnp.inf)
        p = jax.nn.softmax(sc, axis=-1)
        a = p[:, :, 0] - lam * p[:, :, 1]
        return jnp.einsum("bhqk,bhkv->bhqv", a.astype(v.dtype), v)

    out = lax.map(one_block, (qb, jnp.arange(nblk)))
    return out.transpose(1, 0, 3, 2, 4).reshape(b, s, h, dv)


def hybrid_mixer(h, positions, layer_idx, w_in, lam_q1, lam_k1, lam_q2, lam_k2,
                 subln_g, cv_dw_w, cv_dw_b, cv_ln_g, cv_ln_b, w_out):
    b, s, _ = h.shape
    proj = jnp.einsum("bsd,dc->bsc", h, w_in)
    o1 = Q_COLS
    o2 = o1 + K_COLS
    o3 = o2 + V_COLS
    q = proj[..., :o1].reshape(b, s, DA_HEADS, 2, DA_QKDIM)
    k = proj[..., o1:o2].reshape(b, s, DA_HEADS, 2, DA_QKDIM)
    v = proj[..., o2:o3].reshape(b, s, DA_HEADS, DA_VDIM)
    u = proj[..., o3:]

    q = partial_rope(q, positions).transpose(0, 2, 3, 1, 4)
    k = partial_rope(k, positions).transpose(0, 2, 3, 1, 4)
    v = v.transpose(0, 2, 1, 3)
    lam_init = 0.8 - 0.6 * math.exp(-0.3 * layer_idx)
    lam = (jnp.exp(jnp.sum(lam_q1.astype(jnp.float32) * lam_k1.astype(jnp.float32)))
           - jnp.exp(jnp.sum(lam_q2.astype(jnp.float32) * lam_k2.astype(jnp.float32)))
           + lam_init)
    a = diff_attention(q, k, v, lam)
    a = rmsnorm(a, subln_g) * (1.0 - lam_init)
    a = a.reshape(b, s, DA_WIDTH)

    c = u[..., :CV_WIDTH] * jax.nn.sigmoid(u[..., CV_WIDTH:])
    c = causal_dwconv(c, cv_dw_w) + cv_dw_b.astype(c.dtype)
    c = jax.nn.silu(layernorm(c, cv_ln_g, cv_ln_b))

    y = jnp.concatenate([a, c], axis=-1)
    return jnp.einsum("bsc,cd->bsd", y, w_out)


def cross_attention(h, m, w_cq, w_ckv, w_co):
    b, s, _ = h.shape
    q = jnp.einsum("bsd,de->bse", h, w_cq).reshape(b, s, X_HEADS, X_HEAD_DIM)
    kv = jnp.einsum("bmd,de->bme", m, w_ckv)
    k = kv[..., :D_MODEL].reshape(b, m.shape[1], X_HEADS, X_HEAD_DIM)
    v = kv[..., D_MODEL:].reshape(b, m.shape[1], X_HEADS, X_HEAD_DIM)
    sc = jnp.einsum("bshd,bmhd->bhsm", q, k).astype(jnp.float32) / math.sqrt(X_HEAD_DIM)
    p = jax.nn.softmax(sc, axis=-1).astype(v.dtype)
    o = jnp.einsum("bhsm,bmhd->bshd", p, v).reshape(b, s, D_MODEL)
    return jnp.einsum("bse,ed->bsd", o, w_co)


def conv_ffn(h, w_up, ffn_dw_w, w_down):
    up = jnp.einsum("bsd,df->bsf", h, w_up)
    up = causal_dwconv(up, ffn_dw_w)
    z = jax.nn.silu(up[..., :D_FF]) * up[..., D_FF:]
    return jnp.einsum("bsf,fd->bsd", z, w_down)


def setup_inputs(seed: int = 0) -> dict:
    key = jax.random.key(seed)
    ks = jax.random.split(key, 24)
    f32 = jnp.float32

    def nrm(k, shape, scale):
        return jax.random.normal(k, shape, f32) * scale

    def gain(k, shape):
        return 1.0 + 0.02 * jax.random.normal(k, shape, f32)

    L = DEPTH
    return {
        "x": jax.random.normal(ks[0], (BATCH, SEQ, D_MODEL), f32),
        "mem": jax.random.normal(ks[1], (BATCH, N_MEM, D_MODEL), f32),
        "positions": jnp.broadcast_to(jnp.arange(SEQ, dtype=jnp.int32), (BATCH, SEQ)),
        "norm_mix_g": gain(ks[2], (L, D_MODEL)),
        "w_in": nrm(ks[3], (L, D_MODEL, IN_COLS), D_MODEL ** -0.5),
        "lam_q1": nrm(ks[4], (L, DA_QKDIM), 0.1),
        "lam_k1": nrm(ks[5], (L, DA_QKDIM), 0.1),
        "lam_q2": nrm(ks[6], (L, DA_QKDIM), 0.1),
        "lam_k2": nrm(ks[7], (L, DA_QKDIM), 0.1),
        "subln_g": gain(ks[8], (L, DA_VDIM)),
        "cv_dw_w": nrm(ks[9], (L, CV_KERNEL, CV_WIDTH), CV_KERNEL ** -0.5),
        "cv_dw_b": nrm(ks[10], (L, CV_WIDTH), 0.02),
        "cv_ln_g": gain(ks[11], (L, CV_WIDTH)),
        "cv_ln_b": nrm(ks[12], (L, CV_WIDTH), 0.02),
        "w_out": nrm(ks[13], (L, MIX_WIDTH, D_MODEL), MIX_WIDTH ** -0.5),
        "norm_cross_g": gain(ks[14], (L, D_MODEL)),
        "norm_mem_g": gain(ks[15], (L, D_MODEL)),
        "w_cq": nrm(ks[16], (L, D_MODEL, D_MODEL), D_MODEL ** -0.5),
        "w_ckv": nrm(ks[17], (L, D_MODEL, 2 * D_MODEL), D_MODEL ** -0.5),
        "w_co": nrm(ks[18], (L, D_MODEL, D_MODEL), D_MODEL ** -0.5),
        "norm_ffn_g": gain(ks[19], (L, D_MODEL)),
        "w_up": nrm(ks[20], (L, D_MODEL, 2 * D_FF), D_MODEL ** -0.5),
        "ffn_dw_w": nrm(ks[21], (L, FFN_KERNEL, 2 * D_FF), FFN_KERNEL ** -0.5),
        "w_down": nrm(ks[22], (L, D_FF, D_MODEL), D_FF ** -0.5),
        "norm_final_g": gain(ks[23], (D_MODEL,)),
    }


def reference(x, mem, positions, norm_mix_g, w_in, lam_q1, lam_k1, lam_q2, lam_k2,
              subln_g, cv_dw_w, cv_dw_b, cv_ln_g, cv_ln_b, w_out,
              norm_cross_g, norm_mem_g, w_cq, w_ckv, w_co,
              norm_ffn_g, w_up, ffn_dw_w, w_down, norm_final_g):
    h = x
    for l in range(DEPTH):
        h = h + hybrid_mixer(rmsnorm(h, norm_mix_g[l]), positions, l, w_in[l],
                             lam_q1[l], lam_k1[l], lam_q2[l], lam_k2[l], subln_g[l],
                             cv_dw_w[l], cv_dw_b[l], cv_ln_g[l], cv_ln_b[l], w_out[l])
        h = h + cross_attention(rmsnorm(h, norm_cross_g[l]), rmsnorm(mem, norm_mem_g[l]),
                                w_cq[l], w_ckv[l], w_co[l])
        h = h + conv_ffn(rmsnorm(h, norm_ffn_g[l]), w_up[l], ffn_dw_w[l], w_down[l])
    return rmsnorm(h, norm_final_g)
```

```python
import numpy as np
import ml_dtypes
from contextlib import ExitStack
import concourse.bass as bass
import concourse.mybir as mybir
from concourse.bass_utils import run_bass_kernel_spmd

F32 = mybir.dt.float32
BF16 = mybir.dt.bfloat16
I32 = mybir.dt.int32
ALU = mybir.AluOpType
AF = mybir.ActivationFunctionType

NDS = 24


class Res:
    __slots__ = ("name", "ap", "w", "r")

    def __init__(self, name, ap=None):
        self.name = name
        self.ap = ap
        self.w = {}
        self.r = {}


class Trk:
    def __init__(self, nc, es):
        self.nc = nc
        self.es = es
        self.eng = {"pe": nc.tensor, "act": nc.scalar, "dve": nc.vector, "pool": nc.gpsimd, "sp": nc.sync}
        self.cnt = {k: 0 for k in self.eng}
        self.sem = {k: es.enter_context(nc.semaphore("s_" + k)) for k in self.eng if k != "sp"}
        self.waited = {k: {} for k in self.eng}
        self.dsem = [es.enter_context(nc.semaphore("d%d" % i)) for i in range(NDS)]
        self.dcnt = [0] * NDS
        self.rr = 0
        self.nres = 0
        self._cst = {}
        self.n_ins = 0
        self.log = {k: [] for k in self.eng}

    def sb(self, name, shape, dt):
        self.nres += 1
        t = self.es.enter_context(self.nc.sbuf_tensor("sb%d_%s" % (self.nres, name), list(shape), dt))
        return Res(name, t)

    def ps(self, name, shape, dt):
        self.nres += 1
        t = self.es.enter_context(self.nc.psum_tensor("ps%d_%s" % (self.nres, name), list(shape), dt))
        return Res(name, t)

    def cst(self, val):
        key = float(val)
        if key not in self._cst:
            r = self.sb("cst%d" % len(self._cst), [128, 1], F32)
            self.op("pool", lambda e: e.memset(r.ap[:, :], key), [], [r])
            self._cst[key] = r
        r = self._cst[key]
        return r

    def _sync(self, eng, reads, writes):
        deps = {}

        def add(tag):
            k, sem, val = tag
            if k not in deps or deps[k][1] < val:
                deps[k] = (sem, val)

        for r in reads:
            for t in r.w.values():
                add(t)
        import os
        strict = bool(os.environ.get("KDBG_WAW"))
        for w in writes:
            for t in w.w.values():
                if t[0] != eng or strict:
                    add(t)
            for t in w.r.values():
                if t[0] != eng or strict:
                    add(t)
        e = self.eng[eng]
        for k, (sem, val) in deps.items():
            if k == eng and eng == "pe":
                continue
            if self.waited[eng].get(k, 0) >= val:
                continue
            e.wait_ge(sem, val)
            self.log[eng].append(("wait", id(sem), val, k))
            self.waited[eng][k] = val
            self.n_ins += 1

    def _mark(self, tag, reads, writes):
        k = tag[0]
        for r in reads:
            if k not in r.r or r.r[k][2] < tag[2]:
                r.r[k] = tag
        for w in writes:
            if k not in w.w or w.w[k][2] < tag[2]:
                w.w[k] = tag
            w.r = {}

    def op(self, eng, fn, reads=(), writes=(), inc=True):
        reads = [x for x in reads if x is not None]
        writes = [x for x in writes if x is not None]
        self._sync(eng, reads, writes)
        ins = fn(self.eng[eng])
        self.n_ins += 1
        if inc:
            self.cnt[eng] += 1
            ins.then_inc(self.sem[eng], 1)
            self.log[eng].append(("inc", id(self.sem[eng]), 1, eng))
            val = self.cnt[eng]
        else:
            val = self.cnt[eng] + 1
        self._mark((eng, self.sem[eng], val), reads, writes)
        return ins

    def dma(self, queue, out_ap, in_ap, reads=(), writes=(), slow=False):
        reads = [x for x in reads if x is not None]
        writes = [x for x in writes if x is not None]
        self._sync(queue, reads, writes)
        i = self.rr
        self.rr = (i + 1) % NDS
        self.dcnt[i] += 16
        self.eng[queue].dma_start(out=out_ap, in_=in_ap, allow_slow_non_contiguous=slow).then_inc(self.dsem[i], 16)
        self.log[queue].append(("inc", id(self.dsem[i]), 16, ("dma", i)))
        self.n_ins += 1
        self._mark((("dma", i), self.dsem[i], self.dcnt[i]), reads, writes)

    def finish(self):
        e = self.eng["sp"]
        for i in range(NDS):
            if self.dcnt[i]:
                e.wait_ge(self.dsem[i], self.dcnt[i])
        for k in self.sem:
            if self.cnt[k]:
                e.wait_ge(self.sem[k], self.cnt[k])

    def barrier(self):
        for k, e in self.eng.items():
            for i in range(NDS):
                if self.dcnt[i] and self.waited[k].get(("dma", i), 0) < self.dcnt[i]:
                    e.wait_ge(self.dsem[i], self.dcnt[i])
                    self.log[k].append(("wait", id(self.dsem[i]), self.dcnt[i], ("dma", i)))
                    self.waited[k][("dma", i)] = self.dcnt[i]
            for k2 in self.sem:
                if k2 != k and self.cnt[k2] and self.waited[k].get(k2, 0) < self.cnt[k2]:
                    e.wait_ge(self.sem[k2], self.cnt[k2])
                    self.log[k].append(("wait", id(self.sem[k2]), self.cnt[k2], k2))
                    self.waited[k][k2] = self.cnt[k2]


class _Scope:
    def __init__(self, T):
        self.T = T

    def __enter__(self):
        self.old = self.T.es
        self.st = ExitStack()
        self.st.__enter__()
        self.T.es = self.st
        return self

    def __exit__(self, *a):
        self.T.barrier()
        self.T.es = self.old
        return self.st.__exit__(*a)


D = 1024
EPS = 1e-6
BIG = -30000.0
ROPE_THETA = 500000.0
D_FF = 2816
NFC = D_FF // 128
CVK = 31


class Rot:
    def __init__(self, items):
        self.items = items
        self.i = 0

    def next(self):
        r = self.items[self.i % len(self.items)]
        self.i += 1
        return r


def bcast_mid(ap2d, n):
    l = [list(x) for x in ap2d.ap]
    return bass.AP(ap2d.tensor, ap2d.offset, [l[0], [0, n], l[-1]])


def load_w(T, dst, kcn, src2d, col0, ncols, gain, stg, eng="pool"):
    CH = stg.items[0].ap.shape[1]
    for kc in range(kcn):
        for c0 in range(0, ncols, CH):
            cw = min(CH, ncols - c0)
            st = stg.next()
            T.dma("sp", st.ap[:, 0:cw], src2d[kc * 128:(kc + 1) * 128, col0 + c0:col0 + c0 + cw], [], [st])
            if gain is not None:
                T.op(eng, lambda e, st=st, kc=kc, c0=c0, cw=cw: e.tensor_scalar(
                    out=dst.ap[:, kc, c0:c0 + cw], in0=st.ap[:, 0:cw], scalar1=gain.ap[:, kc:kc + 1], scalar2=0.0,
                    op0=ALU.mult, op1=ALU.add), [st, gain], [dst])
            else:
                T.op(eng, lambda e, st=st, kc=kc, c0=c0, cw=cw: e.tensor_copy(
                    out=dst.ap[:, kc, c0:c0 + cw], in_=st.ap[:, 0:cw]), [st], [dst])


def load_wd(T, dst, kcn, src2d, col0, ncols):
    import os
    if os.environ.get("KDBG_POOLW"):
        if not hasattr(T, "_stg"):
            T._stg = None
        stg = Rot([T.sb("wstg%d" % i, [128, 1024], F32) for i in range(2)])
        for kc in range(kcn):
            for c0 in range(0, ncols, 1024):
                cw = min(1024, ncols - c0)
                st = stg.next()
                T.dma("sp", st.ap[:, 0:cw], src2d[kc * 128:(kc + 1) * 128, col0 + c0:col0 + c0 + cw], [], [st])
                T.op("pool", lambda e, st=st, kc=kc, c0=c0, cw=cw: e.tensor_copy(out=dst.ap[:, kc, c0:c0 + cw], in_=st.ap[:, 0:cw]), [st], [dst])
        return
    CH = 512
    for kc in range(kcn):
        for c0 in range(0, ncols, CH):
            cw = min(CH, ncols - c0)
            T.dma("pool", dst.ap[:, kc, c0:c0 + cw], src2d[kc * 128:(kc + 1) * 128, col0 + c0:col0 + c0 + cw], [], [dst])


def load_wb(T, dst, kcn, src_bf, res, ncols, col0=0):
    for kc in range(kcn):
        T.dma("sp", dst.ap[:, kc, 0:ncols], src_bf[kc * 128:(kc + 1) * 128, col0:col0 + ncols], [res], [dst])


class NormT:
    def __init__(self, T, ident, neghalf, g8, nbuf=3, nptr=2):
        self.T = T
        self.ident = ident
        self.neghalf = neghalf
        self.g8 = g8
        self.junk = T.sb("nt_junk", [128, D], BF16)
        self.ss = Rot([T.sb("nt_ss%d" % i, [128, 4], F32) for i in range(nbuf)])
        self.xs = Rot([T.sb("nt_xs%d" % i, [128, D], BF16) for i in range(2)])
        self.ptr = Rot([T.ps("nt_ptr%d" % i, [128, D], BF16) for i in range(nptr)])

    def stage_a(self, src_ap, src_res):
        T = self.T
        ss = self.ss.next()
        T.op("act", lambda e: e.activation(out=self.junk.ap[:, :], in_=src_ap, func=AF.Square, accum_out=ss.ap[:, 0:1]),
             [src_res], [ss])
        T.op("dve", lambda e: e.tensor_scalar(out=ss.ap[:, 1:2], in0=ss.ap[:, 0:1], scalar1=1.0 / D, scalar2=EPS,
                                              op0=ALU.mult, op1=ALU.add), [ss], [ss])
        T.op("pool", lambda e: e.tensor_tensor(out=ss.ap[:, 2:3], in0=ss.ap[:, 1:2], in1=self.neghalf.ap[:, 0:1], op=ALU.pow),
             [ss, self.neghalf], [ss])
        return ss

    def stage_b(self, src_ap, src_res, ss, dstT, col0, g8=None):
        T = self.T
        xs = self.xs.next()
        ptr = self.ptr.next()
        T.op("act", lambda e: e.activation(out=xs.ap[:, :], in_=src_ap, func=AF.Copy, scale=ss.ap[:, 2:3]),
             [src_res, ss], [xs])
        for kc in range(8):
            T.op("pe", lambda e, kc=kc: e.transpose(out=ptr.ap[:, kc * 128:(kc + 1) * 128],
                                                    in_=xs.ap[:, kc * 128:(kc + 1) * 128], identity=self.ident.ap[:, :]),
                 [xs, self.ident], [ptr], inc=(kc == 7))
        g8 = self.g8 if g8 is None else g8
        l = [list(x) for x in g8.ap[:, :].ap]
        gb = bass.AP(g8.ap[:, :].tensor, g8.ap[:, :].offset, [l[0], l[1], [0, 128]])
        T.op("dve", lambda e: e.tensor_tensor(out=dstT.ap[:, :, col0:col0 + 128], in0=ptr.ap[:, :].rearrange("p (c t) -> p c t", c=8),
                                              in1=gb, op=ALU.mult), [ptr, g8], [dstT])


def build(S):
    NT = S // 128
    OWN = NT // 4
    NQ = OWN + 1
    NQT = NQ * 128
    NMEM = 256
    nc = bass.Bass("TRN2", target_bir_lowering=False)

    def din(name, shape, dt=F32):
        return nc.dram_tensor(name, list(shape), dt, kind="ExternalInput").ap()

    x_b = din("x_b", [NT, 128, D])
    x_kv = din("x_own", [NQ, 128, D])
    pos_kv = din("pos_b", [128, NT], I32)
    sel_d = din("sel", [128, 4])
    hvalid_d = din("hvalid", [128, 1])
    ident_d = din("ident", [128, 128])
    tri_d = din("tri", [128, 128])
    mem_d = din("mem", [NMEM, D])
    norm_mix_g = din("norm_mix_g", [D])
    w_qkv = din("w_qkv", [D, 384])
    w_in = din("w_u", [D, 1024])
    lam_d = [din(n, [64]) for n in ("lam_q1", "lam_k1", "lam_q2", "lam_k2")]
    subln_g = din("subln_g", [128])
    cv_dw_w = din("cv_dw_w", [CVK, 512])
    cv_dw_b = din("cv_dw_b", [512])
    cv_ln_g = din("cv_ln_g", [512])
    cv_ln_b = din("cv_ln_b", [512])
    w_out = din("w_out", [D, D])
    norm_cross_g = din("norm_cross_g", [D])
    norm_mem_g = din("norm_mem_g", [D])
    w_cq = din("w_cq", [D, D])
    w_ckv = din("w_ckv", [D, 2 * D])
    w_co = din("w_co", [D, D])
    norm_ffn_g = din("norm_ffn_g", [D])
    w_up = din("w_up", [D, 2 * D_FF])
    ffn_dw_w = din("ffn_dw_w", [3, 2 * D_FF])
    w_down = din("w_down", [D_FF, D])
    norm_final_g = din("norm_final_g", [D])
    out_d = nc.dram_tensor("out", [OWN, 128, D], F32, kind="ExternalOutput").ap()

    pieces = []
    _p = 0
    while _p < NQ:
        _w = min(7, NQ - _p)
        pieces.append((_p, _w))
        _p += _w

    def piece_of(pos):
        for i_, (s_, w_) in enumerate(pieces):
            if s_ <= pos < s_ + w_:
                return i_, pos - s_

    sendp = [nc.dram_tensor("sendb%d" % i, [4 * 128, w * 128], BF16, kind="Internal").ap() for i, (s_, w) in enumerate(pieces)]
    recvp = [nc.dram_tensor("recvb%d" % i, [16 * 128, w * 128], BF16, kind="Internal").ap() for i, (s_, w) in enumerate(pieces)]
    wsrc = {"u": (w_in, D, 1024), "out": (w_out, D, D), "cq": (w_cq, D, D), "co": (w_co, D, D), "ckv": (w_ckv, D, 2 * D),
            "up": (w_up, D, 2 * D_FF), "dn": (w_down, D_FF, D)}
    wbf = {k: nc.dram_tensor("wbf_" + k, [r_, c_], BF16, kind="Internal").ap() for k, (a_, r_, c_) in wsrc.items()}
    wres = {k: Res("wbf_" + k) for k in wsrc}
    h1_scr = nc.dram_tensor("h1_scr", [NQ, 128, D], F32, kind="Internal").ap()
    h2_scr = nc.dram_tensor("h2_scr", [NQ, 128, D], F32, kind="Internal").ap()
    sendr = [Res("sendr%d" % i) for i in range(len(pieces))]
    recvr = [Res("recvr%d" % i) for i in range(len(pieces))]
    h1r = [Res("h1r%d" % s) for s in range(NQ)]
    h2r = [Res("h2r%d" % s) for s in range(NQ)]

    inv_freq = (np.float32(ROPE_THETA) ** (-(np.arange(0, 16, 2, dtype=np.float32)) / np.float32(16))).astype(np.float32)
    TWO_PI = float(2 * np.pi)

    with ExitStack() as es:
        T = Trk(nc, es)
        ident = T.sb("ident", [128, 128], BF16)
        tri = T.sb("tri", [128, 128], BF16)
        neghalf = T.sb("neghalf", [128, 1], F32)
        neg_lam = T.sb("neg_lam", [128, 1], F32)
        gsub = T.sb("gsub", [128, 128], F32)
        sel = T.sb("sel", [128, 4], F32)
        hvalid = T.sb("hvalid", [128, 1], F32)
        T.op("pool", lambda e: e.memset(neghalf.ap[:, :], -0.5), [], [neghalf])
        T.dma("sp", sel.ap[:, :], sel_d[:, :], [], [sel])
        T.dma("sp", hvalid.ap[:, :], hvalid_d[:, :], [], [hvalid])

        def load_vec(dst, src1d):
            T.dma("sp", dst.ap[:, :], src1d.rearrange("(c p) -> p c", p=128), [], [dst], slow=True)

        with _Scope(T):
            tmpf = T.sb("tmpf", [128, 128], F32)
            tmpf2 = T.sb("tmpf2", [128, 128], F32)
            T.dma("sp", tmpf.ap[:, :], ident_d[:, :], [], [tmpf])
            T.dma("sp", tmpf2.ap[:, :], tri_d[:, :], [], [tmpf2])
            T.op("dve", lambda e: e.tensor_copy(out=ident.ap[:, :], in_=tmpf.ap[:, :]), [tmpf], [ident])
            T.op("dve", lambda e: e.tensor_copy(out=tri.ap[:, :], in_=tmpf2.ap[:, :]), [tmpf2], [tri])
            lv = [T.sb("lv%d" % i, [128, 64], F32) for i in range(4)]
            for i in range(4):
                T.dma("sp", lv[i].ap[:, :], lam_d[i].partition_broadcast(128), [], [lv[i]])
            lsum = T.sb("lsum", [128, 4], F32)
            ljunk = T.sb("ljunk", [128, 64], F32)
            for i in range(2):
                T.op("dve", lambda e, i=i: e.tensor_tensor(out=ljunk.ap[:, :], in0=lv[2 * i].ap[:, :], in1=lv[2 * i + 1].ap[:, :],
                                                        op=ALU.mult), [lv[2 * i], lv[2 * i + 1]], [ljunk])
                T.op("dve", lambda e, i=i: e.tensor_reduce(out=lsum.ap[:, i:i + 1], in_=ljunk.ap[:, :],
                                                        axis=mybir.AxisListType.X, op=ALU.add), [ljunk], [lsum])
            T.op("act", lambda e: e.activation(out=lsum.ap[:, 2:4], in_=lsum.ap[:, 0:2], func=AF.Exp), [lsum], [lsum])
            T.op("dve", lambda e: e.scalar_tensor_tensor(out=neg_lam.ap[:, :], in0=lsum.ap[:, 3:4], scalar=-0.2, in1=lsum.ap[:, 2:3],
                                                         op0=ALU.add, op1=ALU.subtract), [lsum], [neg_lam])
            T.dma("sp", tmpf.ap[:, :], subln_g.partition_broadcast(128), [ident], [tmpf])
            T.op("dve", lambda e: e.tensor_scalar(out=gsub.ap[:, :], in0=tmpf.ap[:, :], scalar1=0.8, scalar2=None, op0=ALU.mult),
                 [tmpf], [gsub])

        scopeA = _Scope(T)
        scopeA.__enter__()
        QTz = T.sb("QTz", [128, 2, S], BF16)
        KT = T.sb("KT", [128, S], BF16)
        VS = T.sb("VS", [128, NT, 129], BF16)
        T.op("pool", lambda e: e.memset(QTz.ap[:, :, :], 0.0), [], [QTz])
        T.op("pool", lambda e: e.memset(VS.ap[:, :, 128:129], 1.0), [], [VS])
        with _Scope(T):
            gmix = T.sb("gmix", [128, 8], F32)
            load_vec(gmix, norm_mix_g)
            nt = NormT(T, ident, neghalf, gmix, nbuf=4)
            Wq = T.sb("Wq", [128, 8, 384], BF16)
            for kc in range(8):
                T.dma("pool", Wq.ap[:, kc, :], w_qkv[kc * 128:(kc + 1) * 128, :], [], [Wq])
            posi = T.sb("posi", [128, NT], I32)
            posf = T.sb("posf", [128, NT], F32)
            ang = T.sb("ang", [128, NT, 8], F32)
            kfi = T.sb("kfi", [128, NT, 8], I32)
            kf = T.sb("kf", [128, NT, 8], F32)
            mk = T.sb("mk", [128, NT, 8], F32)
            cosT = T.sb("cosT", [128, NT, 8], F32)
            sinT = T.sb("sinT", [128, NT, 8], F32)
            T.dma("sp", posi.ap[:, :], pos_kv[:, :], [], [posi])
            T.op("dve", lambda e: e.tensor_copy(out=posf.ap[:, :], in_=posi.ap[:, :]), [posi], [posf])
            for j in range(8):
                T.op("dve", lambda e, j=j: e.tensor_scalar(out=ang.ap[:, :, j], in0=posf.ap[:, :], scalar1=float(inv_freq[j]),
                                                        scalar2=None, op0=ALU.mult), [posf], [ang])
            T.op("dve", lambda e: e.tensor_scalar(out=kfi.ap[:, :, :], in0=ang.ap[:, :, :], scalar1=1.0 / TWO_PI, scalar2=None,
                                                  op0=ALU.mult), [ang], [kfi])
            T.op("dve", lambda e: e.tensor_copy(out=kf.ap[:, :, :], in_=kfi.ap[:, :, :]), [kfi], [kf])
            T.op("dve", lambda e: e.scalar_tensor_tensor(out=ang.ap[:, :, :], in0=kf.ap[:, :, :], scalar=-TWO_PI, in1=ang.ap[:, :, :],
                                                         op0=ALU.mult, op1=ALU.add), [kf, ang], [ang])

            def wrap(dst, shift):
                T.op("dve", lambda e: e.tensor_scalar(out=dst.ap[:, :, :], in0=ang.ap[:, :, :], scalar1=float(shift), scalar2=None,
                                                      op0=ALU.add), [ang], [dst])
                T.op("dve", lambda e: e.tensor_scalar(out=mk.ap[:, :, :], in0=dst.ap[:, :, :], scalar1=float(np.pi), scalar2=-TWO_PI,
                                                      op0=ALU.is_gt, op1=ALU.mult), [dst], [mk])
                T.op("dve", lambda e: e.tensor_tensor(out=dst.ap[:, :, :], in0=dst.ap[:, :, :], in1=mk.ap[:, :, :], op=ALU.add),
                     [dst, mk], [dst])
                T.op("dve", lambda e: e.tensor_scalar(out=mk.ap[:, :, :], in0=dst.ap[:, :, :], scalar1=float(-np.pi), scalar2=TWO_PI,
                                                      op0=ALU.is_lt, op1=ALU.mult), [dst], [mk])
                T.op("dve", lambda e: e.tensor_tensor(out=dst.ap[:, :, :], in0=dst.ap[:, :, :], in1=mk.ap[:, :, :], op=ALU.add),
                     [dst, mk], [dst])
                T.op("act", lambda e: e.activation(out=dst.ap[:, :, :], in_=dst.ap[:, :, :], func=AF.Sin), [dst], [dst])

            wrap(sinT, 0.0)
            wrap(cosT, np.pi / 2)

            xt = Rot([T.sb("xt%d" % i, [128, D], F32) for i in range(4)])
            xnT = Rot([T.sb("xnT%d" % i, [128, 8, 128], BF16) for i in range(3)])
            ps_p = Rot([T.ps("ps_p%d" % i, [128, 512], F32) for i in range(2)])
            ps_t = Rot([T.ps("ps_t%d" % i, [128, 1024], BF16) for i in range(2)])
            qksb = Rot([T.sb("qksb%d" % i, [128, 256], BF16) for i in range(2)])
            tA = Rot([T.sb("tA%d" % i, [128, 4, 8], F32) for i in range(4)])
            tB = Rot([T.sb("tB%d" % i, [128, 4, 8], F32) for i in range(4)])

            def stage_a(sl):
                xt_ = xt.next()
                T.dma("sp", xt_.ap[:, :], x_b[sl], [], [xt_])
                return xt_, nt.stage_a(xt_.ap[:, :], xt_)

            def stage_b(sl, a):
                xt_, ss = a
                xn = xnT.next()
                nt.stage_b(xt_.ap[:, :], xt_, ss, xn, 0)
                ps = ps_p.next()
                for kc in range(8):
                    T.op("pe", lambda e, kc=kc: e.matmul(ps.ap[:, 0:384], lhsT=xn.ap[:, kc, :], rhs=Wq.ap[:, kc, :],
                                                         start=(kc == 0), stop=(kc == 7)), [xn, Wq], [ps], inc=(kc == 7))
                return ps

            def stage_c(sl, ps):
                T.op("dve", lambda e: e.tensor_copy(out=VS.ap[:, sl, 0:128], in_=ps.ap[:, 256:384]), [ps], [VS])
                sb_ = qksb.next()
                T.op("dve", lambda e: e.tensor_copy(out=sb_.ap[:, :], in_=ps.ap[:, 0:256]), [ps], [sb_])
                v3 = ps.ap[:, 0:256].rearrange("p (g d) -> p g d", g=4)
                o3 = sb_.ap[:, :].rearrange("p (g d) -> p g d", g=4)
                t1 = v3[:, :, 0:8]
                t2 = v3[:, :, 8:16]
                cb = bcast_mid(cosT.ap[:, sl, :], 4)
                sbb = bcast_mid(sinT.ap[:, sl, :], 4)
                a1 = tA.next()
                b1 = tB.next()
                a2 = tA.next()
                b2 = tB.next()
                TT = lambda o, i0, i1, op, rd, wr: T.op("dve", lambda e: e.tensor_tensor(out=o, in0=i0, in1=i1, op=op), rd, wr)
                TT(a1.ap[:, :, :], t1, cb, ALU.mult, [ps, cosT], [a1])
                TT(b1.ap[:, :, :], t2, sbb, ALU.mult, [ps, sinT], [b1])
                TT(a2.ap[:, :, :], t2, cb, ALU.mult, [ps, cosT], [a2])
                TT(b2.ap[:, :, :], t1, sbb, ALU.mult, [ps, sinT], [b2])
                TT(o3[:, :, 0:8], a1.ap[:, :, :], b1.ap[:, :, :], ALU.subtract, [a1, b1], [sb_])
                TT(o3[:, :, 8:16], a2.ap[:, :, :], b2.ap[:, :, :], ALU.add, [a2, b2], [sb_])
                pt = ps_t.next()
                for h in range(2):
                    T.op("pe", lambda e, h=h: e.transpose(out=pt.ap[:, h * 128:(h + 1) * 128], in_=sb_.ap[:, h * 128:(h + 1) * 128],
                                                          identity=ident.ap[:, :]), [sb_, ident], [pt], inc=(h == 1))
                for m in range(2):
                    T.op("dve", lambda e, m=m: e.tensor_copy(out=QTz.ap[64 * m:64 * m + 64, m, sl * 128:(sl + 1) * 128],
                                                          in_=pt.ap[64 * m:64 * m + 64, 0:128]), [pt], [QTz])
                T.op("dve", lambda e: e.tensor_copy(out=KT.ap[:, sl * 128:(sl + 1) * 128], in_=pt.ap[:, 128:256]), [pt], [KT])

            sa = {0: stage_a(0)}
            if NT > 1:
                sa[1] = stage_a(1)
            sbq = {0: stage_b(0, sa.pop(0))}
            for sl in range(NT):
                if sl + 2 < NT:
                    sa[sl + 2] = stage_a(sl + 2)
                if sl + 1 < NT:
                    sbq[sl + 1] = stage_b(sl + 1, sa.pop(sl + 1))
                stage_c(sl, sbq.pop(sl))

        import os
        KSTOP = int(os.environ.get("KDBG_STOP", "99"))
        if KSTOP == 1:
            T.finish()
            scopeA.__exit__(None, None, None)
            return nc, T
        GQ = 2
        groups = [list(range(i, i + GQ)) for i in range(0, NT, GQ)]
        LOOK = 2
        with _Scope(T):
            ps_s = Rot([T.ps("ps_s%d" % i, [128, 2, 128 * GQ], F32) for i in range(LOOK + 1)])
            ps_o = [T.ps("ps_o%d" % i, [128, 512], F32) for i in range(GQ)]
            ps_at = Rot([T.ps("ps_at%d" % i, [128, 1024], BF16) for i in range(1)])
            PT = Rot([T.sb("PT%d" % i, [128, 2, 128 * GQ], BF16) for i in range(4)])
            osb = Rot([T.sb("osb%d" % i, [128, 258], F32) for i in range(2)])
            sm = Rot([T.sb("sm%d" % i, [128, 8], F32) for i in range(2)])
            af = Rot([T.sb("af%d" % i, [128, 128], F32) for i in range(2)])
            tf = Rot([T.sb("tf%d" % i, [128, 128], F32) for i in range(2)])
            anb = Rot([T.sb("anb%d" % i, [128, 128], BF16) for i in range(2)])
            ast = Rot([T.sb("ast%d" % i, [128, 128 * GQ], BF16) for i in range(3)])
            precast = []
            for k_ in ("u", "out", "cq", "co", "ckv", "up", "dn"):
                a_, r_, c_ = wsrc[k_]
                for r0 in range(0, r_, 128):
                    for c0 in range(0, c_, 512):
                        cw_ = min(512, c_ - c0)
                        precast.append((k_, r0, c0, cw_))
            n_pre_groups = max(1, len(groups) - min(8, len(groups) - 1))
            per_group = -(-len(precast) // n_pre_groups)
            zt = T.sb("zt", [128, 128], BF16)
            T.op("pool", lambda e: e.memset(zt.ap[:, :], 0.0), [], [zt])
            T.dma("pool", sendp[0][0:128, 0:128], zt.ap[:, :], [zt], [sendr[0]])
            for gi, grp in enumerate(groups):
                if gi >= len(groups) - n_pre_groups:
                    for _ in range(per_group):
                        if precast:
                            k_, r0, c0, cw_ = precast.pop(0)
                            T.dma("pool", wbf[k_][r0:r0 + 128, c0:c0 + cw_], wsrc[k_][0][r0:r0 + 128, c0:c0 + cw_], [], [wres[k_]])
                i0 = grp[0]
                G = len(grp)
                keys = []
                for k in range(0, i0 + G):
                    j0 = max(0, k - i0)
                    keys.append((k, j0, k >= i0))
                nk = len(keys)
                started = [False] * G
                a_st = ast.next()

                def qk(idx):
                    k, j0, dg = keys[idx]
                    pss = ps_s.next()
                    n0 = j0 * 128
                    n1 = G * 128
                    for m in range(2):
                        T.op("pe", lambda e, m=m: e.matmul(pss.ap[:, m, n0:n1], lhsT=KT.ap[:, k * 128:(k + 1) * 128],
                                                           rhs=QTz.ap[:, m, i0 * 128 + n0:i0 * 128 + n1], start=True, stop=True),
                             [KT, QTz], [pss], inc=(m == 1))
                    return pss

                pend = [qk(i) for i in range(min(LOOK, nk))]
                for idx in range(nk):
                    k, j0, dg = keys[idx]
                    pss = pend.pop(0)
                    if idx + LOOK < nk:
                        pend.append(qk(idx + LOOK))
                    n0 = j0 * 128
                    n1 = G * 128
                    pt_ = PT.next()
                    T.op("act", lambda e: e.activation(out=pt_.ap[:, :, n0:n1], in_=pss.ap[:, :, n0:n1], func=AF.Exp, scale=0.125),
                         [pss], [pt_])
                    if dg:
                        for m in range(2):
                            T.op("dve", lambda e, m=m: e.tensor_tensor(out=pt_.ap[:, m, n0:n0 + 128], in0=pt_.ap[:, m, n0:n0 + 128],
                                                                    in1=tri.ap[:, :], op=ALU.mult), [pt_, tri], [pt_])
                    for j in range(j0, G):
                        last_key = (k == i0 + j)
                        for m in range(2):
                            st = not started[j]
                            started[j] = True
                            T.op("pe", lambda e, j=j, m=m, st=st, last_key=last_key: e.matmul(
                                ps_o[j].ap[:, m * 129:(m + 1) * 129], lhsT=pt_.ap[:, m, j * 128:(j + 1) * 128], rhs=VS.ap[:, k, :],
                                start=st, stop=(last_key and m == 1), skip_group_check=True),
                                 [pt_, VS], [ps_o[j]], inc=(m == 1 and j == G - 1) or (last_key and m == 1))
                        if last_key:
                            o_ = osb.next()
                            s_ = sm.next()
                            a_ = af.next()
                            t_ = tf.next()
                            n_ = anb.next()
                            pa = ps_at.next()
                            T.op("dve", lambda e, j=j: e.tensor_copy(out=o_.ap[:, :], in_=ps_o[j].ap[:, 0:258]), [ps_o[j]], [o_])
                            T.op("dve", lambda e: e.reciprocal(out=s_.ap[:, 0:1], in_=o_.ap[:, 128:129]), [o_], [s_])
                            T.op("dve", lambda e: e.reciprocal(out=s_.ap[:, 1:2], in_=o_.ap[:, 257:258]), [o_], [s_])
                            T.op("dve", lambda e: e.tensor_tensor(out=s_.ap[:, 2:3], in0=s_.ap[:, 1:2], in1=neg_lam.ap[:, 0:1], op=ALU.mult),
                                 [s_, neg_lam], [s_])
                            T.op("dve", lambda e: e.tensor_scalar(out=t_.ap[:, :], in0=o_.ap[:, 129:257], scalar1=s_.ap[:, 2:3], scalar2=None,
                                                                  op0=ALU.mult), [o_, s_], [t_])
                            T.op("dve", lambda e: e.scalar_tensor_tensor(out=a_.ap[:, :], in0=o_.ap[:, 0:128], scalar=s_.ap[:, 0:1],
                                                                         in1=t_.ap[:, :], op0=ALU.mult, op1=ALU.add), [o_, s_, t_], [a_])
                            T.op("dve", lambda e: e.tensor_tensor(out=t_.ap[:, :], in0=a_.ap[:, :], in1=a_.ap[:, :], op=ALU.mult), [a_], [t_])
                            T.op("dve", lambda e: e.tensor_reduce(out=s_.ap[:, 3:4], in_=t_.ap[:, :], axis=mybir.AxisListType.X, op=ALU.add),
                                 [t_], [s_])
                            T.op("dve", lambda e: e.tensor_scalar(out=s_.ap[:, 4:5], in0=s_.ap[:, 3:4], scalar1=1.0 / 128, scalar2=EPS,
                                                                  op0=ALU.mult, op1=ALU.add), [s_], [s_])
                            T.op("pool", lambda e: e.tensor_tensor(out=s_.ap[:, 5:6], in0=s_.ap[:, 4:5], in1=neghalf.ap[:, 0:1], op=ALU.pow),
                                 [s_, neghalf], [s_])
                            T.op("dve", lambda e: e.scalar_tensor_tensor(out=n_.ap[:, :], in0=a_.ap[:, :], scalar=s_.ap[:, 5:6],
                                                                         in1=gsub.ap[:, :], op0=ALU.mult, op1=ALU.mult), [a_, s_, gsub], [n_])
                            T.op("pe", lambda e: e.transpose(out=pa.ap[:, 0:128], in_=n_.ap[:, :], identity=ident.ap[:, :]), [n_, ident], [pa])
                            T.op("dve", lambda e, j=j: e.tensor_copy(out=a_st.ap[:, j * 128:(j + 1) * 128], in_=pa.ap[:, 0:128]), [pa], [a_st])
                for j in range(G):
                    tq = i0 + j
                    c = tq // OWN
                    pi, off = piece_of(tq % OWN + 1)
                    T.dma("pool", sendp[pi][c * 128:(c + 1) * 128, off * 128:(off + 1) * 128], a_st.ap[:, j * 128:(j + 1) * 128], [a_st], [sendr[pi]])
                    if tq % OWN == OWN - 1 and c < 3:
                        T.dma("pool", sendp[0][(c + 1) * 128:(c + 2) * 128, 0:128], a_st.ap[:, j * 128:(j + 1) * 128], [a_st], [sendr[0]])
        scopeA.__exit__(None, None, None)

        if KSTOP == 2:
            T.finish()
            return nc, T
        ccs = es.enter_context(nc.semaphore("ccs"))
        for pi in range(len(pieces)):
            T._sync("pool", [sendr[pi]], [recvr[pi]])
            nc.gpsimd.collective_compute("AllGather", ALU.bypass, replica_groups=[[0, 1, 2, 3], [4, 5, 6, 7]],
                                         ins=[sendp[pi].opt()], outs=[recvp[pi].opt()]).then_inc(ccs, 1)
            T.log["pool"].append(("inc", id(ccs), 1, ("cc", 0)))
            T._mark((("cc", 0), ccs, pi + 1), [sendr[pi]], [recvr[pi]])

        if KSTOP == 3:
            T.finish()
            return nc, T
        scopeX = _Scope(T)
        scopeX.__enter__()
        aT = T.sb("aT", [128, 4, NQT], BF16)
        if KSTOP == 4:
            T.finish()
            scopeX.__exit__(None, None, None)
            return nc, T

        def bcast_free(ap1, n):
            l = [list(x) for x in ap1.ap]
            return bass.AP(ap1.tensor, ap1.offset, [l[0], [0, n]])

        def load_vec(dst, src1d):
            T.dma("sp", dst.ap[:, :], src1d.rearrange("(c p) -> p c", p=128), [], [dst], slow=True)

        blocks = [[0]] + [list(range(i, min(i + 4, NQ))) for i in range(1, NQ, 4)]
        with _Scope(T):
            gmix = T.sb("gmix", [128, 8], F32)
            load_vec(gmix, norm_mix_g)
            Wu = T.sb("Wu", [128, 8, 1024], BF16)
            Wout = T.sb("Wout", [128, 8, 1024], BF16)
            load_wb(T, Wu, 8, wbf["u"], wres["u"], 1024)
            load_wb(T, Wout, 8, wbf["out"], wres["out"], 1024)
            cw = T.sb("cw", [128, 4, CVK], F32)
            for cc in range(4):
                T.dma("sp", cw.ap[:, cc, :], cv_dw_w[:, cc * 128:(cc + 1) * 128].rearrange("k p -> p k"), [], [cw], slow=True)
            Dg = T.sb("Dg", [128, 4, CVK, 128], BF16)
            for cc in range(4):
                for k in range(CVK):
                    T.op("dve", lambda e, cc=cc, k=k: e.tensor_scalar(out=Dg.ap[:, cc, k, :], in0=ident.ap[:, :], scalar1=cw.ap[:, cc, k:k + 1],
                                                                   scalar2=None, op0=ALU.mult), [ident, cw], [Dg])
            coT_all = T.sb("coT_all", [128, 4, NQT], BF16)
            with _Scope(T):
                nt = NormT(T, ident, neghalf, gmix)
                cvb = T.sb("cvb", [128, 4], F32)
                lng = T.sb("lng", [128, 4], F32)
                lnb = T.sb("lnb", [128, 4], F32)
                load_vec(cvb, cv_dw_b)
                load_vec(lng, cv_ln_g)
                load_vec(lnb, cv_ln_b)
                epst = T.sb("epst", [128, 1], F32)
                T.op("pool", lambda e: e.memset(epst.ap[:, :], EPS), [], [epst])
                onesf = T.sb("onesf", [128, 128], F32)
                T.op("pool", lambda e: e.memset(onesf.ap[:, :], 1.0 / 512), [], [onesf])
                cTb = [T.sb("cTb%d" % i, [128, 4, 32 + 512], BF16) for i in range(2)]
                T.op("pool", lambda e: e.memset(cTb[0].ap[:, :, 0:32], 0.0), [], [cTb[0]])
                xnT = Rot([T.sb("xnT%d" % i, [128, 8, 512], BF16) for i in range(2)])
                xt = Rot([T.sb("xt%d" % i, [128, D], F32) for i in range(3)])
                sig = Rot([T.sb("sig%d" % i, [128, 512], F32) for i in range(2)])
                yT = T.sb("yT", [128, 4, 512], F32)
                ysq = T.sb("ysq", [128, 4, 512], F32)
                mean_sb = T.sb("mean_sb", [128, 512], F32)
                msq = T.sb("msq", [128, 512], F32)
                rstd = T.sb("rstd", [128, 512], F32)
                dtm = Rot([T.sb("dtm%d" % i, [128, 512], F32) for i in range(2)])
                ps_g = Rot([T.ps("ps_g%d" % i, [128, 512], F32) for i in range(2)])
                ps_m = T.ps("ps_m", [128, 512], F32)
                ps_q2 = T.ps("ps_q2", [128, 512], F32)

                for bi, tl in enumerate(blocks):
                    N = len(tl) * 128
                    cur_c = cTb[bi % 2]
                    nxt_c = cTb[(bi + 1) % 2]
                    xn = xnT.next()
                    pend = []
                    for j, t in enumerate(tl):
                        xt_ = xt.next()
                        T.dma("sp", xt_.ap[:, :], x_kv[t], [], [xt_])
                        pend.append((xt_, nt.stage_a(xt_.ap[:, :], xt_)))
                        if j >= 1:
                            a, b = pend[j - 1]
                            nt.stage_b(a.ap[:, :], a, b, xn, (j - 1) * 128)
                    a, b = pend[-1]
                    nt.stage_b(a.ap[:, :], a, b, xn, (len(tl) - 1) * 128)
                    for jc in range(4):
                        pv = ps_g.next()
                        pg = ps_g.next()
                        for ps, col in ((pv, jc * 128), (pg, 512 + jc * 128)):
                            for kc in range(8):
                                T.op("pe", lambda e, ps=ps, col=col, kc=kc: e.matmul(ps.ap[:, 0:N], lhsT=Wu.ap[:, kc, col:col + 128], rhs=xn.ap[:, kc, 0:N],
                                                                                  start=(kc == 0), stop=(kc == 7)), [Wu, xn], [ps], inc=(kc == 7))
                        sg = sig.next()
                        T.op("act", lambda e: e.activation(out=sg.ap[:, 0:N], in_=pg.ap[:, 0:N], func=AF.Sigmoid), [pg], [sg])
                        T.op("dve", lambda e, jc=jc: e.tensor_tensor(out=cur_c.ap[:, jc, 32:32 + N], in0=pv.ap[:, 0:N], in1=sg.ap[:, 0:N], op=ALU.mult),
                             [pv, sg], [cur_c])
                    for cc in range(4):
                        py = ps_g.next()
                        for k in range(CVK):
                            T.op("pe", lambda e, cc=cc, k=k: e.matmul(py.ap[:, 0:N], lhsT=Dg.ap[:, cc, k, :], rhs=cur_c.ap[:, cc, 2 + k:2 + k + N],
                                                                  start=(k == 0), stop=(k == CVK - 1)), [Dg, cur_c], [py], inc=(k == CVK - 1))
                        T.op("act", lambda e, cc=cc: e.activation(out=yT.ap[:, cc, 0:N], in_=py.ap[:, 0:N], func=AF.Identity, bias=cvb.ap[:, cc:cc + 1]),
                             [py, cvb], [yT])
                        T.op("act", lambda e, cc=cc: e.activation(out=ysq.ap[:, cc, 0:N], in_=py.ap[:, 0:N], func=AF.Square, bias=cvb.ap[:, cc:cc + 1]),
                             [py, cvb], [ysq])
                    T.op("pool", lambda e: e.tensor_copy(out=nxt_c.ap[:, :, 0:32], in_=cur_c.ap[:, :, N:N + 32]), [cur_c], [nxt_c])
                    for ps, src in ((ps_m, yT), (ps_q2, ysq)):
                        for cc in range(4):
                            T.op("pe", lambda e, ps=ps, src=src, cc=cc: e.matmul(ps.ap[:, 0:N], lhsT=onesf.ap[:, :], rhs=src.ap[:, cc, 0:N],
                                                                              start=(cc == 0), stop=(cc == 3)), [onesf, src], [ps], inc=(cc == 3))
                    T.op("act", lambda e: e.copy(out=mean_sb.ap[:, 0:N], in_=ps_m.ap[:, 0:N]), [ps_m], [mean_sb])
                    T.op("dve", lambda e: e.tensor_tensor(out=msq.ap[:, 0:N], in0=mean_sb.ap[:, 0:N], in1=mean_sb.ap[:, 0:N], op=ALU.mult), [mean_sb], [msq])
                    T.op("dve", lambda e: e.tensor_tensor(out=msq.ap[:, 0:N], in0=ps_q2.ap[:, 0:N], in1=msq.ap[:, 0:N], op=ALU.subtract), [ps_q2, msq], [msq])
                    T.op("act", lambda e: e.activation(out=msq.ap[:, 0:N], in_=msq.ap[:, 0:N], func=AF.Sqrt, bias=epst.ap[:, 0:1]), [msq, epst], [msq])
                    T.op("dve", lambda e: e.reciprocal(out=rstd.ap[:, 0:N], in_=msq.ap[:, 0:N]), [msq], [rstd])
                    cbase = tl[0] * 128
                    for cc in range(4):
                        d_ = dtm.next()
                        T.op("dve", lambda e, cc=cc: e.tensor_tensor(out=d_.ap[:, 0:N], in0=yT.ap[:, cc, 0:N], in1=mean_sb.ap[:, 0:N], op=ALU.subtract),
                             [yT, mean_sb], [d_])
                        T.op("dve", lambda e: e.tensor_tensor(out=d_.ap[:, 0:N], in0=d_.ap[:, 0:N], in1=rstd.ap[:, 0:N], op=ALU.mult), [d_, rstd], [d_])
                        T.op("act", lambda e, cc=cc: e.activation(out=coT_all.ap[:, cc, cbase:cbase + N], in_=d_.ap[:, 0:N], func=AF.Silu, scale=lng.ap[:, cc:cc + 1],
                                                                  bias=lnb.ap[:, cc:cc + 1]), [d_, lng, lnb], [coT_all])
            with _Scope(T):
                cand = Rot([T.sb("cand%d" % i, [128, NQT], BF16) for i in range(3)])
                for h in range(4):
                    for j in range(4):
                        cd = cand.next()
                        for pi, (ps0, pw) in enumerate(pieces):
                            T.dma("sp", cd.ap[:, ps0 * 128:(ps0 + pw) * 128], recvp[pi][(h * 4 + j) * 128:(h * 4 + j + 1) * 128, :], [recvr[pi]], [cd])
                        eng = "dve" if (j % 2 == 0) else "pool"
                        if j == 0:
                            T.op("dve", lambda e, h=h, j=j: e.tensor_scalar(out=aT.ap[:, h, :], in0=cd.ap[:, :], scalar1=sel.ap[:, j:j + 1], scalar2=None,
                                                                         op0=ALU.mult), [cd, sel], [aT])
                        else:
                            T.op("dve", lambda e, h=h, j=j: e.scalar_tensor_tensor(out=aT.ap[:, h, :], in0=cd.ap[:, :], scalar=sel.ap[:, j:j + 1],
                                                                                in1=aT.ap[:, h, :], op0=ALU.mult, op1=ALU.add), [cd, sel, aT], [aT])

            with _Scope(T):
                xr = Rot([T.sb("xr%d" % i, [128, D], F32) for i in range(2)])
                h1t = Rot([T.sb("h1t%d" % i, [128, D], F32) for i in range(2)])
                ps_mx4 = [T.ps("ps_mx%d" % i, [128, 512], F32) for i in range(4)]
                for bi, tl in enumerate(blocks):
                    for j, t in enumerate(tl):
                        ps_mx = ps_mx4[(t % 2) * 2:(t % 2) * 2 + 2]
                        xr_ = xr.next()
                        T.dma("sp", xr_.ap[:, :], x_kv[t], [], [xr_])
                        for hf in range(2):
                            for kc in range(8):
                                lhsT = aT.ap[:, kc, t * 128:(t + 1) * 128] if kc < 4 else coT_all.ap[:, kc - 4, t * 128:(t + 1) * 128]
                                T.op("pe", lambda e, hf=hf, kc=kc, lhsT=lhsT: e.matmul(ps_mx[hf].ap[:, :], lhsT=lhsT, rhs=Wout.ap[:, kc, hf * 512:(hf + 1) * 512],
                                                                                    start=(kc == 0), stop=(kc == 7)), [aT, coT_all, Wout], [ps_mx[hf]], inc=(kc == 7))
                        h1_ = h1t.next()
                        for hf in range(2):
                            T.op("dve", lambda e, hf=hf: e.tensor_tensor(out=h1_.ap[:, hf * 512:(hf + 1) * 512], in0=xr_.ap[:, hf * 512:(hf + 1) * 512],
                                                                      in1=ps_mx[hf].ap[:, :], op=ALU.add), [xr_, ps_mx[hf]], [h1_])
                        T.dma("pool", h1_scr[t], h1_.ap[:, :], [h1_], [h1r[t]])
        scopeX.__exit__(None, None, None)

        with _Scope(T):
            Wcq = T.sb("Wcq", [128, 8, 1024], BF16)
            Wco = T.sb("Wco", [128, 8, 1024], BF16)
            Wckv = T.sb("Wckv", [128, 8, 2048], BF16)
            gcr = T.sb("gcr", [128, 8], F32)
            gme = T.sb("gme", [128, 8], F32)
            load_vec(gcr, norm_cross_g)
            load_vec(gme, norm_mem_g)
            nt = NormT(T, ident, neghalf, gcr)
            load_wb(T, Wckv, 8, wbf["ckv"], wres["ckv"], 2048)
            load_wb(T, Wcq, 8, wbf["cq"], wres["cq"], 1024)
            load_wb(T, Wco, 8, wbf["co"], wres["co"], 1024)
            onesb = T.sb("onesb", [128, 128], BF16)
            T.op("pool", lambda e: e.memset(onesb.ap[:, :], 1.0), [], [onesb])
            ps_g = Rot([T.ps("ps_g%d" % i, [128, 512], F32) for i in range(4)])
            ps_mx = [T.ps("ps_mx%d" % i, [128, 512], F32) for i in range(2)]
            xt = Rot([T.sb("xt%d" % i, [128, D], F32) for i in range(2)])
            memT = T.sb("memT", [128, 8, NMEM], BF16)
            KcT = T.sb("KcT", [128, 8, NMEM], BF16)
            Vc = T.sb("Vc", [128, 2, 1024], BF16)
            for mt in range(2):
                xt_ = xt.next()
                T.dma("sp", xt_.ap[:, :], mem_d[mt * 128:(mt + 1) * 128, :], [], [xt_])
                ss = nt.stage_a(xt_.ap[:, :], xt_)
                nt.stage_b(xt_.ap[:, :], xt_, ss, memT, mt * 128, g8=gme)
            for e8 in range(8):
                ps = ps_g.next()
                for kc in range(8):
                    T.op("pe", lambda e, kc=kc, e8=e8: e.matmul(ps.ap[:, 0:NMEM], lhsT=Wckv.ap[:, kc, e8 * 128:(e8 + 1) * 128], rhs=memT.ap[:, kc, :],
                                                            start=(kc == 0), stop=(kc == 7)), [Wckv, memT], [ps], inc=(kc == 7))
                T.op("dve", lambda e, e8=e8: e.tensor_copy(out=KcT.ap[:, e8, :], in_=ps.ap[:, 0:NMEM]), [ps], [KcT])
            for mt in range(2):
                for hf in range(2):
                    ps = ps_g.next()
                    for kc in range(8):
                        T.op("pe", lambda e, kc=kc, mt=mt, hf=hf: e.matmul(ps.ap[:, :], lhsT=memT.ap[:, kc, mt * 128:(mt + 1) * 128],
                                                                        rhs=Wckv.ap[:, kc, 1024 + hf * 512:1024 + (hf + 1) * 512],
                                                                        start=(kc == 0), stop=(kc == 7)), [Wckv, memT], [ps], inc=(kc == 7))
                    T.op("act", lambda e, mt=mt, hf=hf: e.copy(out=Vc.ap[:, mt, hf * 512:(hf + 1) * 512], in_=ps.ap[:, :]), [ps], [Vc])
            h1b = Rot([T.sb("h1b%d" % i, [128, 4, D], F32) for i in range(2)])
            h1nT = Rot([T.sb("h1nT%d" % i, [128, 8, 512], BF16) for i in range(2)])
            qcT = T.sb("qcT", [128, 8, 512], BF16)
            ocT = T.sb("ocT", [128, 8, 512], BF16)
            PcT = Rot([T.sb("PcT%d" % i, [128, 2, 512], BF16) for i in range(2)])
            rl = Rot([T.sb("rl%d" % i, [128, 512], F32) for i in range(2)])
            h2t = Rot([T.sb("h2t%d" % i, [128, D], F32) for i in range(2)])
            for bi, tl in enumerate(blocks):
                N = len(tl) * 128
                hb = h1b.next()
                hn = h1nT.next()
                pend = []
                for j, t in enumerate(tl):
                    T.dma("sp", hb.ap[:, j, :], h1_scr[t], [h1r[t]], [hb])
                    pend.append(nt.stage_a(hb.ap[:, j, :], hb))
                    if j >= 1:
                        nt.stage_b(hb.ap[:, j - 1, :], hb, pend[j - 1], hn, (j - 1) * 128)
                nt.stage_b(hb.ap[:, len(tl) - 1, :], hb, pend[-1], hn, (len(tl) - 1) * 128)
                for e8 in range(8):
                    ps = ps_g.next()
                    for kc in range(8):
                        T.op("pe", lambda e, kc=kc, e8=e8: e.matmul(ps.ap[:, 0:N], lhsT=Wcq.ap[:, kc, e8 * 128:(e8 + 1) * 128], rhs=hn.ap[:, kc, 0:N],
                                                                start=(kc == 0), stop=(kc == 7)), [Wcq, hn], [ps], inc=(kc == 7))
                    T.op("act", lambda e, e8=e8: e.copy(out=qcT.ap[:, e8, 0:N], in_=ps.ap[:, 0:N]), [ps], [qcT])
                for hh in range(4):
                    pc = PcT.next()
                    for mt in range(2):
                        ps = ps_g.next()
                        for dc in range(2):
                            T.op("pe", lambda e, mt=mt, dc=dc: e.matmul(ps.ap[:, 0:N], lhsT=KcT.ap[:, hh * 2 + dc, mt * 128:(mt + 1) * 128],
                                                                    rhs=qcT.ap[:, hh * 2 + dc, 0:N], start=(dc == 0), stop=(dc == 1)),
                                 [KcT, qcT], [ps], inc=(dc == 1))
                        T.op("act", lambda e, mt=mt: e.activation(out=pc.ap[:, mt, 0:N], in_=ps.ap[:, 0:N], func=AF.Exp, scale=1.0 / 16), [ps], [pc])
                    pl = ps_g.next()
                    for mt in range(2):
                        T.op("pe", lambda e, mt=mt: e.matmul(pl.ap[:, 0:N], lhsT=onesb.ap[:, :], rhs=pc.ap[:, mt, 0:N], start=(mt == 0), stop=(mt == 1)),
                             [onesb, pc], [pl], inc=(mt == 1))
                    rl_ = rl.next()
                    T.op("dve", lambda e: e.reciprocal(out=rl_.ap[:, 0:N], in_=pl.ap[:, 0:N]), [pl], [rl_])
                    for dc in range(2):
                        po = ps_g.next()
                        for mt in range(2):
                            T.op("pe", lambda e, mt=mt, dc=dc: e.matmul(po.ap[:, 0:N], lhsT=Vc.ap[:, mt, hh * 256 + dc * 128:hh * 256 + (dc + 1) * 128],
                                                                    rhs=pc.ap[:, mt, 0:N], start=(mt == 0), stop=(mt == 1)), [Vc, pc], [po], inc=(mt == 1))
                        T.op("dve", lambda e, dc=dc: e.tensor_tensor(out=ocT.ap[:, hh * 2 + dc, 0:N], in0=po.ap[:, 0:N], in1=rl_.ap[:, 0:N], op=ALU.mult),
                             [po, rl_], [ocT])
                for j, t in enumerate(tl):
                    for hf in range(2):
                        for e8 in range(8):
                            T.op("pe", lambda e, hf=hf, e8=e8, j=j: e.matmul(ps_mx[hf].ap[:, :], lhsT=ocT.ap[:, e8, j * 128:(j + 1) * 128],
                                                                          rhs=Wco.ap[:, e8, hf * 512:(hf + 1) * 512], start=(e8 == 0), stop=(e8 == 7)),
                                 [ocT, Wco], [ps_mx[hf]], inc=(e8 == 7))
                    h2_ = h2t.next()
                    for hf in range(2):
                        T.op("dve", lambda e, hf=hf, j=j: e.tensor_tensor(out=h2_.ap[:, hf * 512:(hf + 1) * 512], in0=hb.ap[:, j, hf * 512:(hf + 1) * 512],
                                                                       in1=ps_mx[hf].ap[:, :], op=ALU.add), [hb, ps_mx[hf]], [h2_])
                    T.dma("pool", h2_scr[t], h2_.ap[:, :], [h2_], [h2r[t]])

        fblocks = [[0]] + [list(range(i, min(i + 2, NQ))) for i in range(1, NQ, 2)]
        with _Scope(T):
            gff = T.sb("gff", [128, 8], F32)
            load_vec(gff, norm_ffn_g)
            nt = NormT(T, ident, neghalf, gff, nptr=1)
            Wup = T.sb("Wup", [128, 8, 2 * D_FF], BF16)
            Wdn = T.sb("Wdn", [128, NFC, D], BF16)
            load_wb(T, Wup, 8, wbf["up"], wres["up"], 2 * D_FF)
            load_wb(T, Wdn, NFC, wbf["dn"], wres["dn"], D)
            ffw = T.sb("ffw", [128, 3, 2 * NFC], F32)
            for k in range(3):
                T.dma("sp", ffw.ap[:, k, :], ffn_dw_w[k].rearrange("(c p) -> p c", p=128), [], [ffw], slow=True)
            gfin = T.sb("gfin", [128, D], F32)
            T.dma("sp", gfin.ap[:, :], norm_final_g.partition_broadcast(128), [], [gfin])
            stash = T.sb("stash", [128, NFC, 2, 2], F32)
            T.op("pool", lambda e: e.memset(stash.ap[:, :, :, :], 0.0), [], [stash])
            U = Rot([T.sb("U%d" % i, [128, 2, 258], F32) for i in range(3)])
            ca = Rot([T.sb("ca%d" % i, [128, 256], F32) for i in range(2)])
            cb_ = Rot([T.sb("cb%d" % i, [128, 256], F32) for i in range(2)])
            sa = Rot([T.sb("sa%d" % i, [128, 256], F32) for i in range(2)])
            pt1 = Rot([T.sb("pt1_%d" % i, [128, 256], F32) for i in range(2)])
            pt0 = Rot([T.sb("pt0_%d" % i, [128, 256], F32) for i in range(2)])
            zT = Rot([T.sb("zT%d" % i, [128, 256], BF16) for i in range(3)])
            h2b = Rot([T.sb("h2b%d" % i, [128, 2, D], F32) for i in range(2)])
            h3nT = Rot([T.sb("h3nT%d" % i, [128, 8, 256], BF16) for i in range(2)])
            hfin = T.sb("hfin", [128, D], F32)
            ot = Rot([T.sb("ot%d" % i, [128, D], F32) for i in range(1)])
            fs = Rot([T.sb("fs%d" % i, [128, 4], F32) for i in range(2)])
            ps_up = Rot([T.ps("ps_up%d" % i, [128, 2, 256], F32) for i in range(3)])
            ps_acc = [T.ps("ps_acc%d" % i, [128, 512], F32) for i in range(4)]
            def prep_block(tl_):
                hb_ = h2b.next()
                hn_ = h3nT.next()
                pend = []
                for j, t in enumerate(tl_):
                    T.dma("sp", hb_.ap[:, j, :], h2_scr[t], [h2r[t]], [hb_])
                    pend.append(nt.stage_a(hb_.ap[:, j, :], hb_))
                    if j >= 1:
                        nt.stage_b(hb_.ap[:, j - 1, :], hb_, pend[j - 1], hn_, (j - 1) * 128)
                nt.stage_b(hb_.ap[:, len(tl_) - 1, :], hb_, pend[-1], hn_, (len(tl_) - 1) * 128)
                return hb_, hn_

            prepped = prep_block(fblocks[0])
            for bi, tl in enumerate(fblocks):
                N = len(tl) * 128
                halo = (bi == 0)
                hb, hn = prepped

                def up(fc):
                    pu = ps_up.next()
                    for ab in range(2):
                        col = ab * D_FF + fc * 128
                        for kc in range(8):
                            T.op("pe", lambda e, ab=ab, col=col, kc=kc: e.matmul(pu.ap[:, ab, 0:N], lhsT=Wup.ap[:, kc, col:col + 128], rhs=hn.ap[:, kc, 0:N],
                                                                              start=(kc == 0), stop=(kc == 7)), [Wup, hn], [pu], inc=(kc == 7))
                    return pu

                def mid_a(fc, pu):
                    u_ = U.next()
                    T.op("act", lambda e: e.copy(out=u_.ap[:, :, 2:2 + N], in_=pu.ap[:, :, 0:N]), [pu], [u_])
                    T.op("act", lambda e: e.copy(out=u_.ap[:, :, 0:2], in_=stash.ap[:, fc, :, :]), [stash], [u_])
                    T.op("act", lambda e: e.copy(out=stash.ap[:, fc, :, :], in_=u_.ap[:, :, N:N + 2]), [u_], [stash])
                    return u_

                def mid_b(fc, u_):
                    if halo:
                        return None
                    a_ = ca.next()
                    b_ = cb_.next()
                    dst = (a_, b_)
                    ptmp = {1: pt1.next(), 0: pt0.next()}
                    for k in (2, 1, 0):
                        for ab in range(2):
                            wc = ab * NFC + fc
                            d_ = dst[ab]
                            if ab == 0:
                                if k == 2:
                                    T.op("dve", lambda e, wc=wc, d_=d_: e.tensor_scalar(out=d_.ap[:, 0:N], in0=u_.ap[:, 0, 2:2 + N], scalar1=ffw.ap[:, 2, wc:wc + 1],
                                                                                     scalar2=None, op0=ALU.mult), [u_, ffw], [d_])
                                else:
                                    T.op("dve", lambda e, wc=wc, k=k, d_=d_: e.scalar_tensor_tensor(out=d_.ap[:, 0:N], in0=u_.ap[:, 0, k:k + N],
                                                                                                 scalar=ffw.ap[:, k, wc:wc + 1], in1=d_.ap[:, 0:N],
                                                                                                 op0=ALU.mult, op1=ALU.add), [u_, ffw, d_], [d_])
                            else:
                                tgt = d_ if k == 2 else ptmp[k]
                                T.op("pool", lambda e, wc=wc, k=k, tgt=tgt: e.tensor_scalar(out=tgt.ap[:, 0:N], in0=u_.ap[:, 1, k:k + N], scalar1=ffw.ap[:, k, wc:wc + 1],
                                                                                         scalar2=0.0, op0=ALU.mult, op1=ALU.add), [u_, ffw], [tgt])
                    T.op("pool", lambda e: e.tensor_tensor(out=b_.ap[:, 0:N], in0=b_.ap[:, 0:N], in1=ptmp[1].ap[:, 0:N], op=ALU.add), [b_, ptmp[1]], [b_])
                    T.op("dve", lambda e: e.tensor_tensor(out=b_.ap[:, 0:N], in0=b_.ap[:, 0:N], in1=ptmp[0].ap[:, 0:N], op=ALU.add), [b_, ptmp[0]], [b_])
                    s_ = sa.next()
                    z_ = zT.next()
                    T.op("act", lambda e: e.activation(out=s_.ap[:, 0:N], in_=a_.ap[:, 0:N], func=AF.Silu), [a_], [s_])
                    T.op("dve", lambda e: e.tensor_tensor(out=z_.ap[:, 0:N], in0=s_.ap[:, 0:N], in1=b_.ap[:, 0:N], op=ALU.mult), [s_, b_], [z_])
                    return z_

                def down(fc, z_):
                    for j in range(len(tl)):
                        for hf in range(2):
                            lastmm = (j == len(tl) - 1 and hf == 1)
                            T.op("pe", lambda e, j=j, hf=hf: e.matmul(ps_acc[j * 2 + hf].ap[:, :], lhsT=z_.ap[:, j * 128:(j + 1) * 128],
                                                                   rhs=Wdn.ap[:, fc, hf * 512:(hf + 1) * 512], start=(fc == 0), stop=(fc == NFC - 1)),
                                 [z_, Wdn], [ps_acc[j * 2 + hf]], inc=(lastmm or fc == NFC - 1))

                ups = [up(0), up(1)]
                unext = mid_a(0, ups.pop(0))
                zprev = None
                for fc in range(NFC):
                    if fc + 2 < NFC:
                        ups.append(up(fc + 2))
                    ucur = unext
                    if fc + 1 < NFC:
                        unext = mid_a(fc + 1, ups.pop(0))
                    z_ = mid_b(fc, ucur)
                    if zprev is not None:
                        down(fc - 1, zprev)
                    zprev = z_
                    if fc == NFC // 2 and bi + 1 < len(fblocks):
                        prepped = prep_block(fblocks[bi + 1])
                if zprev is not None:
                    down(NFC - 1, zprev)
                if halo:
                    T.op("pool", lambda e: e.tensor_scalar(out=stash.ap[:, :, :, :], in0=stash.ap[:, :, :, :], scalar1=hvalid.ap[:, 0:1], scalar2=0.0,
                                                           op0=ALU.mult, op1=ALU.add), [stash, hvalid], [stash])
                    continue
                for j, t in enumerate(tl):
                    for hf in range(2):
                        T.op("dve", lambda e, j=j, hf=hf: e.tensor_tensor(out=hfin.ap[:, hf * 512:(hf + 1) * 512], in0=hb.ap[:, j, hf * 512:(hf + 1) * 512],
                                                                       in1=ps_acc[j * 2 + hf].ap[:, :], op=ALU.add), [hb, ps_acc[j * 2 + hf]], [hfin])
                    f_ = fs.next()
                    T.op("act", lambda e: e.activation(out=nt.junk.ap[:, :], in_=hfin.ap[:, :], func=AF.Square, accum_out=f_.ap[:, 0:1]), [hfin], [f_])
                    T.op("dve", lambda e: e.tensor_scalar(out=f_.ap[:, 1:2], in0=f_.ap[:, 0:1], scalar1=1.0 / D, scalar2=EPS, op0=ALU.mult, op1=ALU.add),
                         [f_], [f_])
                    T.op("pool", lambda e: e.tensor_tensor(out=f_.ap[:, 2:3], in0=f_.ap[:, 1:2], in1=neghalf.ap[:, 0:1], op=ALU.pow), [f_, neghalf], [f_])
                    o_ = ot.next()
                    T.op("dve", lambda e: e.scalar_tensor_tensor(out=o_.ap[:, :], in0=hfin.ap[:, :], scalar=f_.ap[:, 2:3], in1=gfin.ap[:, :],
                                                                 op0=ALU.mult, op1=ALU.mult), [hfin, f_, gfin], [o_])
                    T.dma("pool", out_d[t - 1], o_.ap[:, :], [o_], [])
        T.finish()
    return nc, T


def _host_prep(inputs, S):
    NT = S // 128
    OWN = NT // 4
    x = np.asarray(inputs["x"], dtype=np.float32)
    mem = np.asarray(inputs["mem"], dtype=np.float32)
    pos = np.asarray(inputs["positions"], dtype=np.int32)
    B = x.shape[0]
    ident = np.eye(128, dtype=np.float32)
    tri = np.triu(np.ones((128, 128), dtype=np.float32))
    wnames = ["norm_mix_g", "lam_q1", "lam_k1", "lam_q2", "lam_k2", "subln_g", "cv_dw_w", "cv_dw_b", "cv_ln_g", "cv_ln_b",
              "w_out", "norm_cross_g", "norm_mem_g", "w_cq", "w_ckv", "w_co", "norm_ffn_g", "w_up", "ffn_dw_w", "w_down"]
    common = {n: np.ascontiguousarray(np.asarray(inputs[n], dtype=np.float32)[0]) for n in wnames}
    common["norm_final_g"] = np.ascontiguousarray(np.asarray(inputs["norm_final_g"], dtype=np.float32))
    common["ident"] = ident
    common["tri"] = tri
    w_in = np.asarray(inputs["w_in"], dtype=np.float32)[0]
    common["w_u"] = np.ascontiguousarray(w_in[:, 1536:2560])
    in_maps = []
    for b in range(B):
        xt = np.ascontiguousarray(x[b].reshape(NT, 128, D))
        pk = np.ascontiguousarray(pos[b].reshape(NT, 128).T)
        memb = np.ascontiguousarray(mem[b])
        for r in range(4):
            m = dict(common)
            m["x_b"] = xt
            m["pos_b"] = pk
            m["w_qkv"] = np.ascontiguousarray(np.concatenate(
                [w_in[:, r * 128:(r + 1) * 128], w_in[:, 512 + r * 128:512 + (r + 1) * 128], w_in[:, 1024 + r * 128:1024 + (r + 1) * 128]], axis=1))
            xo = np.zeros((OWN + 1, 128, D), dtype=np.float32)
            xo[1:] = xt[r * OWN:(r + 1) * OWN]
            if r > 0:
                xo[0] = xt[r * OWN - 1]
            m["x_own"] = xo
            sl = np.zeros((128, 4), dtype=np.float32)
            sl[:, r] = 1.0
            m["sel"] = sl
            m["hvalid"] = np.full((128, 1), 1.0 if r > 0 else 0.0, dtype=np.float32)
            m["mem"] = memb
            in_maps.append(m)
    return in_maps


_CACHE = {}


def kernel(**inputs):
    x = np.asarray(inputs["x"])
    B, S, _ = x.shape
    NT = S // 128
    OWN = NT // 4
    in_maps = _host_prep(inputs, S)
    if S not in _CACHE:
        _CACHE[S] = build(S)[0]
    nc = _CACHE[S]
    res = run_bass_kernel_spmd(nc, in_maps, core_ids=list(range(len(in_maps))))
    out = np.zeros((B, S, D), dtype=np.float32)
    i = 0
    for b in range(B):
        for c in range(4):
            out[b, c * OWN * 128:(c + 1) * OWN * 128, :] = np.asarray(res.results[i]["out"]).reshape(OWN * 128, D)
            i += 1
    return out
```

```python
import numpy as np
import ml_dtypes
from contextlib import ExitStack
import concourse.bass as bass
import concourse.mybir as mybir
from concourse.bass_utils import run_bass_kernel_spmd

F32 = mybir.dt.float32
BF16 = mybir.dt.bfloat16
I32 = mybir.dt.int32
ALU = mybir.AluOpType
AF = mybir.ActivationFunctionType

NDS = 24


class Res:
    __slots__ = ("name", "ap", "w", "r")

    def __init__(self, name, ap=None):
        self.name = name
        self.ap = ap
        self.w = {}
        self.r = {}


class Trk:
    def __init__(self, nc, es):
        self.nc = nc
        self.es = es
        self.eng = {"pe": nc.tensor, "act": nc.scalar, "dve": nc.vector, "pool": nc.gpsimd, "sp": nc.sync}
        self.cnt = {k: 0 for k in self.eng}
        self.sem = {k: es.enter_context(nc.semaphore("s_" + k)) for k in self.eng if k != "sp"}
        self.waited = {k: {} for k in self.eng}
        self.dsem = [es.enter_context(nc.semaphore("d%d" % i)) for i in range(NDS)]
        self.dcnt = [0] * NDS
        self.rr = 0
        self.nres = 0
        self._cst = {}
        self.n_ins = 0
        self.log = {k: [] for k in self.eng}

    def sb(self, name, shape, dt):
        self.nres += 1
        t = self.es.enter_context(self.nc.sbuf_tensor("sb%d_%s" % (self.nres, name), list(shape), dt))
        return Res(name, t)

    def ps(self, name, shape, dt):
        self.nres += 1
        t = self.es.enter_context(self.nc.psum_tensor("ps%d_%s" % (self.nres, name), list(shape), dt))
        return Res(name, t)

    def cst(self, val):
        key = float(val)
        if key not in self._cst:
            r = self.sb("cst%d" % len(self._cst), [128, 1], F32)
            self.op("pool", lambda e: e.memset(r.ap[:, :], key), [], [r])
            self._cst[key] = r
        r = self._cst[key]
        return r

    def _sync(self, eng, reads, writes):
        deps = {}

        def add(tag):
            k, sem, val = tag
            if k not in deps or deps[k][1] < val:
                deps[k] = (sem, val)

        for r in reads:
            for t in r.w.values():
                add(t)
        import os
        strict = bool(os.environ.get("KDBG_WAW"))
        for w in writes:
            for t in w.w.values():
                if t[0] != eng or strict:
                    add(t)
            for t in w.r.values():
                if t[0] != eng or strict:
                    add(t)
        e = self.eng[eng]
        for k, (sem, val) in deps.items():
            if k == eng and eng == "pe":
                continue
            if self.waited[eng].get(k, 0) >= val:
                continue
            e.wait_ge(sem, val)
            self.log[eng].append(("wait", id(sem), val, k))
            self.waited[eng][k] = val
            self.n_ins += 1

    def _mark(self, tag, reads, writes):
        k = tag[0]
        for r in reads:
            if k not in r.r or r.r[k][2] < tag[2]:
                r.r[k] = tag
        for w in writes:
            if k not in w.w or w.w[k][2] < tag[2]:
                w.w[k] = tag
            w.r = {}

    def op(self, eng, fn, reads=(), writes=(), inc=True):
        reads = [x for x in reads if x is not None]
        writes = [x for x in writes if x is not None]
        self._sync(eng, reads, writes)
        ins = fn(self.eng[eng])
        self.n_ins += 1
        if inc:
            self.cnt[eng] += 1
            ins.then_inc(self.sem[eng], 1)
            self.log[eng].append(("inc", id(self.sem[eng]), 1, eng))
            val = self.cnt[eng]
        else:
            val = self.cnt[eng] + 1
        self._mark((eng, self.sem[eng], val), reads, writes)
        return ins

    def dma(self, queue, out_ap, in_ap, reads=(), writes=(), slow=False):
        reads = [x for x in reads if x is not None]
        writes = [x for x in writes if x is not None]
        self._sync(queue, reads, writes)
        i = self.rr
        self.rr = (i + 1) % NDS
        self.dcnt[i] += 16
        self.eng[queue].dma_start(out=out_ap, in_=in_ap, allow_slow_non_contiguous=slow).then_inc(self.dsem[i], 16)
        self.log[queue].append(("inc", id(self.dsem[i]), 16, ("dma", i)))
        self.n_ins += 1
        self._mark((("dma", i), self.dsem[i], self.dcnt[i]), reads, writes)

    def finish(self):
        e = self.eng["sp"]
        for i in range(NDS):
            if self.dcnt[i]:
                e.wait_ge(self.dsem[i], self.dcnt[i])
        for k in self.sem:
            if self.cnt[k]:
                e.wait_ge(self.sem[k], self.cnt[k])

    def barrier(self):
        for k, e in self.eng.items():
            for i in range(NDS):
                if self.dcnt[i] and self.waited[k].get(("dma", i), 0) < self.dcnt[i]:
                    e.wait_ge(self.dsem[i], self.dcnt[i])
                    self.log[k].append(("wait", id(self.dsem[i]), self.dcnt[i], ("dma", i)))
                    self.waited[k][("dma", i)] = self.dcnt[i]
            for k2 in self.sem:
                if k2 != k and self.cnt[k2] and self.waited[k].get(k2, 0) < self.cnt[k2]:
                    e.wait_ge(self.sem[k2], self.cnt[k2])
                    self.log[k].append(("wait", id(self.sem[k2]), self.cnt[k2], k2))
                    self.waited[k][k2] = self.cnt[k2]


class _Scope:
    def __init__(self, T):
        self.T = T

    def __enter__(self):
        self.old = self.T.es
        self.st = ExitStack()
        self.st.__enter__()
        self.T.es = self.st
        return self

    def __exit__(self, *a):
        self.T.barrier()
        self.T.es = self.old
        return self.st.__exit__(*a)


D = 1024
EPS = 1e-6
BIG = -30000.0
ROPE_THETA = 500000.0
D_FF = 2816
NFC = D_FF // 128
CVK = 31


class Rot:
    def __init__(self, items):
        self.items = items
        self.i = 0

    def next(self):
        r = self.items[self.i % len(self.items)]
        self.i += 1
        return r


def bcast_mid(ap2d, n):
    l = [list(x) for x in ap2d.ap]
    return bass.AP(ap2d.tensor, ap2d.offset, [l[0], [0, n], l[-1]])


def load_w(T, dst, kcn, src2d, col0, ncols, gain, stg, eng="pool"):
    CH = stg.items[0].ap.shape[1]
    for kc in range(kcn):
        for c0 in range(0, ncols, CH):
            cw = min(CH, ncols - c0)
            st = stg.next()
            T.dma("sp", st.ap[:, 0:cw], src2d[kc * 128:(kc + 1) * 128, col0 + c0:col0 + c0 + cw], [], [st])
            if gain is not None:
                T.op(eng, lambda e, st=st, kc=kc, c0=c0, cw=cw: e.tensor_scalar(
                    out=dst.ap[:, kc, c0:c0 + cw], in0=st.ap[:, 0:cw], scalar1=gain.ap[:, kc:kc + 1], scalar2=0.0,
                    op0=ALU.mult, op1=ALU.add), [st, gain], [dst])
            else:
                T.op(eng, lambda e, st=st, kc=kc, c0=c0, cw=cw: e.tensor_copy(
                    out=dst.ap[:, kc, c0:c0 + cw], in_=st.ap[:, 0:cw]), [st], [dst])


def load_wd(T, dst, kcn, src2d, col0, ncols):
    import os
    if os.environ.get("KDBG_POOLW"):
        if not hasattr(T, "_stg"):
            T._stg = None
        stg = Rot([T.sb("wstg%d" % i, [128, 1024], F32) for i in range(2)])
        for kc in range(kcn):
            for c0 in range(0, ncols, 1024):
                cw = min(1024, ncols - c0)
                st = stg.next()
                T.dma("sp", st.ap[:, 0:cw], src2d[kc * 128:(kc + 1) * 128, col0 + c0:col0 + c0 + cw], [], [st])
                T.op("pool", lambda e, st=st, kc=kc, c0=c0, cw=cw: e.tensor_copy(out=dst.ap[:, kc, c0:c0 + cw], in_=st.ap[:, 0:cw]), [st], [dst])
        return
    CH = 512
    for kc in range(kcn):
        for c0 in range(0, ncols, CH):
            cw = min(CH, ncols - c0)
            T.dma("pool", dst.ap[:, kc, c0:c0 + cw], src2d[kc * 128:(kc + 1) * 128, col0 + c0:col0 + c0 + cw], [], [dst])


def load_wb(T, dst, kcn, src_bf, res, ncols, col0=0):
    for kc in range(kcn):
        T.dma("sp", dst.ap[:, kc, 0:ncols], src_bf[kc * 128:(kc + 1) * 128, col0:col0 + ncols], [res], [dst])


class NormT:
    def __init__(self, T, ident, neghalf, g8, nbuf=3, nptr=2):
        self.T = T
        self.ident = ident
        self.neghalf = neghalf
        self.g8 = g8
        self.junk = T.sb("nt_junk", [128, D], BF16)
        self.ss = Rot([T.sb("nt_ss%d" % i, [128, 4], F32) for i in range(nbuf)])
        self.xs = Rot([T.sb("nt_xs%d" % i, [128, D], BF16) for i in range(2)])
        self.ptr = Rot([T.ps("nt_ptr%d" % i, [128, D], BF16) for i in range(nptr)])

    def stage_a(self, src_ap, src_res):
        T = self.T
        ss = self.ss.next()
        T.op("act", lambda e: e.activation(out=self.junk.ap[:, :], in_=src_ap, func=AF.Square, accum_out=ss.ap[:, 0:1]),
             [src_res], [ss])
        T.op("dve", lambda e: e.tensor_scalar(out=ss.ap[:, 1:2], in0=ss.ap[:, 0:1], scalar1=1.0 / D, scalar2=EPS,
                                              op0=ALU.mult, op1=ALU.add), [ss], [ss])
        T.op("pool", lambda e: e.tensor_tensor(out=ss.ap[:, 2:3], in0=ss.ap[:, 1:2], in1=self.neghalf.ap[:, 0:1], op=ALU.pow),
             [ss, self.neghalf], [ss])
        return ss

    def stage_b(self, src_ap, src_res, ss, dstT, col0, g8=None):
        T = self.T
        xs = self.xs.next()
        ptr = self.ptr.next()
        T.op("act", lambda e: e.activation(out=xs.ap[:, :], in_=src_ap, func=AF.Copy, scale=ss.ap[:, 2:3]),
             [src_res, ss], [xs])
        for kc in range(8):
            T.op("pe", lambda e, kc=kc: e.transpose(out=ptr.ap[:, kc * 128:(kc + 1) * 128],
                                                    in_=xs.ap[:, kc * 128:(kc + 1) * 128], identity=self.ident.ap[:, :]),
                 [xs, self.ident], [ptr], inc=(kc == 7))
        g8 = self.g8 if g8 is None else g8
        l = [list(x) for x in g8.ap[:, :].ap]
        gb = bass.AP(g8.ap[:, :].tensor, g8.ap[:, :].offset, [l[0], l[1], [0, 128]])
        T.op("dve", lambda e: e.tensor_tensor(out=dstT.ap[:, :, col0:col0 + 128], in0=ptr.ap[:, :].rearrange("p (c t) -> p c t", c=8),
                                              in1=gb, op=ALU.mult), [ptr, g8], [dstT])


def build(S):
    NT = S // 128
    OWN = NT // 4
    NQ = OWN + 1
    NQT = NQ * 128
    NMEM = 256
    nc = bass.Bass("TRN2", target_bir_lowering=False)

    def din(name, shape, dt=F32):
        return nc.dram_tensor(name, list(shape), dt, kind="ExternalInput").ap()

    x_b = din("x_b", [NT, 128, D])
    x_kv = din("x_own", [NQ, 128, D])
    pos_kv = din("pos_b", [128, NT], I32)
    sel_d = din("sel", [128, 4])
    hvalid_d = din("hvalid", [128, 1])
    ident_d = din("ident", [128, 128])
    tri_d = din("tri", [128, 128])
    mem_d = din("mem", [NMEM, D])
    norm_mix_g = din("norm_mix_g", [D])
    w_qkv = din("w_qkv", [D, 384])
    w_in = din("w_u", [D, 1024])
    lam_d = [din(n, [64]) for n in ("lam_q1", "lam_k1", "lam_q2", "lam_k2")]
    subln_g = din("subln_g", [128])
    cv_dw_w = din("cv_dw_w", [CVK, 512])
    cv_dw_b = din("cv_dw_b", [512])
    cv_ln_g = din("cv_ln_g", [512])
    cv_ln_b = din("cv_ln_b", [512])
    w_out = din("w_out", [D, D])
    norm_cross_g = din("norm_cross_g", [D])
    norm_mem_g = din("norm_mem_g", [D])
    w_cq = din("w_cq", [D, D])
    w_ckv = din("w_ckv", [D, 2 * D])
    w_co = din("w_co", [D, D])
    norm_ffn_g = din("norm_ffn_g", [D])
    w_up = din("w_up", [D, 2 * D_FF])
    ffn_dw_w = din("ffn_dw_w", [3, 2 * D_FF])
    w_down = din("w_down", [D_FF, D])
    norm_final_g = din("norm_final_g", [D])
    out_d = nc.dram_tensor("out", [OWN, 128, D], F32, kind="ExternalOutput").ap()

    pieces = []
    _p = 0
    while _p < NQ:
        _w = min(7, NQ - _p)
        pieces.append((_p, _w))
        _p += _w

    def piece_of(pos):
        for i_, (s_, w_) in enumerate(pieces):
            if s_ <= pos < s_ + w_:
                return i_, pos - s_

    sendp = [nc.dram_tensor("sendb%d" % i, [4 * 128, w * 128], BF16, kind="Internal").ap() for i, (s_, w) in enumerate(pieces)]
    recvp = [nc.dram_tensor("recvb%d" % i, [16 * 128, w * 128], BF16, kind="Internal").ap() for i, (s_, w) in enumerate(pieces)]
    wsrc = {"u": (w_in, D, 1024), "out": (w_out, D, D), "cq": (w_cq, D, D), "co": (w_co, D, D), "ckv": (w_ckv, D, 2 * D),
            "up": (w_up, D, 2 * D_FF), "dn": (w_down, D_FF, D)}
    wbf = {k: nc.dram_tensor("wbf_" + k, [r_, c_], BF16, kind="Internal").ap() for k, (a_, r_, c_) in wsrc.items()}
    wres = {k: Res("wbf_" + k) for k in wsrc}
    h1_scr = nc.dram_tensor("h1_scr", [NQ, 128, D], F32, kind="Internal").ap()
    h2_scr = nc.dram_tensor("h2_scr", [NQ, 128, D], F32, kind="Internal").ap()
    sendr = [Res("sendr%d" % i) for i in range(len(pieces))]
    recvr = [Res("recvr%d" % i) for i in range(len(pieces))]
    h1r = [Res("h1r%d" % s) for s in range(NQ)]
    h2r = [Res("h2r%d" % s) for s in range(NQ)]

    inv_freq = (np.float32(ROPE_THETA) ** (-(np.arange(0, 16, 2, dtype=np.float32)) / np.float32(16))).astype(np.float32)
    TWO_PI = float(2 * np.pi)

    with ExitStack() as es:
        T = Trk(nc, es)
        ident = T.sb("ident", [128, 128], BF16)
        tri = T.sb("tri", [128, 128], BF16)
        neghalf = T.sb("neghalf", [128, 1], F32)
        neg_lam = T.sb("neg_lam", [128, 1], F32)
        gsub = T.sb("gsub", [128, 128], F32)
        sel = T.sb("sel", [128, 4], F32)
        hvalid = T.sb("hvalid", [128, 1], F32)
        T.op("pool", lambda e: e.memset(neghalf.ap[:, :], -0.5), [], [neghalf])
        T.dma("sp", sel.ap[:, :], sel_d[:, :], [], [sel])
        T.dma("sp", hvalid.ap[:, :], hvalid_d[:, :], [], [hvalid])

        def load_vec(dst, src1d):
            T.dma("sp", dst.ap[:, :], src1d.rearrange("(c p) -> p c", p=128), [], [dst], slow=True)

        with _Scope(T):
            tmpf = T.sb("tmpf", [128, 128], F32)
            tmpf2 = T.sb("tmpf2", [128, 128], F32)
            T.dma("sp", tmpf.ap[:, :], ident_d[:, :], [], [tmpf])
            T.dma("sp", tmpf2.ap[:, :], tri_d[:, :], [], [tmpf2])
            T.op("dve", lambda e: e.tensor_copy(out=ident.ap[:, :], in_=tmpf.ap[:, :]), [tmpf], [ident])
            T.op("dve", lambda e: e.tensor_copy(out=tri.ap[:, :], in_=tmpf2.ap[:, :]), [tmpf2], [tri])
            lv = [T.sb("lv%d" % i, [128, 64], F32) for i in range(4)]
            for i in range(4):
                T.dma("sp", lv[i].ap[:, :], lam_d[i].partition_broadcast(128), [], [lv[i]])
            lsum = T.sb("lsum", [128, 4], F32)
            ljunk = T.sb("ljunk", [128, 64], F32)
            for i in range(2):
                T.op("dve", lambda e, i=i: e.tensor_tensor(out=ljunk.ap[:, :], in0=lv[2 * i].ap[:, :], in1=lv[2 * i + 1].ap[:, :],
                                                        op=ALU.mult), [lv[2 * i], lv[2 * i + 1]], [ljunk])
                T.op("dve", lambda e, i=i: e.tensor_reduce(out=lsum.ap[:, i:i + 1], in_=ljunk.ap[:, :],
                                                        axis=mybir.AxisListType.X, op=ALU.add), [ljunk], [lsum])
            T.op("act", lambda e: e.activation(out=lsum.ap[:, 2:4], in_=lsum.ap[:, 0:2], func=AF.Exp), [lsum], [lsum])
            T.op("dve", lambda e: e.scalar_tensor_tensor(out=neg_lam.ap[:, :], in0=lsum.ap[:, 3:4], scalar=-0.2, in1=lsum.ap[:, 2:3],
                                                         op0=ALU.add, op1=ALU.subtract), [lsum], [neg_lam])
            T.dma("sp", tmpf.ap[:, :], subln_g.partition_broadcast(128), [ident], [tmpf])
            T.op("dve", lambda e: e.tensor_scalar(out=gsub.ap[:, :], in0=tmpf.ap[:, :], scalar1=0.8, scalar2=None, op0=ALU.mult),
                 [tmpf], [gsub])

        scopeA = _Scope(T)
        scopeA.__enter__()
        QTz = T.sb("QTz", [128, 2, S], BF16)
        KT = T.sb("KT", [128, S], BF16)
        VS = T.sb("VS", [128, NT, 129], BF16)
        T.op("pool", lambda e: e.memset(QTz.ap[:, :, :], 0.0), [], [QTz])
        T.op("pool", lambda e: e.memset(VS.ap[:, :, 128:129], 1.0), [], [VS])
        with _Scope(T):
            gmix = T.sb("gmix", [128, 8], F32)
            load_vec(gmix, norm_mix_g)
            nt = NormT(T, ident, neghalf, gmix, nbuf=4)
            Wq = T.sb("Wq", [128, 8, 384], BF16)
            for kc in range(8):
                T.dma("pool", Wq.ap[:, kc, :], w_qkv[kc * 128:(kc + 1) * 128, :], [], [Wq])
            posi = T.sb("posi", [128, NT], I32)
            posf = T.sb("posf", [128, NT], F32)
            ang = T.sb("ang", [128, NT, 8], F32)
            kfi = T.sb("kfi", [128, NT, 8], I32)
            kf = T.sb("kf", [128, NT, 8], F32)
            mk = T.sb("mk", [128, NT, 8], F32)
            cosT = T.sb("cosT", [128, NT, 8], F32)
            sinT = T.sb("sinT", [128, NT, 8], F32)
            T.dma("sp", posi.ap[:, :], pos_kv[:, :], [], [posi])
            T.op("dve", lambda e: e.tensor_copy(out=posf.ap[:, :], in_=posi.ap[:, :]), [posi], [posf])
            for j in range(8):
                T.op("dve", lambda e, j=j: e.tensor_scalar(out=ang.ap[:, :, j], in0=posf.ap[:, :], scalar1=float(inv_freq[j]),
                                                        scalar2=None, op0=ALU.mult), [posf], [ang])
            T.op("dve", lambda e: e.tensor_scalar(out=kfi.ap[:, :, :], in0=ang.ap[:, :, :], scalar1=1.0 / TWO_PI, scalar2=None,
                                                  op0=ALU.mult), [ang], [kfi])
            T.op("dve", lambda e: e.tensor_copy(out=kf.ap[:, :, :], in_=kfi.ap[:, :, :]), [kfi], [kf])
            T.op("dve", lambda e: e.scalar_tensor_tensor(out=ang.ap[:, :, :], in0=kf.ap[:, :, :], scalar=-TWO_PI, in1=ang.ap[:, :, :],
                                                         op0=ALU.mult, op1=ALU.add), [kf, ang], [ang])

            def wrap(dst, shift):
                T.op("dve", lambda e: e.tensor_scalar(out=dst.ap[:, :, :], in0=ang.ap[:, :, :], scalar1=float(shift), scalar2=None,
                                                      op0=ALU.add), [ang], [dst])
                T.op("dve", lambda e: e.tensor_scalar(out=mk.ap[:, :, :], in0=dst.ap[:, :, :], scalar1=float(np.pi), scalar2=-TWO_PI,
                                                      op0=ALU.is_gt, op1=ALU.mult), [dst], [mk])
                T.op("dve", lambda e: e.tensor_tensor(out=dst.ap[:, :, :], in0=dst.ap[:, :, :], in1=mk.ap[:, :, :], op=ALU.add),
                     [dst, mk], [dst])
                T.op("dve", lambda e: e.tensor_scalar(out=mk.ap[:, :, :], in0=dst.ap[:, :, :], scalar1=float(-np.pi), scalar2=TWO_PI,
                                                      op0=ALU.is_lt, op1=ALU.mult), [dst], [mk])
                T.op("dve", lambda e: e.tensor_tensor(out=dst.ap[:, :, :], in0=dst.ap[:, :, :], in1=mk.ap[:, :, :], op=ALU.add),
                     [dst, mk], [dst])
                T.op("act", lambda e: e.activation(out=dst.ap[:, :, :], in_=dst.ap[:, :, :], func=AF.Sin), [dst], [dst])

            wrap(sinT, 0.0)
            wrap(cosT, np.pi / 2)

            xt = Rot([T.sb("xt%d" % i, [128, D], F32) for i in range(4)])
            xnT = Rot([T.sb("xnT%d" % i, [128, 8, 128], BF16) for i in range(3)])
            ps_p = Rot([T.ps("ps_p%d" % i, [128, 512], F32) for i in range(2)])
            ps_t = Rot([T.ps("ps_t%d" % i, [128, 1024], BF16) for i in range(2)])
            qksb = Rot([T.sb("qksb%d" % i, [128, 256], BF16) for i in range(2)])
            tA = Rot([T.sb("tA%d" % i, [128, 4, 8], F32) for i in range(4)])
            tB = Rot([T.sb("tB%d" % i, [128, 4, 8], F32) for i in range(4)])

            def stage_a(sl):
                xt_ = xt.next()
                T.dma("sp", xt_.ap[:, :], x_b[sl], [], [xt_])
                return xt_, nt.stage_a(xt_.ap[:, :], xt_)

            def stage_b(sl, a):
                xt_, ss = a
                xn = xnT.next()
                nt.stage_b(xt_.ap[:, :], xt_, ss, xn, 0)
                ps = ps_p.next()
                for kc in range(8):
                    T.op("pe", lambda e, kc=kc: e.matmul(ps.ap[:, 0:384], lhsT=xn.ap[:, kc, :], rhs=Wq.ap[:, kc, :],
                                                         start=(kc == 0), stop=(kc == 7)), [xn, Wq], [ps], inc=(kc == 7))
                return ps

            def stage_c(sl, ps):
                T.op("dve", lambda e: e.tensor_copy(out=VS.ap[:, sl, 0:128], in_=ps.ap[:, 256:384]), [ps], [VS])
                sb_ = qksb.next()
                T.op("dve", lambda e: e.tensor_copy(out=sb_.ap[:, :], in_=ps.ap[:, 0:256]), [ps], [sb_])
                v3 = ps.ap[:, 0:256].rearrange("p (g d) -> p g d", g=4)
                o3 = sb_.ap[:, :].rearrange("p (g d) -> p g d", g=4)
                t1 = v3[:, :, 0:8]
                t2 = v3[:, :, 8:16]
                cb = bcast_mid(cosT.ap[:, sl, :], 4)
                sbb = bcast_mid(sinT.ap[:, sl, :], 4)
                a1 = tA.next()
                b1 = tB.next()
                a2 = tA.next()
                b2 = tB.next()
                TT = lambda o, i0, i1, op, rd, wr: T.op("dve", lambda e: e.tensor_tensor(out=o, in0=i0, in1=i1, op=op), rd, wr)
                TT(a1.ap[:, :, :], t1, cb, ALU.mult, [ps, cosT], [a1])
                TT(b1.ap[:, :, :], t2, sbb, ALU.mult, [ps, sinT], [b1])
                TT(a2.ap[:, :, :], t2, cb, ALU.mult, [ps, cosT], [a2])
                TT(b2.ap[:, :, :], t1, sbb, ALU.mult, [ps, sinT], [b2])
                TT(o3[:, :, 0:8], a1.ap[:, :, :], b1.ap[:, :, :], ALU.subtract, [a1, b1], [sb_])
                TT(o3[:, :, 8:16], a2.ap[:, :, :], b2.ap[:, :, :], ALU.add, [a2, b2], [sb_])
                pt = ps_t.next()
                for h in range(2):
                    T.op("pe", lambda e, h=h: e.transpose(out=pt.ap[:, h * 128:(h + 1) * 128], in_=sb_.ap[:, h * 128:(h + 1) * 128],
                                                          identity=ident.ap[:, :]), [sb_, ident], [pt], inc=(h == 1))
                for m in range(2):
                    T.op("dve", lambda e, m=m: e.tensor_copy(out=QTz.ap[64 * m:64 * m + 64, m, sl * 128:(sl + 1) * 128],
                                                          in_=pt.ap[64 * m:64 * m + 64, 0:128]), [pt], [QTz])
                T.op("dve", lambda e: e.tensor_copy(out=KT.ap[:, sl * 128:(sl + 1) * 128], in_=pt.ap[:, 128:256]), [pt], [KT])

            sa = {0: stage_a(0)}
            if NT > 1:
                sa[1] = stage_a(1)
            sbq = {0: stage_b(0, sa.pop(0))}
            for sl in range(NT):
                if sl + 2 < NT:
                    sa[sl + 2] = stage_a(sl + 2)
                if sl + 1 < NT:
                    sbq[sl + 1] = stage_b(sl + 1, sa.pop(sl + 1))
                stage_c(sl, sbq.pop(sl))

        import os
        KSTOP = int(os.environ.get("KDBG_STOP", "99"))
        if KSTOP == 1:
            T.finish()
            scopeA.__exit__(None, None, None)
            return nc, T
        ccs = es.enter_context(nc.semaphore("ccs"))
        cc_n = [0]

        def gather_piece(pi):
            T._sync("pool", [sendr[pi]], [recvr[pi]])
            nc.gpsimd.collective_compute("AllGather", ALU.bypass, replica_groups=[[0, 1, 2, 3], [4, 5, 6, 7]],
                                         ins=[sendp[pi].opt()], outs=[recvp[pi].opt()]).then_inc(ccs, 1)
            T.log["pool"].append(("inc", id(ccs), 1, ("cc", 0)))
            cc_n[0] += 1
            T._mark((("cc", 0), ccs, cc_n[0]), [sendr[pi]], [recvr[pi]])

        GQ = 2
        groups = [list(range(i, i + GQ)) for i in range(0, NT, GQ)]
        LOOK = 2
        with _Scope(T):
            ps_s = Rot([T.ps("ps_s%d" % i, [128, 2, 128 * GQ], F32) for i in range(LOOK + 1)])
            ps_o = [T.ps("ps_o%d" % i, [128, 512], F32) for i in range(GQ)]
            ps_at = Rot([T.ps("ps_at%d" % i, [128, 1024], BF16) for i in range(1)])
            PT = Rot([T.sb("PT%d" % i, [128, 2, 128 * GQ], BF16) for i in range(4)])
            osb = Rot([T.sb("osb%d" % i, [128, 258], F32) for i in range(2)])
            sm = Rot([T.sb("sm%d" % i, [128, 8], F32) for i in range(2)])
            af = Rot([T.sb("af%d" % i, [128, 128], F32) for i in range(2)])
            tf = Rot([T.sb("tf%d" % i, [128, 128], F32) for i in range(2)])
            anb = Rot([T.sb("anb%d" % i, [128, 128], BF16) for i in range(2)])
            ast = Rot([T.sb("ast%d" % i, [128, 128 * GQ], BF16) for i in range(3)])
            precast = []
            for k_ in ("u", "out", "cq", "co", "ckv", "up", "dn"):
                a_, r_, c_ = wsrc[k_]
                for r0 in range(0, r_, 128):
                    for c0 in range(0, c_, 512):
                        cw_ = min(512, c_ - c0)
                        precast.append((k_, r0, c0, cw_))
            n_pre_groups = max(1, len(groups) - min(8, len(groups) - 1))
            per_group = -(-len(precast) // n_pre_groups)
            zt = T.sb("zt", [128, 128], BF16)
            T.op("pool", lambda e: e.memset(zt.ap[:, :], 0.0), [], [zt])
            T.dma("pool", sendp[0][0:128, 0:128], zt.ap[:, :], [zt], [sendr[0]])
            for gi, grp in enumerate(groups):
                if gi >= len(groups) - n_pre_groups:
                    for _ in range(per_group):
                        if precast:
                            k_, r0, c0, cw_ = precast.pop(0)
                            T.dma("pool", wbf[k_][r0:r0 + 128, c0:c0 + cw_], wsrc[k_][0][r0:r0 + 128, c0:c0 + cw_], [], [wres[k_]])
                i0 = grp[0]
                G = len(grp)
                keys = []
                for k in range(0, i0 + G):
                    j0 = max(0, k - i0)
                    keys.append((k, j0, k >= i0))
                nk = len(keys)
                started = [False] * G
                a_st = ast.next()

                def qk(idx):
                    k, j0, dg = keys[idx]
                    pss = ps_s.next()
                    n0 = j0 * 128
                    n1 = G * 128
                    for m in range(2):
                        T.op("pe", lambda e, m=m: e.matmul(pss.ap[:, m, n0:n1], lhsT=KT.ap[:, k * 128:(k + 1) * 128],
                                                           rhs=QTz.ap[:, m, i0 * 128 + n0:i0 * 128 + n1], start=True, stop=True),
                             [KT, QTz], [pss], inc=(m == 1))
                    return pss

                pend = [qk(i) for i in range(min(LOOK, nk))]
                for idx in range(nk):
                    k, j0, dg = keys[idx]
                    pss = pend.pop(0)
                    if idx + LOOK < nk:
                        pend.append(qk(idx + LOOK))
                    n0 = j0 * 128
                    n1 = G * 128
                    pt_ = PT.next()
                    T.op("act", lambda e: e.activation(out=pt_.ap[:, :, n0:n1], in_=pss.ap[:, :, n0:n1], func=AF.Exp, scale=0.125),
                         [pss], [pt_])
                    if dg:
                        for m in range(2):
                            T.op("dve", lambda e, m=m: e.tensor_tensor(out=pt_.ap[:, m, n0:n0 + 128], in0=pt_.ap[:, m, n0:n0 + 128],
                                                                    in1=tri.ap[:, :], op=ALU.mult), [pt_, tri], [pt_])
                    for j in range(j0, G):
                        last_key = (k == i0 + j)
                        for m in range(2):
                            st = not started[j]
                            started[j] = True
                            T.op("pe", lambda e, j=j, m=m, st=st, last_key=last_key: e.matmul(
                                ps_o[j].ap[:, m * 129:(m + 1) * 129], lhsT=pt_.ap[:, m, j * 128:(j + 1) * 128], rhs=VS.ap[:, k, :],
                                start=st, stop=(last_key and m == 1), skip_group_check=True),
                                 [pt_, VS], [ps_o[j]], inc=(m == 1 and j == G - 1) or (last_key and m == 1))
                        if last_key:
                            o_ = osb.next()
                            s_ = sm.next()
                            a_ = af.next()
                            t_ = tf.next()
                            n_ = anb.next()
                            pa = ps_at.next()
                            T.op("dve", lambda e, j=j: e.tensor_copy(out=o_.ap[:, :], in_=ps_o[j].ap[:, 0:258]), [ps_o[j]], [o_])
                            T.op("dve", lambda e: e.reciprocal(out=s_.ap[:, 0:1], in_=o_.ap[:, 128:129]), [o_], [s_])
                            T.op("dve", lambda e: e.reciprocal(out=s_.ap[:, 1:2], in_=o_.ap[:, 257:258]), [o_], [s_])
                            T.op("dve", lambda e: e.tensor_tensor(out=s_.ap[:, 2:3], in0=s_.ap[:, 1:2], in1=neg_lam.ap[:, 0:1], op=ALU.mult),
                                 [s_, neg_lam], [s_])
                            T.op("dve", lambda e: e.tensor_scalar(out=t_.ap[:, :], in0=o_.ap[:, 129:257], scalar1=s_.ap[:, 2:3], scalar2=None,
                                                                  op0=ALU.mult), [o_, s_], [t_])
                            T.op("dve", lambda e: e.scalar_tensor_tensor(out=a_.ap[:, :], in0=o_.ap[:, 0:128], scalar=s_.ap[:, 0:1],
                                                                         in1=t_.ap[:, :], op0=ALU.mult, op1=ALU.add), [o_, s_, t_], [a_])
                            T.op("dve", lambda e: e.tensor_tensor(out=t_.ap[:, :], in0=a_.ap[:, :], in1=a_.ap[:, :], op=ALU.mult), [a_], [t_])
                            T.op("dve", lambda e: e.tensor_reduce(out=s_.ap[:, 3:4], in_=t_.ap[:, :], axis=mybir.AxisListType.X, op=ALU.add),
                                 [t_], [s_])
                            T.op("dve", lambda e: e.tensor_scalar(out=s_.ap[:, 4:5], in0=s_.ap[:, 3:4], scalar1=1.0 / 128, scalar2=EPS,
                                                                  op0=ALU.mult, op1=ALU.add), [s_], [s_])
                            T.op("pool", lambda e: e.tensor_tensor(out=s_.ap[:, 5:6], in0=s_.ap[:, 4:5], in1=neghalf.ap[:, 0:1], op=ALU.pow),
                                 [s_, neghalf], [s_])
                            T.op("dve", lambda e: e.scalar_tensor_tensor(out=n_.ap[:, :], in0=a_.ap[:, :], scalar=s_.ap[:, 5:6],
                                                                         in1=gsub.ap[:, :], op0=ALU.mult, op1=ALU.mult), [a_, s_, gsub], [n_])
                            T.op("pe", lambda e: e.transpose(out=pa.ap[:, 0:128], in_=n_.ap[:, :], identity=ident.ap[:, :]), [n_, ident], [pa])
                            T.op("dve", lambda e, j=j: e.tensor_copy(out=a_st.ap[:, j * 128:(j + 1) * 128], in_=pa.ap[:, 0:128]), [pa], [a_st])
                for j in range(G):
                    tq = i0 + j
                    c = tq // OWN
                    pi, off = piece_of(tq % OWN + 1)
                    T.dma("pool", sendp[pi][c * 128:(c + 1) * 128, off * 128:(off + 1) * 128], a_st.ap[:, j * 128:(j + 1) * 128], [a_st], [sendr[pi]])
                    if tq % OWN == OWN - 1 and c < 3:
                        T.dma("pool", sendp[0][(c + 1) * 128:(c + 2) * 128, 0:128], a_st.ap[:, j * 128:(j + 1) * 128], [a_st], [sendr[0]])
                    if c == 3:
                        ps0_, pw_ = pieces[pi]
                        if tq % OWN + 1 == ps0_ + pw_ - 1:
                            gather_piece(pi)
        scopeA.__exit__(None, None, None)

        if KSTOP == 2:
            T.finish()
            return nc, T
        if KSTOP == 3:
            T.finish()
            return nc, T
        scopeX = _Scope(T)
        scopeX.__enter__()
        aT = T.sb("aT", [128, 4, NQT], BF16)
        if KSTOP == 4:
            T.finish()
            scopeX.__exit__(None, None, None)
            return nc, T

        def bcast_free(ap1, n):
            l = [list(x) for x in ap1.ap]
            return bass.AP(ap1.tensor, ap1.offset, [l[0], [0, n]])

        def load_vec(dst, src1d):
            T.dma("sp", dst.ap[:, :], src1d.rearrange("(c p) -> p c", p=128), [], [dst], slow=True)

        blocks = [[0]] + [list(range(i, min(i + 4, NQ))) for i in range(1, NQ, 4)]
        with _Scope(T):
            gmix = T.sb("gmix", [128, 8], F32)
            load_vec(gmix, norm_mix_g)
            Wu = T.sb("Wu", [128, 8, 1024], BF16)
            Wout = T.sb("Wout", [128, 8, 1024], BF16)
            load_wb(T, Wu, 8, wbf["u"], wres["u"], 1024)
            load_wb(T, Wout, 8, wbf["out"], wres["out"], 1024)
            cw = T.sb("cw", [128, 4, CVK], F32)
            for cc in range(4):
                T.dma("sp", cw.ap[:, cc, :], cv_dw_w[:, cc * 128:(cc + 1) * 128].rearrange("k p -> p k"), [], [cw], slow=True)
            Dg = T.sb("Dg", [128, 4, CVK, 128], BF16)
            for cc in range(4):
                for k in range(CVK):
                    T.op("dve", lambda e, cc=cc, k=k: e.tensor_scalar(out=Dg.ap[:, cc, k, :], in0=ident.ap[:, :], scalar1=cw.ap[:, cc, k:k + 1],
                                                                   scalar2=None, op0=ALU.mult), [ident, cw], [Dg])
            coT_all = T.sb("coT_all", [128, 4, NQT], BF16)
            with _Scope(T):
                nt = NormT(T, ident, neghalf, gmix)
                cvb = T.sb("cvb", [128, 4], F32)
                lng = T.sb("lng", [128, 4], F32)
                lnb = T.sb("lnb", [128, 4], F32)
                load_vec(cvb, cv_dw_b)
                load_vec(lng, cv_ln_g)
                load_vec(lnb, cv_ln_b)
                epst = T.sb("epst", [128, 1], F32)
                T.op("pool", lambda e: e.memset(epst.ap[:, :], EPS), [], [epst])
                onesf = T.sb("onesf", [128, 128], F32)
                T.op("pool", lambda e: e.memset(onesf.ap[:, :], 1.0 / 512), [], [onesf])
                cTb = [T.sb("cTb%d" % i, [128, 4, 32 + 512], BF16) for i in range(2)]
                T.op("pool", lambda e: e.memset(cTb[0].ap[:, :, 0:32], 0.0), [], [cTb[0]])
                xnT = Rot([T.sb("xnT%d" % i, [128, 8, 512], BF16) for i in range(2)])
                xt = Rot([T.sb("xt%d" % i, [128, D], F32) for i in range(3)])
                sig = Rot([T.sb("sig%d" % i, [128, 512], F32) for i in range(2)])
                yT = T.sb("yT", [128, 4, 512], F32)
                ysq = T.sb("ysq", [128, 4, 512], F32)
                mean_sb = T.sb("mean_sb", [128, 512], F32)
                msq = T.sb("msq", [128, 512], F32)
                rstd = T.sb("rstd", [128, 512], F32)
                dtm = Rot([T.sb("dtm%d" % i, [128, 512], F32) for i in range(2)])
                ps_g = Rot([T.ps("ps_g%d" % i, [128, 512], F32) for i in range(2)])
                ps_m = T.ps("ps_m", [128, 512], F32)
                ps_q2 = T.ps("ps_q2", [128, 512], F32)

                for bi, tl in enumerate(blocks):
                    N = len(tl) * 128
                    cur_c = cTb[bi % 2]
                    nxt_c = cTb[(bi + 1) % 2]
                    xn = xnT.next()
                    pend = []
                    for j, t in enumerate(tl):
                        xt_ = xt.next()
                        T.dma("sp", xt_.ap[:, :], x_kv[t], [], [xt_])
                        pend.append((xt_, nt.stage_a(xt_.ap[:, :], xt_)))
                        if j >= 1:
                            a, b = pend[j - 1]
                            nt.stage_b(a.ap[:, :], a, b, xn, (j - 1) * 128)
                    a, b = pend[-1]
                    nt.stage_b(a.ap[:, :], a, b, xn, (len(tl) - 1) * 128)
                    for jc in range(4):
                        pv = ps_g.next()
                        pg = ps_g.next()
                        for ps, col in ((pv, jc * 128), (pg, 512 + jc * 128)):
                            for kc in range(8):
                                T.op("pe", lambda e, ps=ps, col=col, kc=kc: e.matmul(ps.ap[:, 0:N], lhsT=Wu.ap[:, kc, col:col + 128], rhs=xn.ap[:, kc, 0:N],
                                                                                  start=(kc == 0), stop=(kc == 7)), [Wu, xn], [ps], inc=(kc == 7))
                        sg = sig.next()
                        T.op("act", lambda e: e.activation(out=sg.ap[:, 0:N], in_=pg.ap[:, 0:N], func=AF.Sigmoid), [pg], [sg])
                        T.op("dve", lambda e, jc=jc: e.tensor_tensor(out=cur_c.ap[:, jc, 32:32 + N], in0=pv.ap[:, 0:N], in1=sg.ap[:, 0:N], op=ALU.mult),
                             [pv, sg], [cur_c])
                    for cc in range(4):
                        py = ps_g.next()
                        for k in range(CVK):
                            T.op("pe", lambda e, cc=cc, k=k: e.matmul(py.ap[:, 0:N], lhsT=Dg.ap[:, cc, k, :], rhs=cur_c.ap[:, cc, 2 + k:2 + k + N],
                                                                  start=(k == 0), stop=(k == CVK - 1)), [Dg, cur_c], [py], inc=(k == CVK - 1))
                        T.op("act", lambda e, cc=cc: e.activation(out=yT.ap[:, cc, 0:N], in_=py.ap[:, 0:N], func=AF.Identity, bias=cvb.ap[:, cc:cc + 1]),
                             [py, cvb], [yT])
                        T.op("act", lambda e, cc=cc: e.activation(out=ysq.ap[:, cc, 0:N], in_=py.ap[:, 0:N], func=AF.Square, bias=cvb.ap[:, cc:cc + 1]),
                             [py, cvb], [ysq])
                    T.op("pool", lambda e: e.tensor_copy(out=nxt_c.ap[:, :, 0:32], in_=cur_c.ap[:, :, N:N + 32]), [cur_c], [nxt_c])
                    for ps, src in ((ps_m, yT), (ps_q2, ysq)):
                        for cc in range(4):
                            T.op("pe", lambda e, ps=ps, src=src, cc=cc: e.matmul(ps.ap[:, 0:N], lhsT=onesf.ap[:, :], rhs=src.ap[:, cc, 0:N],
                                                                              start=(cc == 0), stop=(cc == 3)), [onesf, src], [ps], inc=(cc == 3))
                    T.op("act", lambda e: e.copy(out=mean_sb.ap[:, 0:N], in_=ps_m.ap[:, 0:N]), [ps_m], [mean_sb])
                    T.op("dve", lambda e: e.tensor_tensor(out=msq.ap[:, 0:N], in0=mean_sb.ap[:, 0:N], in1=mean_sb.ap[:, 0:N], op=ALU.mult), [mean_sb], [msq])
                    T.op("dve", lambda e: e.tensor_tensor(out=msq.ap[:, 0:N], in0=ps_q2.ap[:, 0:N], in1=msq.ap[:, 0:N], op=ALU.subtract), [ps_q2, msq], [msq])
                    T.op("act", lambda e: e.activation(out=msq.ap[:, 0:N], in_=msq.ap[:, 0:N], func=AF.Sqrt, bias=epst.ap[:, 0:1]), [msq, epst], [msq])
                    T.op("dve", lambda e: e.reciprocal(out=rstd.ap[:, 0:N], in_=msq.ap[:, 0:N]), [msq], [rstd])
                    cbase = tl[0] * 128
                    for cc in range(4):
                        d_ = dtm.next()
                        T.op("dve", lambda e, cc=cc: e.tensor_tensor(out=d_.ap[:, 0:N], in0=yT.ap[:, cc, 0:N], in1=mean_sb.ap[:, 0:N], op=ALU.subtract),
                             [yT, mean_sb], [d_])
                        T.op("dve", lambda e: e.tensor_tensor(out=d_.ap[:, 0:N], in0=d_.ap[:, 0:N], in1=rstd.ap[:, 0:N], op=ALU.mult), [d_, rstd], [d_])
                        T.op("act", lambda e, cc=cc: e.activation(out=coT_all.ap[:, cc, cbase:cbase + N], in_=d_.ap[:, 0:N], func=AF.Silu, scale=lng.ap[:, cc:cc + 1],
                                                                  bias=lnb.ap[:, cc:cc + 1]), [d_, lng, lnb], [coT_all])
            with _Scope(T):
                cand = Rot([T.sb("cand%d" % i, [128, NQT], BF16) for i in range(3)])
                for h in range(4):
                    for j in range(4):
                        cd = cand.next()
                        for pi, (ps0, pw) in enumerate(pieces):
                            T.dma("sp", cd.ap[:, ps0 * 128:(ps0 + pw) * 128], recvp[pi][(h * 4 + j) * 128:(h * 4 + j + 1) * 128, :], [recvr[pi]], [cd])
                        eng = "dve" if (j % 2 == 0) else "pool"
                        if j == 0:
                            T.op("dve", lambda e, h=h, j=j: e.tensor_scalar(out=aT.ap[:, h, :], in0=cd.ap[:, :], scalar1=sel.ap[:, j:j + 1], scalar2=None,
                                                                         op0=ALU.mult), [cd, sel], [aT])
                        else:
                            T.op("dve", lambda e, h=h, j=j: e.scalar_tensor_tensor(out=aT.ap[:, h, :], in0=cd.ap[:, :], scalar=sel.ap[:, j:j + 1],
                                                                                in1=aT.ap[:, h, :], op0=ALU.mult, op1=ALU.add), [cd, sel, aT], [aT])

            with _Scope(T):
                xr = Rot([T.sb("xr%d" % i, [128, D], F32) for i in range(2)])
                h1t = Rot([T.sb("h1t%d" % i, [128, D], F32) for i in range(2)])
                ps_mx4 = [T.ps("ps_mx%d" % i, [128, 512], F32) for i in range(4)]
                for bi, tl in enumerate(blocks):
                    for j, t in enumerate(tl):
                        ps_mx = ps_mx4[(t % 2) * 2:(t % 2) * 2 + 2]
                        xr_ = xr.next()
                        T.dma("sp", xr_.ap[:, :], x_kv[t], [], [xr_])
                        for hf in range(2):
                            for kc in range(8):
                                lhsT = aT.ap[:, kc, t * 128:(t + 1) * 128] if kc < 4 else coT_all.ap[:, kc - 4, t * 128:(t + 1) * 128]
                                T.op("pe", lambda e, hf=hf, kc=kc, lhsT=lhsT: e.matmul(ps_mx[hf].ap[:, :], lhsT=lhsT, rhs=Wout.ap[:, kc, hf * 512:(hf + 1) * 512],
                                                                                    start=(kc == 0), stop=(kc == 7)), [aT, coT_all, Wout], [ps_mx[hf]], inc=(kc == 7))
                        h1_ = h1t.next()
                        for hf in range(2):
                            T.op("dve", lambda e, hf=hf: e.tensor_tensor(out=h1_.ap[:, hf * 512:(hf + 1) * 512], in0=xr_.ap[:, hf * 512:(hf + 1) * 512],
                                                                      in1=ps_mx[hf].ap[:, :], op=ALU.add), [xr_, ps_mx[hf]], [h1_])
                        T.dma("pool", h1_scr[t], h1_.ap[:, :], [h1_], [h1r[t]])
        scopeX.__exit__(None, None, None)

        with _Scope(T):
            Wcq = T.sb("Wcq", [128, 8, 1024], BF16)
            Wco = T.sb("Wco", [128, 8, 1024], BF16)
            Wckv = T.sb("Wckv", [128, 8, 2048], BF16)
            gcr = T.sb("gcr", [128, 8], F32)
            gme = T.sb("gme", [128, 8], F32)
            load_vec(gcr, norm_cross_g)
            load_vec(gme, norm_mem_g)
            nt = NormT(T, ident, neghalf, gcr)
            load_wb(T, Wckv, 8, wbf["ckv"], wres["ckv"], 2048)
            load_wb(T, Wcq, 8, wbf["cq"], wres["cq"], 1024)
            load_wb(T, Wco, 8, wbf["co"], wres["co"], 1024)
            onesb = T.sb("onesb", [128, 128], BF16)
            T.op("pool", lambda e: e.memset(onesb.ap[:, :], 1.0), [], [onesb])
            ps_g = Rot([T.ps("ps_g%d" % i, [128, 512], F32) for i in range(4)])
            ps_mx = [T.ps("ps_mx%d" % i, [128, 512], F32) for i in range(2)]
            xt = Rot([T.sb("xt%d" % i, [128, D], F32) for i in range(2)])
            memT = T.sb("memT", [128, 8, NMEM], BF16)
            KcT = T.sb("KcT", [128, 8, NMEM], BF16)
            Vc = T.sb("Vc", [128, 2, 1024], BF16)
            for mt in range(2):
                xt_ = xt.next()
                T.dma("sp", xt_.ap[:, :], mem_d[mt * 128:(mt + 1) * 128, :], [], [xt_])
                ss = nt.stage_a(xt_.ap[:, :], xt_)
                nt.stage_b(xt_.ap[:, :], xt_, ss, memT, mt * 128, g8=gme)
            for e8 in range(8):
                ps = ps_g.next()
                for kc in range(8):
                    T.op("pe", lambda e, kc=kc, e8=e8: e.matmul(ps.ap[:, 0:NMEM], lhsT=Wckv.ap[:, kc, e8 * 128:(e8 + 1) * 128], rhs=memT.ap[:, kc, :],
                                                            start=(kc == 0), stop=(kc == 7)), [Wckv, memT], [ps], inc=(kc == 7))
                T.op("dve", lambda e, e8=e8: e.tensor_copy(out=KcT.ap[:, e8, :], in_=ps.ap[:, 0:NMEM]), [ps], [KcT])
            for mt in range(2):
                for hf in range(2):
                    ps = ps_g.next()
                    for kc in range(8):
                        T.op("pe", lambda e, kc=kc, mt=mt, hf=hf: e.matmul(ps.ap[:, :], lhsT=memT.ap[:, kc, mt * 128:(mt + 1) * 128],
                                                                        rhs=Wckv.ap[:, kc, 1024 + hf * 512:1024 + (hf + 1) * 512],
                                                                        start=(kc == 0), stop=(kc == 7)), [Wckv, memT], [ps], inc=(kc == 7))
                    T.op("act", lambda e, mt=mt, hf=hf: e.copy(out=Vc.ap[:, mt, hf * 512:(hf + 1) * 512], in_=ps.ap[:, :]), [ps], [Vc])
            h1b = Rot([T.sb("h1b%d" % i, [128, 4, D], F32) for i in range(2)])
            h1nT = Rot([T.sb("h1nT%d" % i, [128, 8, 512], BF16) for i in range(2)])
            qcT = T.sb("qcT", [128, 8, 512], BF16)
            ocT = T.sb("ocT", [128, 8, 512], BF16)
            PcT = Rot([T.sb("PcT%d" % i, [128, 2, 512], BF16) for i in range(2)])
            rl = Rot([T.sb("rl%d" % i, [128, 512], F32) for i in range(2)])
            h2t = Rot([T.sb("h2t%d" % i, [128, D], F32) for i in range(2)])
            for bi, tl in enumerate(blocks):
                N = len(tl) * 128
                hb = h1b.next()
                hn = h1nT.next()
                pend = []
                for j, t in enumerate(tl):
                    T.dma("sp", hb.ap[:, j, :], h1_scr[t], [h1r[t]], [hb])
                    pend.append(nt.stage_a(hb.ap[:, j, :], hb))
                    if j >= 1:
                        nt.stage_b(hb.ap[:, j - 1, :], hb, pend[j - 1], hn, (j - 1) * 128)
                nt.stage_b(hb.ap[:, len(tl) - 1, :], hb, pend[-1], hn, (len(tl) - 1) * 128)
                for e8 in range(8):
                    ps = ps_g.next()
                    for kc in range(8):
                        T.op("pe", lambda e, kc=kc, e8=e8: e.matmul(ps.ap[:, 0:N], lhsT=Wcq.ap[:, kc, e8 * 128:(e8 + 1) * 128], rhs=hn.ap[:, kc, 0:N],
                                                                start=(kc == 0), stop=(kc == 7)), [Wcq, hn], [ps], inc=(kc == 7))
                    T.op("act", lambda e, e8=e8: e.copy(out=qcT.ap[:, e8, 0:N], in_=ps.ap[:, 0:N]), [ps], [qcT])
                for hh in range(4):
                    pc = PcT.next()
                    for mt in range(2):
                        ps = ps_g.next()
                        for dc in range(2):
                            T.op("pe", lambda e, mt=mt, dc=dc: e.matmul(ps.ap[:, 0:N], lhsT=KcT.ap[:, hh * 2 + dc, mt * 128:(mt + 1) * 128],
                                                                    rhs=qcT.ap[:, hh * 2 + dc, 0:N], start=(dc == 0), stop=(dc == 1)),
                                 [KcT, qcT], [ps], inc=(dc == 1))
                        T.op("act", lambda e, mt=mt: e.activation(out=pc.ap[:, mt, 0:N], in_=ps.ap[:, 0:N], func=AF.Exp, scale=1.0 / 16), [ps], [pc])
                    pl = ps_g.next()
                    for mt in range(2):
                        T.op("pe", lambda e, mt=mt: e.matmul(pl.ap[:, 0:N], lhsT=onesb.ap[:, :], rhs=pc.ap[:, mt, 0:N], start=(mt == 0), stop=(mt == 1)),
                             [onesb, pc], [pl], inc=(mt == 1))
                    rl_ = rl.next()
                    T.op("dve", lambda e: e.reciprocal(out=rl_.ap[:, 0:N], in_=pl.ap[:, 0:N]), [pl], [rl_])
                    for dc in range(2):
                        po = ps_g.next()
                        for mt in range(2):
                            T.op("pe", lambda e, mt=mt, dc=dc: e.matmul(po.ap[:, 0:N], lhsT=Vc.ap[:, mt, hh * 256 + dc * 128:hh * 256 + (dc + 1) * 128],
                                                                    rhs=pc.ap[:, mt, 0:N], start=(mt == 0), stop=(mt == 1)), [Vc, pc], [po], inc=(mt == 1))
                        T.op("dve", lambda e, dc=dc: e.tensor_tensor(out=ocT.ap[:, hh * 2 + dc, 0:N], in0=po.ap[:, 0:N], in1=rl_.ap[:, 0:N], op=ALU.mult),
                             [po, rl_], [ocT])
                for j, t in enumerate(tl):
                    for hf in range(2):
                        for e8 in range(8):
                            T.op("pe", lambda e, hf=hf, e8=e8, j=j: e.matmul(ps_mx[hf].ap[:, :], lhsT=ocT.ap[:, e8, j * 128:(j + 1) * 128],
                                                                          rhs=Wco.ap[:, e8, hf * 512:(hf + 1) * 512], start=(e8 == 0), stop=(e8 == 7)),
                                 [ocT, Wco], [ps_mx[hf]], inc=(e8 == 7))
                    h2_ = h2t.next()
                    for hf in range(2):
                        T.op("dve", lambda e, hf=hf, j=j: e.tensor_tensor(out=h2_.ap[:, hf * 512:(hf + 1) * 512], in0=hb.ap[:, j, hf * 512:(hf + 1) * 512],
                                                                       in1=ps_mx[hf].ap[:, :], op=ALU.add), [hb, ps_mx[hf]], [h2_])
                    T.dma("pool", h2_scr[t], h2_.ap[:, :], [h2_], [h2r[t]])

        fblocks = [[0]] + [list(range(i, min(i + 2, NQ))) for i in range(1, NQ, 2)]
        with _Scope(T):
            gff = T.sb("gff", [128, 8], F32)
            load_vec(gff, norm_ffn_g)
            nt = NormT(T, ident, neghalf, gff, nptr=1)
            Wup = T.sb("Wup", [128, 8, 2 * D_FF], BF16)
            Wdn = T.sb("Wdn", [128, NFC, D], BF16)
            load_wb(T, Wup, 8, wbf["up"], wres["up"], 2 * D_FF)
            load_wb(T, Wdn, NFC, wbf["dn"], wres["dn"], D)
            ffw = T.sb("ffw", [128, 3, 2 * NFC], F32)
            for k in range(3):
                T.dma("sp", ffw.ap[:, k, :], ffn_dw_w[k].rearrange("(c p) -> p c", p=128), [], [ffw], slow=True)
            gfin = T.sb("gfin", [128, D], F32)
            T.dma("sp", gfin.ap[:, :], norm_final_g.partition_broadcast(128), [], [gfin])
            stash = T.sb("stash", [128, NFC, 2, 2], F32)
            T.op("pool", lambda e: e.memset(stash.ap[:, :, :, :], 0.0), [], [stash])
            U = Rot([T.sb("U%d" % i, [128, 2, 258], F32) for i in range(3)])
            ca = Rot([T.sb("ca%d" % i, [128, 256], F32) for i in range(2)])
            cb_ = Rot([T.sb("cb%d" % i, [128, 256], F32) for i in range(2)])
            sa = Rot([T.sb("sa%d" % i, [128, 256], F32) for i in range(2)])
            pt1 = Rot([T.sb("pt1_%d" % i, [128, 256], F32) for i in range(2)])
            pt0 = Rot([T.sb("pt0_%d" % i, [128, 256], F32) for i in range(2)])
            zT = Rot([T.sb("zT%d" % i, [128, 256], BF16) for i in range(3)])
            h2b = Rot([T.sb("h2b%d" % i, [128, 2, D], F32) for i in range(2)])
            h3nT = Rot([T.sb("h3nT%d" % i, [128, 8, 256], BF16) for i in range(2)])
            hfin = T.sb("hfin", [128, D], F32)
            ot = Rot([T.sb("ot%d" % i, [128, D], F32) for i in range(1)])
            fs = Rot([T.sb("fs%d" % i, [128, 4], F32) for i in range(2)])
            ps_up = Rot([T.ps("ps_up%d" % i, [128, 2, 256], F32) for i in range(3)])
            ps_acc = [T.ps("ps_acc%d" % i, [128, 512], F32) for i in range(4)]
            def prep_block(tl_):
                hb_ = h2b.next()
                hn_ = h3nT.next()
                pend = []
                for j, t in enumerate(tl_):
                    T.dma("sp", hb_.ap[:, j, :], h2_scr[t], [h2r[t]], [hb_])
                    pend.append(nt.stage_a(hb_.ap[:, j, :], hb_))
                    if j >= 1:
                        nt.stage_b(hb_.ap[:, j - 1, :], hb_, pend[j - 1], hn_, (j - 1) * 128)
                nt.stage_b(hb_.ap[:, len(tl_) - 1, :], hb_, pend[-1], hn_, (len(tl_) - 1) * 128)
                return hb_, hn_

            prepped = prep_block(fblocks[0])
            for bi, tl in enumerate(fblocks):
                N = len(tl) * 128
                halo = (bi == 0)
                hb, hn = prepped

                def up(fc):
                    pu = ps_up.next()
                    for ab in range(2):
                        col = ab * D_FF + fc * 128
                        for kc in range(8):
                            T.op("pe", lambda e, ab=ab, col=col, kc=kc: e.matmul(pu.ap[:, ab, 0:N], lhsT=Wup.ap[:, kc, col:col + 128], rhs=hn.ap[:, kc, 0:N],
                                                                              start=(kc == 0), stop=(kc == 7)), [Wup, hn], [pu], inc=(kc == 7))
                    return pu

                def mid_a(fc, pu):
                    u_ = U.next()
                    T.op("act", lambda e: e.copy(out=u_.ap[:, :, 2:2 + N], in_=pu.ap[:, :, 0:N]), [pu], [u_])
                    T.op("act", lambda e: e.copy(out=u_.ap[:, :, 0:2], in_=stash.ap[:, fc, :, :]), [stash], [u_])
                    T.op("act", lambda e: e.copy(out=stash.ap[:, fc, :, :], in_=u_.ap[:, :, N:N + 2]), [u_], [stash])
                    return u_

                def mid_b(fc, u_):
                    if halo:
                        return None
                    a_ = ca.next()
                    b_ = cb_.next()
                    dst = (a_, b_)
                    ptmp = {1: pt1.next(), 0: pt0.next()}
                    for k in (2, 1, 0):
                        for ab in range(2):
                            wc = ab * NFC + fc
                            d_ = dst[ab]
                            if ab == 0:
                                if k == 2:
                                    T.op("dve", lambda e, wc=wc, d_=d_: e.tensor_scalar(out=d_.ap[:, 0:N], in0=u_.ap[:, 0, 2:2 + N], scalar1=ffw.ap[:, 2, wc:wc + 1],
                                                                                     scalar2=None, op0=ALU.mult), [u_, ffw], [d_])
                                else:
                                    T.op("dve", lambda e, wc=wc, k=k, d_=d_: e.scalar_tensor_tensor(out=d_.ap[:, 0:N], in0=u_.ap[:, 0, k:k + N],
                                                                                                 scalar=ffw.ap[:, k, wc:wc + 1], in1=d_.ap[:, 0:N],
                                                                                                 op0=ALU.mult, op1=ALU.add), [u_, ffw, d_], [d_])
                            else:
                                tgt = d_ if k == 2 else ptmp[k]
                                T.op("pool", lambda e, wc=wc, k=k, tgt=tgt: e.tensor_scalar(out=tgt.ap[:, 0:N], in0=u_.ap[:, 1, k:k + N], scalar1=ffw.ap[:, k, wc:wc + 1],
                                                                                         scalar2=0.0, op0=ALU.mult, op1=ALU.add), [u_, ffw], [tgt])
                    T.op("pool", lambda e: e.tensor_tensor(out=b_.ap[:, 0:N], in0=b_.ap[:, 0:N], in1=ptmp[1].ap[:, 0:N], op=ALU.add), [b_, ptmp[1]], [b_])
                    T.op("dve", lambda e: e.tensor_tensor(out=b_.ap[:, 0:N], in0=b_.ap[:, 0:N], in1=ptmp[0].ap[:, 0:N], op=ALU.add), [b_, ptmp[0]], [b_])
                    s_ = sa.next()
                    z_ = zT.next()
                    T.op("act", lambda e: e.activation(out=s_.ap[:, 0:N], in_=a_.ap[:, 0:N], func=AF.Silu), [a_], [s_])
                    T.op("dve", lambda e: e.tensor_tensor(out=z_.ap[:, 0:N], in0=s_.ap[:, 0:N], in1=b_.ap[:, 0:N], op=ALU.mult), [s_, b_], [z_])
                    return z_

                def down(fc, z_):
                    for j in range(len(tl)):
                        for hf in range(2):
                            lastmm = (j == len(tl) - 1 and hf == 1)
                            T.op("pe", lambda e, j=j, hf=hf: e.matmul(ps_acc[j * 2 + hf].ap[:, :], lhsT=z_.ap[:, j * 128:(j + 1) * 128],
                                                                   rhs=Wdn.ap[:, fc, hf * 512:(hf + 1) * 512], start=(fc == 0), stop=(fc == NFC - 1)),
                                 [z_, Wdn], [ps_acc[j * 2 + hf]], inc=(lastmm or fc == NFC - 1))

                ups = [up(0), up(1)]
                unext = mid_a(0, ups.pop(0))
                zprev = None
                for fc in range(NFC):
                    if fc + 2 < NFC:
                        ups.append(up(fc + 2))
                    ucur = unext
                    if fc + 1 < NFC:
                        unext = mid_a(fc + 1, ups.pop(0))
                    z_ = mid_b(fc, ucur)
                    if zprev is not None:
                        down(fc - 1, zprev)
                    zprev = z_
                    if fc == NFC // 2 and bi + 1 < len(fblocks):
                        prepped = prep_block(fblocks[bi + 1])
                if zprev is not None:
                    down(NFC - 1, zprev)
                if halo:
                    T.op("pool", lambda e: e.tensor_scalar(out=stash.ap[:, :, :, :], in0=stash.ap[:, :, :, :], scalar1=hvalid.ap[:, 0:1], scalar2=0.0,
                                                           op0=ALU.mult, op1=ALU.add), [stash, hvalid], [stash])
                    continue
                for j, t in enumerate(tl):
                    for hf in range(2):
                        T.op("dve", lambda e, j=j, hf=hf: e.tensor_tensor(out=hfin.ap[:, hf * 512:(hf + 1) * 512], in0=hb.ap[:, j, hf * 512:(hf + 1) * 512],
                                                                       in1=ps_acc[j * 2 + hf].ap[:, :], op=ALU.add), [hb, ps_acc[j * 2 + hf]], [hfin])
                    f_ = fs.next()
                    T.op("act", lambda e: e.activation(out=nt.junk.ap[:, :], in_=hfin.ap[:, :], func=AF.Square, accum_out=f_.ap[:, 0:1]), [hfin], [f_])
                    T.op("dve", lambda e: e.tensor_scalar(out=f_.ap[:, 1:2], in0=f_.ap[:, 0:1], scalar1=1.0 / D, scalar2=EPS, op0=ALU.mult, op1=ALU.add),
                         [f_], [f_])
                    T.op("pool", lambda e: e.tensor_tensor(out=f_.ap[:, 2:3], in0=f_.ap[:, 1:2], in1=neghalf.ap[:, 0:1], op=ALU.pow), [f_, neghalf], [f_])
                    o_ = ot.next()
                    T.op("dve", lambda e: e.scalar_tensor_tensor(out=o_.ap[:, :], in0=hfin.ap[:, :], scalar=f_.ap[:, 2:3], in1=gfin.ap[:, :],
                                                                 op0=ALU.mult, op1=ALU.mult), [hfin, f_, gfin], [o_])
                    T.dma("pool", out_d[t - 1], o_.ap[:, :], [o_], [])
        T.finish()
    return nc, T


def _host_prep(inputs, S):
    NT = S // 128
    OWN = NT // 4
    x = np.asarray(inputs["x"], dtype=np.float32)
    mem = np.asarray(inputs["mem"], dtype=np.float32)
    pos = np.asarray(inputs["positions"], dtype=np.int32)
    B = x.shape[0]
    ident = np.eye(128, dtype=np.float32)
    tri = np.triu(np.ones((128, 128), dtype=np.float32))
    wnames = ["norm_mix_g", "lam_q1", "lam_k1", "lam_q2", "lam_k2", "subln_g", "cv_dw_w", "cv_dw_b", "cv_ln_g", "cv_ln_b",
              "w_out", "norm_cross_g", "norm_mem_g", "w_cq", "w_ckv", "w_co", "norm_ffn_g", "w_up", "ffn_dw_w", "w_down"]
    common = {n: np.ascontiguousarray(np.asarray(inputs[n], dtype=np.float32)[0]) for n in wnames}
    common["norm_final_g"] = np.ascontiguousarray(np.asarray(inputs["norm_final_g"], dtype=np.float32))
    common["ident"] = ident
    common["tri"] = tri
    w_in = np.asarray(inputs["w_in"], dtype=np.float32)[0]
    common["w_u"] = np.ascontiguousarray(w_in[:, 1536:2560])
    in_maps = []
    for b in range(B):
        xt = np.ascontiguousarray(x[b].reshape(NT, 128, D))
        pk = np.ascontiguousarray(pos[b].reshape(NT, 128).T)
        memb = np.ascontiguousarray(mem[b])
        for r in range(4):
            m = dict(common)
            m["x_b"] = xt
            m["pos_b"] = pk
            m["w_qkv"] = np.ascontiguousarray(np.concatenate(
                [w_in[:, r * 128:(r + 1) * 128], w_in[:, 512 + r * 128:512 + (r + 1) * 128], w_in[:, 1024 + r * 128:1024 + (r + 1) * 128]], axis=1))
            xo = np.zeros((OWN + 1, 128, D), dtype=np.float32)
            xo[1:] = xt[r * OWN:(r + 1) * OWN]
            if r > 0:
                xo[0] = xt[r * OWN - 1]
            m["x_own"] = xo
            sl = np.zeros((128, 4), dtype=np.float32)
            sl[:, r] = 1.0
            m["sel"] = sl
            m["hvalid"] = np.full((128, 1), 1.0 if r > 0 else 0.0, dtype=np.float32)
            m["mem"] = memb
            in_maps.append(m)
    return in_maps


_CACHE = {}


def kernel(**inputs):
    x = np.asarray(inputs["x"])
    B, S, _ = x.shape
    NT = S // 128
    OWN = NT // 4
    in_maps = _host_prep(inputs, S)
    if S not in _CACHE:
        _CACHE[S] = build(S)[0]
    nc = _CACHE[S]
    res = run_bass_kernel_spmd(nc, in_maps, core_ids=list(range(len(in_maps))))
    out = np.zeros((B, S, D), dtype=np.float32)
    i = 0
    for b in range(B):
        for c in range(4):
            out[b, c * OWN * 128:(c + 1) * OWN * 128, :] = np.asarray(res.results[i]["out"]).reshape(OWN * 128, D)
            i += 1
    return out
```

```python
import numpy as np
import ml_dtypes
from contextlib import ExitStack
import concourse.bass as bass
import concourse.mybir as mybir
from concourse.bass_utils import run_bass_kernel_spmd

F32 = mybir.dt.float32
BF16 = mybir.dt.bfloat16
I32 = mybir.dt.int32
ALU = mybir.AluOpType
AF = mybir.ActivationFunctionType

NDS = 24


class Res:
    __slots__ = ("name", "ap", "w", "r")

    def __init__(self, name, ap=None):
        self.name = name
        self.ap = ap
        self.w = {}
        self.r = {}


class Trk:
    def __init__(self, nc, es):
        self.nc = nc
        self.es = es
        self.eng = {"pe": nc.tensor, "act": nc.scalar, "dve": nc.vector, "pool": nc.gpsimd, "sp": nc.sync}
        self.cnt = {k: 0 for k in self.eng}
        self.sem = {k: es.enter_context(nc.semaphore("s_" + k)) for k in self.eng if k != "sp"}
        self.waited = {k: {} for k in self.eng}
        self.dsem = [es.enter_context(nc.semaphore("d%d" % i)) for i in range(NDS)]
        self.dcnt = [0] * NDS
        self.rr = 0
        self.nres = 0
        self._cst = {}
        self.n_ins = 0
        self.log = {k: [] for k in self.eng}

    def sb(self, name, shape, dt):
        self.nres += 1
        t = self.es.enter_context(self.nc.sbuf_tensor("sb%d_%s" % (self.nres, name), list(shape), dt))
        return Res(name, t)

    def ps(self, name, shape, dt):
        self.nres += 1
        t = self.es.enter_context(self.nc.psum_tensor("ps%d_%s" % (self.nres, name), list(shape), dt))
        return Res(name, t)

    def cst(self, val):
        key = float(val)
        if key not in self._cst:
            r = self.sb("cst%d" % len(self._cst), [128, 1], F32)
            self.op("pool", lambda e: e.memset(r.ap[:, :], key), [], [r])
            self._cst[key] = r
        r = self._cst[key]
        return r

    def _sync(self, eng, reads, writes):
        deps = {}

        def add(tag):
            k, sem, val = tag
            if k not in deps or deps[k][1] < val:
                deps[k] = (sem, val)

        for r in reads:
            for t in r.w.values():
                add(t)
        import os
        strict = bool(os.environ.get("KDBG_WAW"))
        for w in writes:
            for t in w.w.values():
                if t[0] != eng or strict:
                    add(t)
            for t in w.r.values():
                if t[0] != eng or strict:
                    add(t)
        e = self.eng[eng]
        for k, (sem, val) in deps.items():
            if k == eng and eng == "pe":
                continue
            if self.waited[eng].get(k, 0) >= val:
                continue
            e.wait_ge(sem, val)
            self.log[eng].append(("wait", id(sem), val, k))
            self.waited[eng][k] = val
            self.n_ins += 1

    def _mark(self, tag, reads, writes):
        k = tag[0]
        for r in reads:
            if k not in r.r or r.r[k][2] < tag[2]:
                r.r[k] = tag
        for w in writes:
            if k not in w.w or w.w[k][2] < tag[2]:
                w.w[k] = tag
            w.r = {}

    def op(self, eng, fn, reads=(), writes=(), inc=True):
        reads = [x for x in reads if x is not None]
        writes = [x for x in writes if x is not None]
        self._sync(eng, reads, writes)
        ins = fn(self.eng[eng])
        self.n_ins += 1
        if inc:
            self.cnt[eng] += 1
            ins.then_inc(self.sem[eng], 1)
            self.log[eng].append(("inc", id(self.sem[eng]), 1, eng))
            val = self.cnt[eng]
        else:
            val = self.cnt[eng] + 1
        self._mark((eng, self.sem[eng], val), reads, writes)
        return ins

    def dma(self, queue, out_ap, in_ap, reads=(), writes=(), slow=False):
        reads = [x for x in reads if x is not None]
        writes = [x for x in writes if x is not None]
        self._sync(queue, reads, writes)
        i = self.rr
        self.rr = (i + 1) % NDS
        self.dcnt[i] += 16
        self.eng[queue].dma_start(out=out_ap, in_=in_ap, allow_slow_non_contiguous=slow).then_inc(self.dsem[i], 16)
        self.log[queue].append(("inc", id(self.dsem[i]), 16, ("dma", i)))
        self.n_ins += 1
        self._mark((("dma", i), self.dsem[i], self.dcnt[i]), reads, writes)

    def finish(self):
        e = self.eng["sp"]
        for i in range(NDS):
            if self.dcnt[i]:
                e.wait_ge(self.dsem[i], self.dcnt[i])
        for k in self.sem:
            if self.cnt[k]:
                e.wait_ge(self.sem[k], self.cnt[k])

    def barrier(self):
        for k, e in self.eng.items():
            for i in range(NDS):
                if self.dcnt[i] and self.waited[k].get(("dma", i), 0) < self.dcnt[i]:
                    e.wait_ge(self.dsem[i], self.dcnt[i])
                    self.log[k].append(("wait", id(self.dsem[i]), self.dcnt[i], ("dma", i)))
                    self.waited[k][("dma", i)] = self.dcnt[i]
            for k2 in self.sem:
                if k2 != k and self.cnt[k2] and self.waited[k].get(k2, 0) < self.cnt[k2]:
                    e.wait_ge(self.sem[k2], self.cnt[k2])
                    self.log[k].append(("wait", id(self.sem[k2]), self.cnt[k2], k2))
                    self.waited[k][k2] = self.cnt[k2]


class _Scope:
    def __init__(self, T):
        self.T = T

    def __enter__(self):
        self.old = self.T.es
        self.st = ExitStack()
        self.st.__enter__()
        self.T.es = self.st
        return self

    def __exit__(self, *a):
        self.T.barrier()
        self.T.es = self.old
        return self.st.__exit__(*a)


D = 1024
EPS = 1e-6
BIG = -30000.0
ROPE_THETA = 500000.0
D_FF = 2816
NFC = D_FF // 128
CVK = 31


class Rot:
    def __init__(self, items):
        self.items = items
        self.i = 0

    def next(self):
        r = self.items[self.i % len(self.items)]
        self.i += 1
        return r


def bcast_mid(ap2d, n):
    l = [list(x) for x in ap2d.ap]
    return bass.AP(ap2d.tensor, ap2d.offset, [l[0], [0, n], l[-1]])


def load_w(T, dst, kcn, src2d, col0, ncols, gain, stg, eng="pool"):
    CH = stg.items[0].ap.shape[1]
    for kc in range(kcn):
        for c0 in range(0, ncols, CH):
            cw = min(CH, ncols - c0)
            st = stg.next()
            T.dma("sp", st.ap[:, 0:cw], src2d[kc * 128:(kc + 1) * 128, col0 + c0:col0 + c0 + cw], [], [st])
            if gain is not None:
                T.op(eng, lambda e, st=st, kc=kc, c0=c0, cw=cw: e.tensor_scalar(
                    out=dst.ap[:, kc, c0:c0 + cw], in0=st.ap[:, 0:cw], scalar1=gain.ap[:, kc:kc + 1], scalar2=0.0,
                    op0=ALU.mult, op1=ALU.add), [st, gain], [dst])
            else:
                T.op(eng, lambda e, st=st, kc=kc, c0=c0, cw=cw: e.tensor_copy(
                    out=dst.ap[:, kc, c0:c0 + cw], in_=st.ap[:, 0:cw]), [st], [dst])


def load_wd(T, dst, kcn, src2d, col0, ncols):
    import os
    if os.environ.get("KDBG_POOLW"):
        if not hasattr(T, "_stg"):
            T._stg = None
        stg = Rot([T.sb("wstg%d" % i, [128, 1024], F32) for i in range(2)])
        for kc in range(kcn):
            for c0 in range(0, ncols, 1024):
                cw = min(1024, ncols - c0)
                st = stg.next()
                T.dma("sp", st.ap[:, 0:cw], src2d[kc * 128:(kc + 1) * 128, col0 + c0:col0 + c0 + cw], [], [st])
                T.op("pool", lambda e, st=st, kc=kc, c0=c0, cw=cw: e.tensor_copy(out=dst.ap[:, kc, c0:c0 + cw], in_=st.ap[:, 0:cw]), [st], [dst])
        return
    CH = 512
    for kc in range(kcn):
        for c0 in range(0, ncols, CH):
            cw = min(CH, ncols - c0)
            T.dma("pool", dst.ap[:, kc, c0:c0 + cw], src2d[kc * 128:(kc + 1) * 128, col0 + c0:col0 + c0 + cw], [], [dst])


def load_wb(T, dst, kcn, src_bf, res, ncols, col0=0):
    for kc in range(kcn):
        T.dma("sp", dst.ap[:, kc, 0:ncols], src_bf[kc * 128:(kc + 1) * 128, col0:col0 + ncols], [res], [dst])


class NormT:
    def __init__(self, T, ident, neghalf, g8, nbuf=3, nptr=2):
        self.T = T
        self.ident = ident
        self.neghalf = neghalf
        self.g8 = g8
        self.junk = T.sb("nt_junk", [128, D], BF16)
        self.ss = Rot([T.sb("nt_ss%d" % i, [128, 4], F32) for i in range(nbuf)])
        self.xs = Rot([T.sb("nt_xs%d" % i, [128, D], BF16) for i in range(2)])
        self.ptr = Rot([T.ps("nt_ptr%d" % i, [128, D], BF16) for i in range(nptr)])

    def stage_a(self, src_ap, src_res):
        T = self.T
        ss = self.ss.next()
        T.op("act", lambda e: e.activation(out=self.junk.ap[:, :], in_=src_ap, func=AF.Square, accum_out=ss.ap[:, 0:1]),
             [src_res], [ss])
        T.op("dve", lambda e: e.tensor_scalar(out=ss.ap[:, 1:2], in0=ss.ap[:, 0:1], scalar1=1.0 / D, scalar2=EPS,
                                              op0=ALU.mult, op1=ALU.add), [ss], [ss])
        T.op("pool", lambda e: e.tensor_tensor(out=ss.ap[:, 2:3], in0=ss.ap[:, 1:2], in1=self.neghalf.ap[:, 0:1], op=ALU.pow),
             [ss, self.neghalf], [ss])
        return ss

    def stage_b(self, src_ap, src_res, ss, dstT, col0, g8=None):
        T = self.T
        xs = self.xs.next()
        ptr = self.ptr.next()
        T.op("act", lambda e: e.activation(out=xs.ap[:, :], in_=src_ap, func=AF.Copy, scale=ss.ap[:, 2:3]),
             [src_res, ss], [xs])
        for kc in range(8):
            T.op("pe", lambda e, kc=kc: e.transpose(out=ptr.ap[:, kc * 128:(kc + 1) * 128],
                                                    in_=xs.ap[:, kc * 128:(kc + 1) * 128], identity=self.ident.ap[:, :]),
                 [xs, self.ident], [ptr], inc=(kc == 7))
        g8 = self.g8 if g8 is None else g8
        l = [list(x) for x in g8.ap[:, :].ap]
        gb = bass.AP(g8.ap[:, :].tensor, g8.ap[:, :].offset, [l[0], l[1], [0, 128]])
        T.op("dve", lambda e: e.tensor_tensor(out=dstT.ap[:, :, col0:col0 + 128], in0=ptr.ap[:, :].rearrange("p (c t) -> p c t", c=8),
                                              in1=gb, op=ALU.mult), [ptr, g8], [dstT])


def build(S):
    NT = S // 128
    OWN = NT // 4
    NQ = OWN + 1
    NQT = NQ * 128
    NMEM = 256
    nc = bass.Bass("TRN2", target_bir_lowering=False)

    def din(name, shape, dt=F32):
        return nc.dram_tensor(name, list(shape), dt, kind="ExternalInput").ap()

    x_b = din("x_b", [NT, 128, D])
    x_kv = din("x_own", [NQ, 128, D])
    pos_kv = din("pos_b", [128, NT], I32)
    sel_d = din("sel", [128, 4])
    hvalid_d = din("hvalid", [128, 1])
    ident_d = din("ident", [128, 128])
    tri_d = din("tri", [128, 128])
    mem_d = din("mem", [NMEM, D])
    norm_mix_g = din("norm_mix_g", [D])
    w_qkv = din("w_qkv", [D, 384])
    w_in = din("w_u", [D, 1024])
    lam_d = [din(n, [64]) for n in ("lam_q1", "lam_k1", "lam_q2", "lam_k2")]
    subln_g = din("subln_g", [128])
    cv_dw_w = din("cv_dw_w", [CVK, 512])
    cv_dw_b = din("cv_dw_b", [512])
    cv_ln_g = din("cv_ln_g", [512])
    cv_ln_b = din("cv_ln_b", [512])
    w_out = din("w_out", [D, D])
    norm_cross_g = din("norm_cross_g", [D])
    norm_mem_g = din("norm_mem_g", [D])
    w_cq = din("w_cq", [D, D])
    w_ckv = din("w_ckv", [D, 2 * D])
    w_co = din("w_co", [D, D])
    norm_ffn_g = din("norm_ffn_g", [D])
    w_up = din("w_up", [D, 2 * D_FF])
    ffn_dw_w = din("ffn_dw_w", [3, 2 * D_FF])
    w_down = din("w_down", [D_FF, D])
    norm_final_g = din("norm_final_g", [D])
    out_d = nc.dram_tensor("out", [OWN, 128, D], F32, kind="ExternalOutput").ap()

    pieces = []
    _p = 0
    while _p < NQ:
        _w = min(7, NQ - _p)
        pieces.append((_p, _w))
        _p += _w

    def piece_of(pos):
        for i_, (s_, w_) in enumerate(pieces):
            if s_ <= pos < s_ + w_:
                return i_, pos - s_

    sendp = [nc.dram_tensor("sendb%d" % i, [4 * 128, w * 128], BF16, kind="Internal").ap() for i, (s_, w) in enumerate(pieces)]
    recvp = [nc.dram_tensor("recvb%d" % i, [16 * 128, w * 128], BF16, kind="Internal").ap() for i, (s_, w) in enumerate(pieces)]
    wsrc = {"u": (w_in, D, 1024), "out": (w_out, D, D), "cq": (w_cq, D, D), "co": (w_co, D, D), "ckv": (w_ckv, D, 2 * D),
            "up": (w_up, D, 2 * D_FF), "dn": (w_down, D_FF, D)}
    wbf = {k: nc.dram_tensor("wbf_" + k, [r_, c_], BF16, kind="Internal").ap() for k, (a_, r_, c_) in wsrc.items()}
    wres = {k: Res("wbf_" + k) for k in wsrc}
    h1_scr = nc.dram_tensor("h1_scr", [NQ, 128, D], F32, kind="Internal").ap()
    h2_scr = nc.dram_tensor("h2_scr", [NQ, 128, D], F32, kind="Internal").ap()
    sendr = [Res("sendr%d" % i) for i in range(len(pieces))]
    recvr = [Res("recvr%d" % i) for i in range(len(pieces))]
    h1r = [Res("h1r%d" % s) for s in range(NQ)]
    h2r = [Res("h2r%d" % s) for s in range(NQ)]

    inv_freq = (np.float32(ROPE_THETA) ** (-(np.arange(0, 16, 2, dtype=np.float32)) / np.float32(16))).astype(np.float32)
    TWO_PI = float(2 * np.pi)

    with ExitStack() as es:
        T = Trk(nc, es)
        ident = T.sb("ident", [128, 128], BF16)
        tri = T.sb("tri", [128, 128], BF16)
        neghalf = T.sb("neghalf", [128, 1], F32)
        neg_lam = T.sb("neg_lam", [128, 1], F32)
        gsub = T.sb("gsub", [128, 128], F32)
        sel = T.sb("sel", [128, 4], F32)
        hvalid = T.sb("hvalid", [128, 1], F32)
        T.op("pool", lambda e: e.memset(neghalf.ap[:, :], -0.5), [], [neghalf])
        T.dma("sp", sel.ap[:, :], sel_d[:, :], [], [sel])
        T.dma("sp", hvalid.ap[:, :], hvalid_d[:, :], [], [hvalid])

        def load_vec(dst, src1d):
            T.dma("sp", dst.ap[:, :], src1d.rearrange("(c p) -> p c", p=128), [], [dst], slow=True)

        with _Scope(T):
            tmpf = T.sb("tmpf", [128, 128], F32)
            tmpf2 = T.sb("tmpf2", [128, 128], F32)
            T.dma("sp", tmpf.ap[:, :], ident_d[:, :], [], [tmpf])
            T.dma("sp", tmpf2.ap[:, :], tri_d[:, :], [], [tmpf2])
            T.op("dve", lambda e: e.tensor_copy(out=ident.ap[:, :], in_=tmpf.ap[:, :]), [tmpf], [ident])
            T.op("dve", lambda e: e.tensor_copy(out=tri.ap[:, :], in_=tmpf2.ap[:, :]), [tmpf2], [tri])
            lv = [T.sb("lv%d" % i, [128, 64], F32) for i in range(4)]
            for i in range(4):
                T.dma("sp", lv[i].ap[:, :], lam_d[i].partition_broadcast(128), [], [lv[i]])
            lsum = T.sb("lsum", [128, 4], F32)
            ljunk = T.sb("ljunk", [128, 64], F32)
            for i in range(2):
                T.op("dve", lambda e, i=i: e.tensor_tensor(out=ljunk.ap[:, :], in0=lv[2 * i].ap[:, :], in1=lv[2 * i + 1].ap[:, :],
                                                        op=ALU.mult), [lv[2 * i], lv[2 * i + 1]], [ljunk])
                T.op("dve", lambda e, i=i: e.tensor_reduce(out=lsum.ap[:, i:i + 1], in_=ljunk.ap[:, :],
                                                        axis=mybir.AxisListType.X, op=ALU.add), [ljunk], [lsum])
            T.op("act", lambda e: e.activation(out=lsum.ap[:, 2:4], in_=lsum.ap[:, 0:2], func=AF.Exp), [lsum], [lsum])
            T.op("dve", lambda e: e.scalar_tensor_tensor(out=neg_lam.ap[:, :], in0=lsum.ap[:, 3:4], scalar=-0.2, in1=lsum.ap[:, 2:3],
                                                         op0=ALU.add, op1=ALU.subtract), [lsum], [neg_lam])
            T.dma("sp", tmpf.ap[:, :], subln_g.partition_broadcast(128), [ident], [tmpf])
            T.op("dve", lambda e: e.tensor_scalar(out=gsub.ap[:, :], in0=tmpf.ap[:, :], scalar1=0.8, scalar2=None, op0=ALU.mult),
                 [tmpf], [gsub])

        scopeA = _Scope(T)
        scopeA.__enter__()
        QTz = T.sb("QTz", [128, 2, S], BF16)
        KT = T.sb("KT", [128, S], BF16)
        VS = T.sb("VS", [128, NT, 129], BF16)
        T.op("pool", lambda e: e.memset(QTz.ap[:, :, :], 0.0), [], [QTz])
        T.op("pool", lambda e: e.memset(VS.ap[:, :, 128:129], 1.0), [], [VS])
        with _Scope(T):
            gmix = T.sb("gmix", [128, 8], F32)
            load_vec(gmix, norm_mix_g)
            nt = NormT(T, ident, neghalf, gmix, nbuf=4)
            Wq = T.sb("Wq", [128, 8, 384], BF16)
            for kc in range(8):
                T.dma("pool", Wq.ap[:, kc, :], w_qkv[kc * 128:(kc + 1) * 128, :], [], [Wq])
            posi = T.sb("posi", [128, NT], I32)
            posf = T.sb("posf", [128, NT], F32)
            ang = T.sb("ang", [128, NT, 8], F32)
            kfi = T.sb("kfi", [128, NT, 8], I32)
            kf = T.sb("kf", [128, NT, 8], F32)
            mk = T.sb("mk", [128, NT, 8], F32)
            cosT = T.sb("cosT", [128, NT, 8], F32)
            sinT = T.sb("sinT", [128, NT, 8], F32)
            T.dma("sp", posi.ap[:, :], pos_kv[:, :], [], [posi])
            T.op("dve", lambda e: e.tensor_copy(out=posf.ap[:, :], in_=posi.ap[:, :]), [posi], [posf])
            for j in range(8):
                T.op("dve", lambda e, j=j: e.tensor_scalar(out=ang.ap[:, :, j], in0=posf.ap[:, :], scalar1=float(inv_freq[j]),
                                                        scalar2=None, op0=ALU.mult), [posf], [ang])
            T.op("dve", lambda e: e.tensor_scalar(out=kfi.ap[:, :, :], in0=ang.ap[:, :, :], scalar1=1.0 / TWO_PI, scalar2=None,
                                                  op0=ALU.mult), [ang], [kfi])
            T.op("dve", lambda e: e.tensor_copy(out=kf.ap[:, :, :], in_=kfi.ap[:, :, :]), [kfi], [kf])
            T.op("dve", lambda e: e.scalar_tensor_tensor(out=ang.ap[:, :, :], in0=kf.ap[:, :, :], scalar=-TWO_PI, in1=ang.ap[:, :, :],
                                                         op0=ALU.mult, op1=ALU.add), [kf, ang], [ang])

            def wrap(dst, shift):
                T.op("dve", lambda e: e.tensor_scalar(out=dst.ap[:, :, :], in0=ang.ap[:, :, :], scalar1=float(shift), scalar2=None,
                                                      op0=ALU.add), [ang], [dst])
                T.op("dve", lambda e: e.tensor_scalar(out=mk.ap[:, :, :], in0=dst.ap[:, :, :], scalar1=float(np.pi), scalar2=-TWO_PI,
                                                      op0=ALU.is_gt, op1=ALU.mult), [dst], [mk])
                T.op("dve", lambda e: e.tensor_tensor(out=dst.ap[:, :, :], in0=dst.ap[:, :, :], in1=mk.ap[:, :, :], op=ALU.add),
                     [dst, mk], [dst])
                T.op("dve", lambda e: e.tensor_scalar(out=mk.ap[:, :, :], in0=dst.ap[:, :, :], scalar1=float(-np.pi), scalar2=TWO_PI,
                                                      op0=ALU.is_lt, op1=ALU.mult), [dst], [mk])
                T.op("dve", lambda e: e.tensor_tensor(out=dst.ap[:, :, :], in0=dst.ap[:, :, :], in1=mk.ap[:, :, :], op=ALU.add),
                     [dst, mk], [dst])
                T.op("act", lambda e: e.activation(out=dst.ap[:, :, :], in_=dst.ap[:, :, :], func=AF.Sin), [dst], [dst])

            wrap(sinT, 0.0)
            wrap(cosT, np.pi / 2)

            xt = Rot([T.sb("xt%d" % i, [128, D], F32) for i in range(4)])
            xnT = Rot([T.sb("xnT%d" % i, [128, 8, 128], BF16) for i in range(3)])
            ps_p = Rot([T.ps("ps_p%d" % i, [128, 512], F32) for i in range(2)])
            ps_t = Rot([T.ps("ps_t%d" % i, [128, 1024], BF16) for i in range(2)])
            qksb = Rot([T.sb("qksb%d" % i, [128, 256], BF16) for i in range(2)])
            tA = Rot([T.sb("tA%d" % i, [128, 4, 8], F32) for i in range(4)])
            tB = Rot([T.sb("tB%d" % i, [128, 4, 8], F32) for i in range(4)])

            def stage_a(sl):
                xt_ = xt.next()
                T.dma("sp", xt_.ap[:, :], x_b[sl], [], [xt_])
                return xt_, nt.stage_a(xt_.ap[:, :], xt_)

            def stage_b(sl, a):
                xt_, ss = a
                xn = xnT.next()
                nt.stage_b(xt_.ap[:, :], xt_, ss, xn, 0)
                ps = ps_p.next()
                for kc in range(8):
                    T.op("pe", lambda e, kc=kc: e.matmul(ps.ap[:, 0:384], lhsT=xn.ap[:, kc, :], rhs=Wq.ap[:, kc, :],
                                                         start=(kc == 0), stop=(kc == 7)), [xn, Wq], [ps], inc=(kc == 7))
                return ps

            def stage_c(sl, ps):
                T.op("dve", lambda e: e.tensor_copy(out=VS.ap[:, sl, 0:128], in_=ps.ap[:, 256:384]), [ps], [VS])
                sb_ = qksb.next()
                T.op("dve", lambda e: e.tensor_copy(out=sb_.ap[:, :], in_=ps.ap[:, 0:256]), [ps], [sb_])
                v3 = ps.ap[:, 0:256].rearrange("p (g d) -> p g d", g=4)
                o3 = sb_.ap[:, :].rearrange("p (g d) -> p g d", g=4)
                t1 = v3[:, :, 0:8]
                t2 = v3[:, :, 8:16]
                cb = bcast_mid(cosT.ap[:, sl, :], 4)
                sbb = bcast_mid(sinT.ap[:, sl, :], 4)
                a1 = tA.next()
                b1 = tB.next()
                a2 = tA.next()
                b2 = tB.next()
                TT = lambda o, i0, i1, op, rd, wr: T.op("dve", lambda e: e.tensor_tensor(out=o, in0=i0, in1=i1, op=op), rd, wr)
                TT(a1.ap[:, :, :], t1, cb, ALU.mult, [ps, cosT], [a1])
                TT(b1.ap[:, :, :], t2, sbb, ALU.mult, [ps, sinT], [b1])
                TT(a2.ap[:, :, :], t2, cb, ALU.mult, [ps, cosT], [a2])
                TT(b2.ap[:, :, :], t1, sbb, ALU.mult, [ps, sinT], [b2])
                TT(o3[:, :, 0:8], a1.ap[:, :, :], b1.ap[:, :, :], ALU.subtract, [a1, b1], [sb_])
                TT(o3[:, :, 8:16], a2.ap[:, :, :], b2.ap[:, :, :], ALU.add, [a2, b2], [sb_])
                pt = ps_t.next()
                for h in range(2):
                    T.op("pe", lambda e, h=h: e.transpose(out=pt.ap[:, h * 128:(h + 1) * 128], in_=sb_.ap[:, h * 128:(h + 1) * 128],
                                                          identity=ident.ap[:, :]), [sb_, ident], [pt], inc=(h == 1))
                for m in range(2):
                    T.op("dve", lambda e, m=m: e.tensor_copy(out=QTz.ap[64 * m:64 * m + 64, m, sl * 128:(sl + 1) * 128],
                                                          in_=pt.ap[64 * m:64 * m + 64, 0:128]), [pt], [QTz])
                T.op("dve", lambda e: e.tensor_copy(out=KT.ap[:, sl * 128:(sl + 1) * 128], in_=pt.ap[:, 128:256]), [pt], [KT])

            sa = {0: stage_a(0)}
            if NT > 1:
                sa[1] = stage_a(1)
            sbq = {0: stage_b(0, sa.pop(0))}
            for sl in range(NT):
                if sl + 2 < NT:
                    sa[sl + 2] = stage_a(sl + 2)
                if sl + 1 < NT:
                    sbq[sl + 1] = stage_b(sl + 1, sa.pop(sl + 1))
                stage_c(sl, sbq.pop(sl))

        import os
        KSTOP = int(os.environ.get("KDBG_STOP", "99"))
        if KSTOP == 1:
            T.finish()
            scopeA.__exit__(None, None, None)
            return nc, T
        ccs = es.enter_context(nc.semaphore("ccs"))
        cc_n = [0]

        def gather_piece(pi):
            T._sync("pool", [sendr[pi]], [recvr[pi]])
            nc.gpsimd.collective_compute("AllGather", ALU.bypass, replica_groups=[[0, 1, 2, 3], [4, 5, 6, 7]],
                                         ins=[sendp[pi].opt()], outs=[recvp[pi].opt()]).then_inc(ccs, 1)
            T.log["pool"].append(("inc", id(ccs), 1, ("cc", 0)))
            cc_n[0] += 1
            T._mark((("cc", 0), ccs, cc_n[0]), [sendr[pi]], [recvr[pi]])

        GQ = 2
        groups = [list(range(i, i + GQ)) for i in range(0, NT, GQ)]
        LOOK = 3
        with _Scope(T):
            ps_s = Rot([T.ps("ps_s%d" % i, [128, 2, 128 * GQ], F32) for i in range(LOOK + 1)])
            ps_o = [T.ps("ps_o%d" % i, [128, 512], F32) for i in range(GQ)]
            ps_at = Rot([T.ps("ps_at%d" % i, [128, 1024], BF16) for i in range(1)])
            PT = Rot([T.sb("PT%d" % i, [128, 2, 128 * GQ], BF16) for i in range(5)])
            osb = Rot([T.sb("osb%d" % i, [128, 258], F32) for i in range(2)])
            sm = Rot([T.sb("sm%d" % i, [128, 8], F32) for i in range(2)])
            af = Rot([T.sb("af%d" % i, [128, 128], F32) for i in range(2)])
            tf = Rot([T.sb("tf%d" % i, [128, 128], F32) for i in range(2)])
            anb = Rot([T.sb("anb%d" % i, [128, 128], BF16) for i in range(5)])
            ast = Rot([T.sb("ast%d" % i, [128, 128 * GQ], BF16) for i in range(3)])
            precast = []
            for k_ in ("u", "out", "cq", "co", "ckv", "up", "dn"):
                a_, r_, c_ = wsrc[k_]
                for r0 in range(0, r_, 128):
                    for c0 in range(0, c_, 512):
                        cw_ = min(512, c_ - c0)
                        precast.append((k_, r0, c0, cw_))
            n_pre_groups = max(1, len(groups) - min(8, len(groups) - 1))
            per_group = -(-len(precast) // n_pre_groups)
            zt = T.sb("zt", [128, 128], BF16)
            T.op("pool", lambda e: e.memset(zt.ap[:, :], 0.0), [], [zt])
            T.dma("pool", sendp[0][0:128, 0:128], zt.ap[:, :], [zt], [sendr[0]])
            deferred = []
            for gi, grp in enumerate(groups):
                if gi >= len(groups) - n_pre_groups:
                    for _ in range(per_group):
                        if precast:
                            k_, r0, c0, cw_ = precast.pop(0)
                            T.dma("pool", wbf[k_][r0:r0 + 128, c0:c0 + cw_], wsrc[k_][0][r0:r0 + 128, c0:c0 + cw_], [], [wres[k_]])
                i0 = grp[0]
                G = len(grp)
                keys = []
                for k in range(0, i0 + G):
                    j0 = max(0, k - i0)
                    keys.append((k, j0, k >= i0))
                nk = len(keys)
                started = [False] * G
                a_st = ast.next()

                def qk(idx):
                    k, j0, dg = keys[idx]
                    pss = ps_s.next()
                    n0 = j0 * 128
                    n1 = G * 128
                    for m in range(2):
                        T.op("pe", lambda e, m=m: e.matmul(pss.ap[:, m, n0:n1], lhsT=KT.ap[:, k * 128:(k + 1) * 128],
                                                           rhs=QTz.ap[:, m, i0 * 128 + n0:i0 * 128 + n1], start=True, stop=True),
                             [KT, QTz], [pss], inc=(m == 1))
                    return pss

                pend = [qk(i) for i in range(min(LOOK, nk))]
                for idx in range(nk):
                    if idx == min(4, nk - 1) and deferred:
                        for f_ in deferred:
                            f_()
                        deferred = []
                    k, j0, dg = keys[idx]
                    pss = pend.pop(0)
                    if idx + LOOK < nk:
                        pend.append(qk(idx + LOOK))
                    n0 = j0 * 128
                    n1 = G * 128
                    pt_ = PT.next()
                    T.op("act", lambda e: e.activation(out=pt_.ap[:, :, n0:n1], in_=pss.ap[:, :, n0:n1], func=AF.Exp, scale=0.125),
                         [pss], [pt_])
                    if dg:
                        for m in range(2):
                            T.op("dve", lambda e, m=m: e.tensor_tensor(out=pt_.ap[:, m, n0:n0 + 128], in0=pt_.ap[:, m, n0:n0 + 128],
                                                                    in1=tri.ap[:, :], op=ALU.mult), [pt_, tri], [pt_])
                    for j in range(j0, G):
                        last_key = (k == i0 + j)
                        for m in range(2):
                            st = not started[j]
                            started[j] = True
                            T.op("pe", lambda e, j=j, m=m, st=st, last_key=last_key: e.matmul(
                                ps_o[j].ap[:, m * 129:(m + 1) * 129], lhsT=pt_.ap[:, m, j * 128:(j + 1) * 128], rhs=VS.ap[:, k, :],
                                start=st, stop=(last_key and m == 1), skip_group_check=True),
                                 [pt_, VS], [ps_o[j]], inc=(m == 1 and j == G - 1) or (last_key and m == 1))
                        if last_key:
                            o_ = osb.next()
                            s_ = sm.next()
                            a_ = af.next()
                            t_ = tf.next()
                            n_ = anb.next()
                            pa = ps_at.next()
                            T.op("dve", lambda e, j=j: e.tensor_copy(out=o_.ap[:, :], in_=ps_o[j].ap[:, 0:258]), [ps_o[j]], [o_])
                            T.op("dve", lambda e: e.reciprocal(out=s_.ap[:, 0:1], in_=o_.ap[:, 128:129]), [o_], [s_])
                            T.op("dve", lambda e: e.reciprocal(out=s_.ap[:, 1:2], in_=o_.ap[:, 257:258]), [o_], [s_])
                            T.op("dve", lambda e: e.tensor_tensor(out=s_.ap[:, 2:3], in0=s_.ap[:, 1:2], in1=neg_lam.ap[:, 0:1], op=ALU.mult),
                                 [s_, neg_lam], [s_])
                            T.op("dve", lambda e: e.tensor_scalar(out=t_.ap[:, :], in0=o_.ap[:, 129:257], scalar1=s_.ap[:, 2:3], scalar2=None,
                                                                  op0=ALU.mult), [o_, s_], [t_])
                            T.op("dve", lambda e: e.scalar_tensor_tensor(out=a_.ap[:, :], in0=o_.ap[:, 0:128], scalar=s_.ap[:, 0:1],
                                                                         in1=t_.ap[:, :], op0=ALU.mult, op1=ALU.add), [o_, s_, t_], [a_])
                            T.op("dve", lambda e: e.tensor_tensor(out=t_.ap[:, :], in0=a_.ap[:, :], in1=a_.ap[:, :], op=ALU.mult), [a_], [t_])
                            T.op("dve", lambda e: e.tensor_reduce(out=s_.ap[:, 3:4], in_=t_.ap[:, :], axis=mybir.AxisListType.X, op=ALU.add),
                                 [t_], [s_])
                            T.op("dve", lambda e: e.tensor_scalar(out=s_.ap[:, 4:5], in0=s_.ap[:, 3:4], scalar1=1.0 / 128, scalar2=EPS,
                                                                  op0=ALU.mult, op1=ALU.add), [s_], [s_])
                            T.op("pool", lambda e: e.tensor_tensor(out=s_.ap[:, 5:6], in0=s_.ap[:, 4:5], in1=neghalf.ap[:, 0:1], op=ALU.pow),
                                 [s_, neghalf], [s_])
                            T.op("dve", lambda e: e.scalar_tensor_tensor(out=n_.ap[:, :], in0=a_.ap[:, :], scalar=s_.ap[:, 5:6],
                                                                         in1=gsub.ap[:, :], op0=ALU.mult, op1=ALU.mult), [a_, s_, gsub], [n_])
                            def fin(j=j, n_=n_, pa=pa, a_st=a_st, tq=i0 + j):
                                T.op("pe", lambda e: e.transpose(out=pa.ap[:, 0:128], in_=n_.ap[:, :], identity=ident.ap[:, :]), [n_, ident], [pa])
                                T.op("dve", lambda e: e.tensor_copy(out=a_st.ap[:, j * 128:(j + 1) * 128], in_=pa.ap[:, 0:128]), [pa], [a_st])
                                c = tq // OWN
                                pi, off = piece_of(tq % OWN + 1)
                                T.dma("pool", sendp[pi][c * 128:(c + 1) * 128, off * 128:(off + 1) * 128], a_st.ap[:, j * 128:(j + 1) * 128], [a_st], [sendr[pi]])
                                if tq % OWN == OWN - 1 and c < 3:
                                    T.dma("pool", sendp[0][(c + 1) * 128:(c + 2) * 128, 0:128], a_st.ap[:, j * 128:(j + 1) * 128], [a_st], [sendr[0]])
                                if c == 3:
                                    ps0_, pw_ = pieces[pi]
                                    if tq % OWN + 1 == ps0_ + pw_ - 1:
                                        gather_piece(pi)
                            deferred.append(fin)
            for f_ in deferred:
                f_()
            deferred = []
        scopeA.__exit__(None, None, None)

        if KSTOP == 2:
            T.finish()
            return nc, T
        if KSTOP == 3:
            T.finish()
            return nc, T
        scopeX = _Scope(T)
        scopeX.__enter__()
        aT = T.sb("aT", [128, 4, NQT], BF16)
        if KSTOP == 4:
            T.finish()
            scopeX.__exit__(None, None, None)
            return nc, T

        def bcast_free(ap1, n):
            l = [list(x) for x in ap1.ap]
            return bass.AP(ap1.tensor, ap1.offset, [l[0], [0, n]])

        def load_vec(dst, src1d):
            T.dma("sp", dst.ap[:, :], src1d.rearrange("(c p) -> p c", p=128), [], [dst], slow=True)

        blocks = [[0]] + [list(range(i, min(i + 4, NQ))) for i in range(1, NQ, 4)]
        with _Scope(T):
            gmix = T.sb("gmix", [128, 8], F32)
            load_vec(gmix, norm_mix_g)
            Wu = T.sb("Wu", [128, 8, 1024], BF16)
            Wout = T.sb("Wout", [128, 8, 1024], BF16)
            load_wb(T, Wu, 8, wbf["u"], wres["u"], 1024)
            load_wb(T, Wout, 8, wbf["out"], wres["out"], 1024)
            cw = T.sb("cw", [128, 4, CVK], F32)
            for cc in range(4):
                T.dma("sp", cw.ap[:, cc, :], cv_dw_w[:, cc * 128:(cc + 1) * 128].rearrange("k p -> p k"), [], [cw], slow=True)
            Dg = T.sb("Dg", [128, 4, CVK, 128], BF16)
            for cc in range(4):
                for k in range(CVK):
                    T.op("dve", lambda e, cc=cc, k=k: e.tensor_scalar(out=Dg.ap[:, cc, k, :], in0=ident.ap[:, :], scalar1=cw.ap[:, cc, k:k + 1],
                                                                   scalar2=None, op0=ALU.mult), [ident, cw], [Dg])
            coT_all = T.sb("coT_all", [128, 4, NQT], BF16)
            with _Scope(T):
                nt = NormT(T, ident, neghalf, gmix)
                cvb = T.sb("cvb", [128, 4], F32)
                lng = T.sb("lng", [128, 4], F32)
                lnb = T.sb("lnb", [128, 4], F32)
                load_vec(cvb, cv_dw_b)
                load_vec(lng, cv_ln_g)
                load_vec(lnb, cv_ln_b)
                epst = T.sb("epst", [128, 1], F32)
                T.op("pool", lambda e: e.memset(epst.ap[:, :], EPS), [], [epst])
                onesf = T.sb("onesf", [128, 128], F32)
                T.op("pool", lambda e: e.memset(onesf.ap[:, :], 1.0 / 512), [], [onesf])
                cTb = [T.sb("cTb%d" % i, [128, 4, 32 + 512], BF16) for i in range(2)]
                T.op("pool", lambda e: e.memset(cTb[0].ap[:, :, 0:32], 0.0), [], [cTb[0]])
                xnT = Rot([T.sb("xnT%d" % i, [128, 8, 512], BF16) for i in range(2)])
                xt = Rot([T.sb("xt%d" % i, [128, D], F32) for i in range(3)])
                sig = Rot([T.sb("sig%d" % i, [128, 512], F32) for i in range(2)])
                yT = T.sb("yT", [128, 4, 512], F32)
                ysq = T.sb("ysq", [128, 4, 512], F32)
                mean_sb = T.sb("mean_sb", [128, 512], F32)
                msq = T.sb("msq", [128, 512], F32)
                rstd = T.sb("rstd", [128, 512], F32)
                dtm = Rot([T.sb("dtm%d" % i, [128, 512], F32) for i in range(2)])
                ps_g = Rot([T.ps("ps_g%d" % i, [128, 512], F32) for i in range(2)])
                ps_m = T.ps("ps_m", [128, 512], F32)
                ps_q2 = T.ps("ps_q2", [128, 512], F32)

                for bi, tl in enumerate(blocks):
                    N = len(tl) * 128
                    cur_c = cTb[bi % 2]
                    nxt_c = cTb[(bi + 1) % 2]
                    xn = xnT.next()
                    pend = []
                    for j, t in enumerate(tl):
                        xt_ = xt.next()
                        T.dma("sp", xt_.ap[:, :], x_kv[t], [], [xt_])
                        pend.append((xt_, nt.stage_a(xt_.ap[:, :], xt_)))
                        if j >= 1:
                            a, b = pend[j - 1]
                            nt.stage_b(a.ap[:, :], a, b, xn, (j - 1) * 128)
                    a, b = pend[-1]
                    nt.stage_b(a.ap[:, :], a, b, xn, (len(tl) - 1) * 128)
                    for jc in range(4):
                        pv = ps_g.next()
                        pg = ps_g.next()
                        for ps, col in ((pv, jc * 128), (pg, 512 + jc * 128)):
                            for kc in range(8):
                                T.op("pe", lambda e, ps=ps, col=col, kc=kc: e.matmul(ps.ap[:, 0:N], lhsT=Wu.ap[:, kc, col:col + 128], rhs=xn.ap[:, kc, 0:N],
                                                                                  start=(kc == 0), stop=(kc == 7)), [Wu, xn], [ps], inc=(kc == 7))
                        sg = sig.next()
                        T.op("act", lambda e: e.activation(out=sg.ap[:, 0:N], in_=pg.ap[:, 0:N], func=AF.Sigmoid), [pg], [sg])
                        T.op("dve", lambda e, jc=jc: e.tensor_tensor(out=cur_c.ap[:, jc, 32:32 + N], in0=pv.ap[:, 0:N], in1=sg.ap[:, 0:N], op=ALU.mult),
                             [pv, sg], [cur_c])
                    for cc in range(4):
                        py = ps_g.next()
                        for k in range(CVK):
                            T.op("pe", lambda e, cc=cc, k=k: e.matmul(py.ap[:, 0:N], lhsT=Dg.ap[:, cc, k, :], rhs=cur_c.ap[:, cc, 2 + k:2 + k + N],
                                                                  start=(k == 0), stop=(k == CVK - 1)), [Dg, cur_c], [py], inc=(k == CVK - 1))
                        T.op("act", lambda e, cc=cc: e.activation(out=yT.ap[:, cc, 0:N], in_=py.ap[:, 0:N], func=AF.Identity, bias=cvb.ap[:, cc:cc + 1]),
                             [py, cvb], [yT])
                        T.op("act", lambda e, cc=cc: e.activation(out=ysq.ap[:, cc, 0:N], in_=py.ap[:, 0:N], func=AF.Square, bias=cvb.ap[:, cc:cc + 1]),
                             [py, cvb], [ysq])
                    T.op("pool", lambda e: e.tensor_copy(out=nxt_c.ap[:, :, 0:32], in_=cur_c.ap[:, :, N:N + 32]), [cur_c], [nxt_c])
                    for ps, src in ((ps_m, yT), (ps_q2, ysq)):
                        for cc in range(4):
                            T.op("pe", lambda e, ps=ps, src=src, cc=cc: e.matmul(ps.ap[:, 0:N], lhsT=onesf.ap[:, :], rhs=src.ap[:, cc, 0:N],
                                                                              start=(cc == 0), stop=(cc == 3)), [onesf, src], [ps], inc=(cc == 3))
                    T.op("act", lambda e: e.copy(out=mean_sb.ap[:, 0:N], in_=ps_m.ap[:, 0:N]), [ps_m], [mean_sb])
                    T.op("dve", lambda e: e.tensor_tensor(out=msq.ap[:, 0:N], in0=mean_sb.ap[:, 0:N], in1=mean_sb.ap[:, 0:N], op=ALU.mult), [mean_sb], [msq])
                    T.op("dve", lambda e: e.tensor_tensor(out=msq.ap[:, 0:N], in0=ps_q2.ap[:, 0:N], in1=msq.ap[:, 0:N], op=ALU.subtract), [ps_q2, msq], [msq])
                    T.op("act", lambda e: e.activation(out=msq.ap[:, 0:N], in_=msq.ap[:, 0:N], func=AF.Sqrt, bias=epst.ap[:, 0:1]), [msq, epst], [msq])
                    T.op("dve", lambda e: e.reciprocal(out=rstd.ap[:, 0:N], in_=msq.ap[:, 0:N]), [msq], [rstd])
                    cbase = tl[0] * 128
                    for cc in range(4):
                        d_ = dtm.next()
                        T.op("dve", lambda e, cc=cc: e.tensor_tensor(out=d_.ap[:, 0:N], in0=yT.ap[:, cc, 0:N], in1=mean_sb.ap[:, 0:N], op=ALU.subtract),
                             [yT, mean_sb], [d_])
                        T.op("dve", lambda e: e.tensor_tensor(out=d_.ap[:, 0:N], in0=d_.ap[:, 0:N], in1=rstd.ap[:, 0:N], op=ALU.mult), [d_, rstd], [d_])
                        T.op("act", lambda e, cc=cc: e.activation(out=coT_all.ap[:, cc, cbase:cbase + N], in_=d_.ap[:, 0:N], func=AF.Silu, scale=lng.ap[:, cc:cc + 1],
                                                                  bias=lnb.ap[:, cc:cc + 1]), [d_, lng, lnb], [coT_all])
            with _Scope(T):
                cand = Rot([T.sb("cand%d" % i, [128, NQT], BF16) for i in range(3)])
                for h in range(4):
                    for j in range(4):
                        cd = cand.next()
                        for pi, (ps0, pw) in enumerate(pieces):
                            T.dma("sp", cd.ap[:, ps0 * 128:(ps0 + pw) * 128], recvp[pi][(h * 4 + j) * 128:(h * 4 + j + 1) * 128, :], [recvr[pi]], [cd])
                        eng = "dve" if (j % 2 == 0) else "pool"
                        if j == 0:
                            T.op("dve", lambda e, h=h, j=j: e.tensor_scalar(out=aT.ap[:, h, :], in0=cd.ap[:, :], scalar1=sel.ap[:, j:j + 1], scalar2=None,
                                                                         op0=ALU.mult), [cd, sel], [aT])
                        else:
                            T.op("dve", lambda e, h=h, j=j: e.scalar_tensor_tensor(out=aT.ap[:, h, :], in0=cd.ap[:, :], scalar=sel.ap[:, j:j + 1],
                                                                                in1=aT.ap[:, h, :], op0=ALU.mult, op1=ALU.add), [cd, sel, aT], [aT])

            with _Scope(T):
                xr = Rot([T.sb("xr%d" % i, [128, D], F32) for i in range(2)])
                h1t = Rot([T.sb("h1t%d" % i, [128, D], F32) for i in range(2)])
                ps_mx4 = [T.ps("ps_mx%d" % i, [128, 512], F32) for i in range(4)]
                for bi, tl in enumerate(blocks):
                    for j, t in enumerate(tl):
                        ps_mx = ps_mx4[(t % 2) * 2:(t % 2) * 2 + 2]
                        xr_ = xr.next()
                        T.dma("sp", xr_.ap[:, :], x_kv[t], [], [xr_])
                        for hf in range(2):
                            for kc in range(8):
                                lhsT = aT.ap[:, kc, t * 128:(t + 1) * 128] if kc < 4 else coT_all.ap[:, kc - 4, t * 128:(t + 1) * 128]
                                T.op("pe", lambda e, hf=hf, kc=kc, lhsT=lhsT: e.matmul(ps_mx[hf].ap[:, :], lhsT=lhsT, rhs=Wout.ap[:, kc, hf * 512:(hf + 1) * 512],
                                                                                    start=(kc == 0), stop=(kc == 7)), [aT, coT_all, Wout], [ps_mx[hf]], inc=(kc == 7))
                        h1_ = h1t.next()
                        for hf in range(2):
                            T.op("dve", lambda e, hf=hf: e.tensor_tensor(out=h1_.ap[:, hf * 512:(hf + 1) * 512], in0=xr_.ap[:, hf * 512:(hf + 1) * 512],
                                                                      in1=ps_mx[hf].ap[:, :], op=ALU.add), [xr_, ps_mx[hf]], [h1_])
                        T.dma("pool", h1_scr[t], h1_.ap[:, :], [h1_], [h1r[t]])
        scopeX.__exit__(None, None, None)

        with _Scope(T):
            Wcq = T.sb("Wcq", [128, 8, 1024], BF16)
            Wco = T.sb("Wco", [128, 8, 1024], BF16)
            Wckv = T.sb("Wckv", [128, 8, 2048], BF16)
            gcr = T.sb("gcr", [128, 8], F32)
            gme = T.sb("gme", [128, 8], F32)
            load_vec(gcr, norm_cross_g)
            load_vec(gme, norm_mem_g)
            nt = NormT(T, ident, neghalf, gcr)
            load_wb(T, Wckv, 8, wbf["ckv"], wres["ckv"], 2048)
            load_wb(T, Wcq, 8, wbf["cq"], wres["cq"], 1024)
            load_wb(T, Wco, 8, wbf["co"], wres["co"], 1024)
            onesb = T.sb("onesb", [128, 128], BF16)
            T.op("pool", lambda e: e.memset(onesb.ap[:, :], 1.0), [], [onesb])
            ps_g = Rot([T.ps("ps_g%d" % i, [128, 512], F32) for i in range(4)])
            ps_mx = [T.ps("ps_mx%d" % i, [128, 512], F32) for i in range(2)]
            xt = Rot([T.sb("xt%d" % i, [128, D], F32) for i in range(2)])
            memT = T.sb("memT", [128, 8, NMEM], BF16)
            KcT = T.sb("KcT", [128, 8, NMEM], BF16)
            Vc = T.sb("Vc", [128, 2, 1024], BF16)
            for mt in range(2):
                xt_ = xt.next()
                T.dma("sp", xt_.ap[:, :], mem_d[mt * 128:(mt + 1) * 128, :], [], [xt_])
                ss = nt.stage_a(xt_.ap[:, :], xt_)
                nt.stage_b(xt_.ap[:, :], xt_, ss, memT, mt * 128, g8=gme)
            for e8 in range(8):
                ps = ps_g.next()
                for kc in range(8):
                    T.op("pe", lambda e, kc=kc, e8=e8: e.matmul(ps.ap[:, 0:NMEM], lhsT=Wckv.ap[:, kc, e8 * 128:(e8 + 1) * 128], rhs=memT.ap[:, kc, :],
                                                            start=(kc == 0), stop=(kc == 7)), [Wckv, memT], [ps], inc=(kc == 7))
                T.op("dve", lambda e, e8=e8: e.tensor_copy(out=KcT.ap[:, e8, :], in_=ps.ap[:, 0:NMEM]), [ps], [KcT])
            for mt in range(2):
                for hf in range(2):
                    ps = ps_g.next()
                    for kc in range(8):
                        T.op("pe", lambda e, kc=kc, mt=mt, hf=hf: e.matmul(ps.ap[:, :], lhsT=memT.ap[:, kc, mt * 128:(mt + 1) * 128],
                                                                        rhs=Wckv.ap[:, kc, 1024 + hf * 512:1024 + (hf + 1) * 512],
                                                                        start=(kc == 0), stop=(kc == 7)), [Wckv, memT], [ps], inc=(kc == 7))
                    T.op("act", lambda e, mt=mt, hf=hf: e.copy(out=Vc.ap[:, mt, hf * 512:(hf + 1) * 512], in_=ps.ap[:, :]), [ps], [Vc])
            h1b = Rot([T.sb("h1b%d" % i, [128, 4, D], F32) for i in range(2)])
            h1nT = Rot([T.sb("h1nT%d" % i, [128, 8, 512], BF16) for i in range(2)])
            qcT = T.sb("qcT", [128, 8, 512], BF16)
            ocT = T.sb("ocT", [128, 8, 512], BF16)
            PcT = Rot([T.sb("PcT%d" % i, [128, 2, 512], BF16) for i in range(2)])
            rl = Rot([T.sb("rl%d" % i, [128, 512], F32) for i in range(2)])
            h2t = Rot([T.sb("h2t%d" % i, [128, D], F32) for i in range(2)])
            for bi, tl in enumerate(blocks):
                N = len(tl) * 128
                hb = h1b.next()
                hn = h1nT.next()
                pend = []
                for j, t in enumerate(tl):
                    T.dma("sp", hb.ap[:, j, :], h1_scr[t], [h1r[t]], [hb])
                    pend.append(nt.stage_a(hb.ap[:, j, :], hb))
                    if j >= 1:
                        nt.stage_b(hb.ap[:, j - 1, :], hb, pend[j - 1], hn, (j - 1) * 128)
                nt.stage_b(hb.ap[:, len(tl) - 1, :], hb, pend[-1], hn, (len(tl) - 1) * 128)
                for e8 in range(8):
                    ps = ps_g.next()
                    for kc in range(8):
                        T.op("pe", lambda e, kc=kc, e8=e8: e.matmul(ps.ap[:, 0:N], lhsT=Wcq.ap[:, kc, e8 * 128:(e8 + 1) * 128], rhs=hn.ap[:, kc, 0:N],
                                                                start=(kc == 0), stop=(kc == 7)), [Wcq, hn], [ps], inc=(kc == 7))
                    T.op("act", lambda e, e8=e8: e.copy(out=qcT.ap[:, e8, 0:N], in_=ps.ap[:, 0:N]), [ps], [qcT])
                for hh in range(4):
                    pc = PcT.next()
                    for mt in range(2):
                        ps = ps_g.next()
                        for dc in range(2):
                            T.op("pe", lambda e, mt=mt, dc=dc: e.matmul(ps.ap[:, 0:N], lhsT=KcT.ap[:, hh * 2 + dc, mt * 128:(mt + 1) * 128],
                                                                    rhs=qcT.ap[:, hh * 2 + dc, 0:N], start=(dc == 0), stop=(dc == 1)),
                                 [KcT, qcT], [ps], inc=(dc == 1))
                        T.op("act", lambda e, mt=mt: e.activation(out=pc.ap[:, mt, 0:N], in_=ps.ap[:, 0:N], func=AF.Exp, scale=1.0 / 16), [ps], [pc])
                    pl = ps_g.next()
                    for mt in range(2):
                        T.op("pe", lambda e, mt=mt: e.matmul(pl.ap[:, 0:N], lhsT=onesb.ap[:, :], rhs=pc.ap[:, mt, 0:N], start=(mt == 0), stop=(mt == 1)),
                             [onesb, pc], [pl], inc=(mt == 1))
                    rl_ = rl.next()
                    T.op("dve", lambda e: e.reciprocal(out=rl_.ap[:, 0:N], in_=pl.ap[:, 0:N]), [pl], [rl_])
                    for dc in range(2):
                        po = ps_g.next()
                        for mt in range(2):
                            T.op("pe", lambda e, mt=mt, dc=dc: e.matmul(po.ap[:, 0:N], lhsT=Vc.ap[:, mt, hh * 256 + dc * 128:hh * 256 + (dc + 1) * 128],
                                                                    rhs=pc.ap[:, mt, 0:N], start=(mt == 0), stop=(mt == 1)), [Vc, pc], [po], inc=(mt == 1))
                        T.op("dve", lambda e, dc=dc: e.tensor_tensor(out=ocT.ap[:, hh * 2 + dc, 0:N], in0=po.ap[:, 0:N], in1=rl_.ap[:, 0:N], op=ALU.mult),
                             [po, rl_], [ocT])
                for j, t in enumerate(tl):
                    for hf in range(2):
                        for e8 in range(8):
                            T.op("pe", lambda e, hf=hf, e8=e8, j=j: e.matmul(ps_mx[hf].ap[:, :], lhsT=ocT.ap[:, e8, j * 128:(j + 1) * 128],
                                                                          rhs=Wco.ap[:, e8, hf * 512:(hf + 1) * 512], start=(e8 == 0), stop=(e8 == 7)),
                                 [ocT, Wco], [ps_mx[hf]], inc=(e8 == 7))
                    h2_ = h2t.next()
                    for hf in range(2):
                        T.op("dve", lambda e, hf=hf, j=j: e.tensor_tensor(out=h2_.ap[:, hf * 512:(hf + 1) * 512], in0=hb.ap[:, j, hf * 512:(hf + 1) * 512],
                                                                       in1=ps_mx[hf].ap[:, :], op=ALU.add), [hb, ps_mx[hf]], [h2_])
                    T.dma("pool", h2_scr[t], h2_.ap[:, :], [h2_], [h2r[t]])

        fblocks = [[0]] + [list(range(i, min(i + 2, NQ))) for i in range(1, NQ, 2)]
        with _Scope(T):
            gff = T.sb("gff", [128, 8], F32)
            load_vec(gff, norm_ffn_g)
            nt = NormT(T, ident, neghalf, gff, nptr=1)
            Wup = T.sb("Wup", [128, 8, 2 * D_FF], BF16)
            Wdn = T.sb("Wdn", [128, NFC, D], BF16)
            load_wb(T, Wup, 8, wbf["up"], wres["up"], 2 * D_FF)
            load_wb(T, Wdn, NFC, wbf["dn"], wres["dn"], D)
            ffw = T.sb("ffw", [128, 3, 2 * NFC], F32)
            for k in range(3):
                T.dma("sp", ffw.ap[:, k, :], ffn_dw_w[k].rearrange("(c p) -> p c", p=128), [], [ffw], slow=True)
            gfin = T.sb("gfin", [128, D], F32)
            T.dma("sp", gfin.ap[:, :], norm_final_g.partition_broadcast(128), [], [gfin])
            stash = T.sb("stash", [128, NFC, 2, 2], F32)
            T.op("pool", lambda e: e.memset(stash.ap[:, :, :, :], 0.0), [], [stash])
            U = Rot([T.sb("U%d" % i, [128, 2, 258], F32) for i in range(3)])
            ca = Rot([T.sb("ca%d" % i, [128, 256], F32) for i in range(2)])
            cb_ = Rot([T.sb("cb%d" % i, [128, 256], F32) for i in range(2)])
            sa = Rot([T.sb("sa%d" % i, [128, 256], F32) for i in range(2)])
            pt1 = Rot([T.sb("pt1_%d" % i, [128, 256], F32) for i in range(2)])
            pt0 = Rot([T.sb("pt0_%d" % i, [128, 256], F32) for i in range(2)])
            zT = Rot([T.sb("zT%d" % i, [128, 256], BF16) for i in range(3)])
            h2b = Rot([T.sb("h2b%d" % i, [128, 2, D], F32) for i in range(2)])
            h3nT = Rot([T.sb("h3nT%d" % i, [128, 8, 256], BF16) for i in range(2)])
            hfin = T.sb("hfin", [128, D], F32)
            ot = Rot([T.sb("ot%d" % i, [128, D], F32) for i in range(1)])
            fs = Rot([T.sb("fs%d" % i, [128, 4], F32) for i in range(2)])
            ps_up = Rot([T.ps("ps_up%d" % i, [128, 2, 256], F32) for i in range(3)])
            ps_acc = [T.ps("ps_acc%d" % i, [128, 512], F32) for i in range(4)]
            def prep_block(tl_):
                hb_ = h2b.next()
                hn_ = h3nT.next()
                pend = []
                for j, t in enumerate(tl_):
                    T.dma("sp", hb_.ap[:, j, :], h2_scr[t], [h2r[t]], [hb_])
                    pend.append(nt.stage_a(hb_.ap[:, j, :], hb_))
                    if j >= 1:
                        nt.stage_b(hb_.ap[:, j - 1, :], hb_, pend[j - 1], hn_, (j - 1) * 128)
                nt.stage_b(hb_.ap[:, len(tl_) - 1, :], hb_, pend[-1], hn_, (len(tl_) - 1) * 128)
                return hb_, hn_

            prepped = prep_block(fblocks[0])
            for bi, tl in enumerate(fblocks):
                N = len(tl) * 128
                halo = (bi == 0)
                hb, hn = prepped

                def up(fc):
                    pu = ps_up.next()
                    for ab in range(2):
                        col = ab * D_FF + fc * 128
                        for kc in range(8):
                            T.op("pe", lambda e, ab=ab, col=col, kc=kc: e.matmul(pu.ap[:, ab, 0:N], lhsT=Wup.ap[:, kc, col:col + 128], rhs=hn.ap[:, kc, 0:N],
                                                                              start=(kc == 0), stop=(kc == 7)), [Wup, hn], [pu], inc=(kc == 7))
                    return pu

                def mid_a(fc, pu):
                    u_ = U.next()
                    T.op("act", lambda e: e.copy(out=u_.ap[:, :, 2:2 + N], in_=pu.ap[:, :, 0:N]), [pu], [u_])
                    T.op("act", lambda e: e.copy(out=u_.ap[:, :, 0:2], in_=stash.ap[:, fc, :, :]), [stash], [u_])
                    T.op("act", lambda e: e.copy(out=stash.ap[:, fc, :, :], in_=u_.ap[:, :, N:N + 2]), [u_], [stash])
                    return u_

                def mid_b(fc, u_):
                    if halo:
                        return None
                    a_ = ca.next()
                    b_ = cb_.next()
                    dst = (a_, b_)
                    ptmp = {1: pt1.next(), 0: pt0.next()}
                    for k in (2, 1, 0):
                        for ab in range(2):
                            wc = ab * NFC + fc
                            d_ = dst[ab]
                            if ab == 0:
                                if k == 2:
                                    T.op("dve", lambda e, wc=wc, d_=d_: e.tensor_scalar(out=d_.ap[:, 0:N], in0=u_.ap[:, 0, 2:2 + N], scalar1=ffw.ap[:, 2, wc:wc + 1],
                                                                                     scalar2=None, op0=ALU.mult), [u_, ffw], [d_])
                                else:
                                    T.op("dve", lambda e, wc=wc, k=k, d_=d_: e.scalar_tensor_tensor(out=d_.ap[:, 0:N], in0=u_.ap[:, 0, k:k + N],
                                                                                                 scalar=ffw.ap[:, k, wc:wc + 1], in1=d_.ap[:, 0:N],
                                                                                                 op0=ALU.mult, op1=ALU.add), [u_, ffw, d_], [d_])
                            else:
                                tgt = d_ if k == 2 else ptmp[k]
                                T.op("pool", lambda e, wc=wc, k=k, tgt=tgt: e.tensor_scalar(out=tgt.ap[:, 0:N], in0=u_.ap[:, 1, k:k + N], scalar1=ffw.ap[:, k, wc:wc + 1],
                                                                                         scalar2=0.0, op0=ALU.mult, op1=ALU.add), [u_, ffw], [tgt])
                    T.op("pool", lambda e: e.tensor_tensor(out=b_.ap[:, 0:N], in0=b_.ap[:, 0:N], in1=ptmp[1].ap[:, 0:N], op=ALU.add), [b_, ptmp[1]], [b_])
                    T.op("dve", lambda e: e.tensor_tensor(out=b_.ap[:, 0:N], in0=b_.ap[:, 0:N], in1=ptmp[0].ap[:, 0:N], op=ALU.add), [b_, ptmp[0]], [b_])
                    s_ = sa.next()
                    z_ = zT.next()
                    T.op("act", lambda e: e.activation(out=s_.ap[:, 0:N], in_=a_.ap[:, 0:N], func=AF.Silu), [a_], [s_])
                    T.op("dve", lambda e: e.tensor_tensor(out=z_.ap[:, 0:N], in0=s_.ap[:, 0:N], in1=b_.ap[:, 0:N], op=ALU.mult), [s_, b_], [z_])
                    return z_

                def down(fc, z_):
                    for j in range(len(tl)):
                        for hf in range(2):
                            lastmm = (j == len(tl) - 1 and hf == 1)
                            T.op("pe", lambda e, j=j, hf=hf: e.matmul(ps_acc[j * 2 + hf].ap[:, :], lhsT=z_.ap[:, j * 128:(j + 1) * 128],
                                                                   rhs=Wdn.ap[:, fc, hf * 512:(hf + 1) * 512], start=(fc == 0), stop=(fc == NFC - 1)),
                                 [z_, Wdn], [ps_acc[j * 2 + hf]], inc=(lastmm or fc == NFC - 1))

                ups = [up(0), up(1)]
                unext = mid_a(0, ups.pop(0))
                zprev = None
                for fc in range(NFC):
                    if fc + 2 < NFC:
                        ups.append(up(fc + 2))
                    ucur = unext
                    if fc + 1 < NFC:
                        unext = mid_a(fc + 1, ups.pop(0))
                    z_ = mid_b(fc, ucur)
                    if zprev is not None:
                        down(fc - 1, zprev)
                    zprev = z_
                    if fc == NFC // 2 and bi + 1 < len(fblocks):
                        prepped = prep_block(fblocks[bi + 1])
                if zprev is not None:
                    down(NFC - 1, zprev)
                if halo:
                    T.op("pool", lambda e: e.tensor_scalar(out=stash.ap[:, :, :, :], in0=stash.ap[:, :, :, :], scalar1=hvalid.ap[:, 0:1], scalar2=0.0,
                                                           op0=ALU.mult, op1=ALU.add), [stash, hvalid], [stash])
                    continue
                for j, t in enumerate(tl):
                    for hf in range(2):
                        T.op("dve", lambda e, j=j, hf=hf: e.tensor_tensor(out=hfin.ap[:, hf * 512:(hf + 1) * 512], in0=hb.ap[:, j, hf * 512:(hf + 1) * 512],
                                                                       in1=ps_acc[j * 2 + hf].ap[:, :], op=ALU.add), [hb, ps_acc[j * 2 + hf]], [hfin])
                    f_ = fs.next()
                    T.op("act", lambda e: e.activation(out=nt.junk.ap[:, :], in_=hfin.ap[:, :], func=AF.Square, accum_out=f_.ap[:, 0:1]), [hfin], [f_])
                    T.op("dve", lambda e: e.tensor_scalar(out=f_.ap[:, 1:2], in0=f_.ap[:, 0:1], scalar1=1.0 / D, scalar2=EPS, op0=ALU.mult, op1=ALU.add),
                         [f_], [f_])
                    T.op("pool", lambda e: e.tensor_tensor(out=f_.ap[:, 2:3], in0=f_.ap[:, 1:2], in1=neghalf.ap[:, 0:1], op=ALU.pow), [f_, neghalf], [f_])
                    o_ = ot.next()
                    T.op("dve", lambda e: e.scalar_tensor_tensor(out=o_.ap[:, :], in0=hfin.ap[:, :], scalar=f_.ap[:, 2:3], in1=gfin.ap[:, :],
                                                                 op0=ALU.mult, op1=ALU.mult), [hfin, f_, gfin], [o_])
                    T.dma("pool", out_d[t - 1], o_.ap[:, :], [o_], [])
        T.finish()
    return nc, T


def _host_prep(inputs, S):
    NT = S // 128
    OWN = NT // 4
    x = np.asarray(inputs["x"], dtype=np.float32)
    mem = np.asarray(inputs["mem"], dtype=np.float32)
    pos = np.asarray(inputs["positions"], dtype=np.int32)
    B = x.shape[0]
    ident = np.eye(128, dtype=np.float32)
    tri = np.triu(np.ones((128, 128), dtype=np.float32))
    wnames = ["norm_mix_g", "lam_q1", "lam_k1", "lam_q2", "lam_k2", "subln_g", "cv_dw_w", "cv_dw_b", "cv_ln_g", "cv_ln_b",
              "w_out", "norm_cross_g", "norm_mem_g", "w_cq", "w_ckv", "w_co", "norm_ffn_g", "w_up", "ffn_dw_w", "w_down"]
    common = {n: np.ascontiguousarray(np.asarray(inputs[n], dtype=np.float32)[0]) for n in wnames}
    common["norm_final_g"] = np.ascontiguousarray(np.asarray(inputs["norm_final_g"], dtype=np.float32))
    common["ident"] = ident
    common["tri"] = tri
    w_in = np.asarray(inputs["w_in"], dtype=np.float32)[0]
    common["w_u"] = np.ascontiguousarray(w_in[:, 1536:2560])
    in_maps = []
    for b in range(B):
        xt = np.ascontiguousarray(x[b].reshape(NT, 128, D))
        pk = np.ascontiguousarray(pos[b].reshape(NT, 128).T)
        memb = np.ascontiguousarray(mem[b])
        for r in range(4):
            m = dict(common)
            m["x_b"] = xt
            m["pos_b"] = pk
            m["w_qkv"] = np.ascontiguousarray(np.concatenate(
                [w_in[:, r * 128:(r + 1) * 128], w_in[:, 512 + r * 128:512 + (r + 1) * 128], w_in[:, 1024 + r * 128:1024 + (r + 1) * 128]], axis=1))
            xo = np.zeros((OWN + 1, 128, D), dtype=np.float32)
            xo[1:] = xt[r * OWN:(r + 1) * OWN]
            if r > 0:
                xo[0] = xt[r * OWN - 1]
            m["x_own"] = xo
            sl = np.zeros((128, 4), dtype=np.float32)
            sl[:, r] = 1.0
            m["sel"] = sl
            m["hvalid"] = np.full((128, 1), 1.0 if r > 0 else 0.0, dtype=np.float32)
            m["mem"] = memb
            in_maps.append(m)
    return in_maps


_CACHE = {}


def kernel(**inputs):
    x = np.asarray(inputs["x"])
    B, S, _ = x.shape
    NT = S // 128
    OWN = NT // 4
    in_maps = _host_prep(inputs, S)
    if S not in _CACHE:
        _CACHE[S] = build(S)[0]
    nc = _CACHE[S]
    res = run_bass_kernel_spmd(nc, in_maps, core_ids=list(range(len(in_maps))))
    out = np.zeros((B, S, D), dtype=np.float32)
    i = 0
    for b in range(B):
        for c in range(4):
            out[b, c * OWN * 128:(c + 1) * OWN * 128, :] = np.asarray(res.results[i]["out"]).reshape(OWN * 128, D)
            i += 1
    return out
```
